# Optimizing a Trainium2 kernel written in Bass

```python
import math
import jax, jax.numpy as jnp
from jax import lax
import numpy as np

D_MODEL = 1024
BATCH = 8
SEQ = 2048
DEPTH = 2

N_MIXERS = 2
A_HEADS = 16
A_KV_HEADS = 2
A_HEAD_DIM = 64
WINDOW = 128
BLOCK = 128
NUM_BUCKETS = 32
MAX_DISTANCE = 128
B_HEADS = 16
Q_LORA = 256
KV_LORA = 128
QK_NOPE = 64
QK_ROPE = 32
V_DIM = 64
ROPE_BASE = 10000.0
D_FF = 2816
EPS = 1e-6
NEG = -1e30

N_A_LAYERS = (DEPTH + 1) // 2
N_B_LAYERS = DEPTH // 2
A_IN_COLS = (A_HEADS + 2 * A_KV_HEADS) * A_HEAD_DIM
B_IN_COLS = Q_LORA + KV_LORA + QK_ROPE

kernel_name = "hybrid_swa_sink_mla_macaron"


def rmsnorm(x, g):
    xf = x.astype(jnp.float32)
    y = xf * lax.rsqrt(jnp.mean(xf * xf, axis=-1, keepdims=True) + EPS)
    return (y * g.astype(jnp.float32)).astype(x.dtype)


def swiglu(h, wg, wu, wd):
    return (jax.nn.silu(h @ wg) * (h @ wu)) @ wd


def t5_bucket(dist):
    n = jnp.maximum(dist, 0)
    max_exact = NUM_BUCKETS // 2
    large = max_exact + (jnp.log(jnp.maximum(n, 1).astype(jnp.float32) / max_exact)
                         / math.log(MAX_DISTANCE / max_exact)
                         * (NUM_BUCKETS - max_exact)).astype(jnp.int32)
    large = jnp.minimum(large, NUM_BUCKETS - 1)
    return jnp.where(n < max_exact, n, large)


def apply_rope(t, pos):
    half = QK_ROPE // 2
    inv_freq = ROPE_BASE ** (-jnp.arange(0, QK_ROPE, 2, dtype=jnp.float32) / QK_ROPE)
    ang = pos.astype(jnp.float32)[..., None] * inv_freq
    cos = jnp.cos(ang)[:, :, None, :]
    sin = jnp.sin(ang)[:, :, None, :]
    t1, t2 = t[..., :half].astype(jnp.float32), t[..., half:].astype(jnp.float32)
    return jnp.concatenate([t1 * cos - t2 * sin, t2 * cos + t1 * sin], axis=-1).astype(t.dtype)


def band(t, nb):
    B = t.shape[0]
    rest = t.shape[2:]
    pad = jnp.zeros((B, BLOCK) + rest, t.dtype)
    tp = jnp.concatenate([pad, t], axis=1).reshape((B, nb + 1, BLOCK) + rest)
    return jnp.concatenate([tp[:, :-1], tp[:, 1:]], axis=2)


def sliding_window_attention(h, pos, rel_bias, w_in, q_gain, k_gain, sinks, w_out):
    B, S, _ = h.shape
    nb = S // BLOCK
    G = A_HEADS // A_KV_HEADS
    qkv = h @ w_in
    q, k, v = jnp.split(qkv, [A_HEADS * A_HEAD_DIM, (A_HEADS + A_KV_HEADS) * A_HEAD_DIM], axis=-1)
    q = rmsnorm(q.reshape(B, S, A_HEADS, A_HEAD_DIM), q_gain)
    k = rmsnorm(k.reshape(B, S, A_KV_HEADS, A_HEAD_DIM), k_gain)
    v = v.reshape(B, S, A_KV_HEADS, A_HEAD_DIM)
    q = q.reshape(B, nb, BLOCK, A_KV_HEADS, G, A_HEAD_DIM)
    kb, vb = band(k, nb), band(v, nb)
    posb = band(pos, nb)
    scale = A_HEAD_DIM ** -0.5
    scores = jnp.einsum('bnqkgd,bnskd->bnkgqs', q, kb).astype(jnp.float32) * scale
    dist = pos.reshape(B, nb, BLOCK)[..., :, None] - posb[..., None, :]
    bias = rel_bias[t5_bucket(dist)].astype(jnp.float32)
    bias = bias.reshape(B, nb, BLOCK, 2 * BLOCK, A_KV_HEADS, G).transpose(0, 1, 4, 5, 2, 3)
    scores = scores + bias
    qi = jnp.arange(BLOCK)[:, None] + BLOCK
    si = jnp.arange(2 * BLOCK)[None, :]
    rel = qi - si
    blk = jnp.arange(nb)[:, None, None]
    valid = (rel >= 0) & (rel < WINDOW) & (blk * BLOCK + si - BLOCK >= 0)
    scores = jnp.where(valid[None, :, None, None], scores, NEG)
    sink = sinks.astype(jnp.float32).reshape(A_KV_HEADS, G)[None, None, :, :, None, None]
    m = jnp.maximum(jnp.max(scores, axis=-1, keepdims=True), sink)
    p = jnp.exp(scores - m)
    denom = jnp.sum(p, axis=-1, keepdims=True) + jnp.exp(sink - m)
    probs = (p / denom).astype(v.dtype)
    o = jnp.einsum('bnkgqs,bnskd->bnqkgd', probs, vb).reshape(B, S, A_HEADS * A_HEAD_DIM)
    return o @ w_out


def latent_attention(h, pos, w_in, q_norm, kv_norm, w_uq, w_ukv, q_gain, k_gain, w_out):
    B, S, _ = h.shape
    nb = S // BLOCK
    dqk = QK_NOPE + QK_ROPE
    c = h @ w_in
    c_q, c_kv, k_rope = jnp.split(c, [Q_LORA, Q_LORA + KV_LORA], axis=-1)
    q = (rmsnorm(c_q, q_norm) @ w_uq).reshape(B, S, B_HEADS, dqk)
    kv = (rmsnorm(c_kv, kv_norm) @ w_ukv).reshape(B, S, B_HEADS, QK_NOPE + V_DIM)
    k_nope, v = jnp.split(kv, [QK_NOPE], axis=-1)
    k = jnp.concatenate(
        [k_nope, jnp.broadcast_to(k_rope[:, :, None, :], (B, S, B_HEADS, QK_ROPE))], axis=-1)
    q = rmsnorm(q, q_gain)
    k = rmsnorm(k, k_gain)
    q = jnp.concatenate([q[..., :QK_NOPE], apply_rope(q[..., QK_NOPE:], pos)], axis=-1)
    k = jnp.concatenate([k[..., :QK_NOPE], apply_rope(k[..., QK_NOPE:], pos)], axis=-1)
    scale = dqk ** -0.5
    qb = q.reshape(B, nb, BLOCK, B_HEADS, dqk).transpose(1, 0, 2, 3, 4)
    key_idx = jnp.arange(S)

    def attend(args):
        q_blk, n = args
        s = jnp.einsum('bqhd,bshd->bhqs', q_blk, k).astype(jnp.float32) * scale
        q_idx = n * BLOCK + jnp.arange(BLOCK)
        s = jnp.where(key_idx[None, :] <= q_idx[:, None], s, NEG)
        p = jax.nn.softmax(s, axis=-1).astype(v.dtype)
        return jnp.einsum('bhqs,bshd->bqhd', p, v)

    o = lax.map(attend, (qb, jnp.arange(nb)))
    o = o.transpose(1, 0, 2, 3, 4).reshape(B, S, B_HEADS * V_DIM)
    return o @ w_out


def setup_inputs(seed: int = 0) -> dict:
    key = jax.random.key(seed)
    ks = jax.random.split(key, 32)
    f32 = jnp.float32

    def w(k, shape, fan_in):
        return jax.random.normal(k, shape, f32) * (fan_in ** -0.5)

    def gain(k, shape):
        return 1.0 + 0.05 * jax.random.normal(k, shape, f32)

    x = jax.random.normal(ks[0], (BATCH, SEQ, D_MODEL), f32)
    offsets = jax.random.randint(ks[1], (BATCH, 1), 0, 64, dtype=jnp.int32)
    positions = (jnp.arange(SEQ, dtype=jnp.int32)[None, :] + offsets).astype(jnp.int32)
    return {
        "x": x,
        "positions": positions,
        "rel_bias": 0.5 * jax.random.normal(ks[2], (NUM_BUCKETS, A_HEADS), f32),
        "ffn_norm1": gain(ks[3], (DEPTH, D_MODEL)),
        "ffn1_wg": w(ks[4], (DEPTH, D_MODEL, D_FF), D_MODEL),
        "ffn1_wu": w(ks[5], (DEPTH, D_MODEL, D_FF), D_MODEL),
        "ffn1_wd": w(ks[6], (DEPTH, D_FF, D_MODEL), D_FF),
        "mix_norm": gain(ks[7], (DEPTH, D_MODEL)),
        "ffn_norm2": gain(ks[8], (DEPTH, D_MODEL)),
        "ffn2_wg": w(ks[9], (DEPTH, D_MODEL, D_FF), D_MODEL),
        "ffn2_wu": w(ks[10], (DEPTH, D_MODEL, D_FF), D_MODEL),
        "ffn2_wd": w(ks[11], (DEPTH, D_FF, D_MODEL), D_FF),
        "a_w_in": w(ks[12], (N_A_LAYERS, D_MODEL, A_IN_COLS), D_MODEL),
        "a_q_gain": gain(ks[13], (N_A_LAYERS, A_HEAD_DIM)),
        "a_k_gain": gain(ks[14], (N_A_LAYERS, A_HEAD_DIM)),
        "a_sinks": jax.random.normal(ks[15], (N_A_LAYERS, A_HEADS), f32),
        "a_w_out": w(ks[16], (N_A_LAYERS, A_HEADS * A_HEAD_DIM, D_MODEL), A_HEADS * A_HEAD_DIM),
        "b_w_in": w(ks[17], (N_B_LAYERS, D_MODEL, B_IN_COLS), D_MODEL),
        "b_q_norm": gain(ks[18], (N_B_LAYERS, Q_LORA)),
        "b_kv_norm": gain(ks[19], (N_B_LAYERS, KV_LORA)),
        "b_w_uq": w(ks[20], (N_B_LAYERS, Q_LORA, B_HEADS * (QK_NOPE + QK_ROPE)), Q_LORA),
        "b_w_ukv": w(ks[21], (N_B_LAYERS, KV_LORA, B_HEADS * (QK_NOPE + V_DIM)), KV_LORA),
        "b_q_gain": gain(ks[22], (N_B_LAYERS, QK_NOPE + QK_ROPE)),
        "b_k_gain": gain(ks[23], (N_B_LAYERS, QK_NOPE + QK_ROPE)),
        "b_w_out": w(ks[24], (N_B_LAYERS, B_HEADS * V_DIM, D_MODEL), B_HEADS * V_DIM),
    }


def reference(x, positions, rel_bias, ffn_norm1, ffn1_wg, ffn1_wu, ffn1_wd, mix_norm,
              ffn_norm2, ffn2_wg, ffn2_wu, ffn2_wd, a_w_in, a_q_gain, a_k_gain, a_sinks,
              a_w_out, b_w_in, b_q_norm, b_kv_norm, b_w_uq, b_w_ukv, b_q_gain, b_k_gain,
              b_w_out):
    for i in range(DEPTH):
        x = x + 0.5 * swiglu(rmsnorm(x, ffn_norm1[i]), ffn1_wg[i], ffn1_wu[i], ffn1_wd[i])
        h = rmsnorm(x, mix_norm[i])
        j = i // N_MIXERS
        if i % N_MIXERS == 0:
            x = x + sliding_window_attention(h, positions, rel_bias, a_w_in[j], a_q_gain[j],
                                             a_k_gain[j], a_sinks[j], a_w_out[j])
        else:
            x = x + latent_attention(h, positions, b_w_in[j], b_q_norm[j], b_kv_norm[j],
                                     b_w_uq[j], b_w_ukv[j], b_q_gain[j], b_k_gain[j], b_w_out[j])
        x = x + 0.5 * swiglu(rmsnorm(x, ffn_norm2[i]), ffn2_wg[i], ffn2_wu[i], ffn2_wd[i])
    return x
```

```python
import numpy as np
import concourse.bass as bass
import concourse.mybir as mybir
from concourse.bass_utils import run_bass_kernel_spmd

import os
DBG_STOP = int(os.environ.get("DBG_STOP", "0"))
DBG_DUMP = int(os.environ.get("DBG_DUMP", "0"))
F32 = mybir.dt.float32
BF16 = mybir.dt.bfloat16
I32 = mybir.dt.int32
ALU = mybir.AluOpType
AF = mybir.ActivationFunctionType
AX = mybir.AxisListType


class Tile:
    __slots__ = ("name", "t", "writer", "readers", "dma_sem", "psum")

    def __init__(self, name, t):
        self.psum = False
        self.name = name
        self.t = t
        self.writer = None
        self.readers = []
        self.dma_sem = None

    def __getitem__(self, k):
        return self.t[k]


class Ins:
    __slots__ = ("eng", "fn", "dma", "deps", "signals", "sem", "val", "idx", "dsem_tile")

    def __init__(self, eng, fn, dma):
        self.eng = eng
        self.fn = fn
        self.dma = dma
        self.deps = []
        self.signals = False
        self.sem = None
        self.val = None
        self.idx = None
        self.dsem_tile = None


ENGS = ("pe", "act", "dve", "pool", "sp")


class Prog:
    def __init__(self, nc):
        self.nc = nc
        self.streams = {e: [] for e in ENGS}
        self.n = 0
        self.store_tile = Tile("__store__", None)

    def tile(self, name, t):
        return Tile(name, t)

    def op(self, eng, fn, reads=(), writes=(), dma=False, accum=False):
        ins = Ins(eng, fn, dma)
        ins.idx = self.n
        self.n += 1
        deps = {}

        def add(d, kind):
            if d is None:
                return
            if (not d.dma) and (not dma) and d.eng == eng:
                if eng == "pe" or kind == "war":
                    return
            deps[d.idx] = d

        for t in reads:
            add(t.writer, "raw")
            if t.psum:
                for r in t.readers:
                    if r.eng != eng:
                        add(r, "rr")
        if not accum:
            for t in writes:
                add(t.writer, "waw")
                for r in t.readers:
                    add(r, "war")
        ins.deps = list(deps.values())
        for d in ins.deps:
            d.signals = True
        for t in reads:
            if not dma:
                t.readers = [r for r in t.readers if r.dma or r.eng != eng]
            t.readers.append(ins)
        if not accum:
            for t in writes:
                t.writer = ins
                t.readers = []
        else:
            for t in writes:
                t.writer = ins
        if dma:
            ins.dsem_tile = writes[0] if writes else self.store_tile
        self.streams[eng].append(ins)
        return ins

    def barrier(self, engs=("pe", "act", "dve", "pool")):
        toks = {}
        for e in engs:
            t = Tile("bar_" + e, None)
            last = None
            for i in reversed(self.streams[e]):
                if not i.dma:
                    last = i
                    break
            t.writer = last
            toks[e] = t
        for e in engs:
            self.op(e, (lambda en: en.nop()), reads=[toks[x] for x in engs if x != e and toks[x].writer is not None])

    def emit(self, final_wait_eng="sp"):
        nc = self.nc
        engobj = {"pe": nc.tensor, "act": nc.scalar, "dve": nc.vector, "pool": nc.gpsimd, "sp": nc.sync}
        SEM_EPOCH = 1024
        stack = []
        nsem = [0]

        def new_sem(tag):
            cm = nc.semaphore("%s_%d" % (tag, nsem[0]))
            nsem[0] += 1
            h = cm.__enter__()
            stack.append(cm)
            return h

        esem = {e: [] for e in ENGS}
        dsem = {}
        all_ins = sorted((i for e in ENGS for i in self.streams[e]), key=lambda i: i.idx)
        cnt = {e: 0 for e in ENGS}
        dcnt = {}
        for ins in all_ins:
            if ins.dma:
                t = ins.dsem_tile
                k = id(t)
                if k not in dsem:
                    dsem[k] = []
                    dcnt[k] = 0
                n = dcnt[k]
                dcnt[k] += 1
                ep = n // (SEM_EPOCH // 16)
                if ep >= len(dsem[k]):
                    dsem[k].append(new_sem("d"))
                ins.sem = dsem[k][ep]
                ins.val = (n % (SEM_EPOCH // 16) + 1) * 16
            elif ins.signals:
                e = ins.eng
                r = cnt[e]
                cnt[e] += 1
                ep = r // SEM_EPOCH
                if ep >= len(esem[e]):
                    esem[e].append(new_sem("s_" + e))
                ins.sem = esem[e][ep]
                ins.val = r % SEM_EPOCH + 1
        self.n_sems = nsem[0]
        self.sig_counts = dict(cnt)
        self.ins_counts = {e: len(self.streams[e]) for e in ENGS}
        if os.environ.get("PROG_STATS"):
            print("PROG_STATS sems", self.n_sems, "signals", cnt, "instrs", self.ins_counts)
        self.n_waits = 0
        store_waits = []
        ks = id(self.store_tile)
        if ks in dsem:
            n = dcnt[ks]
            per = SEM_EPOCH // 16
            for ep, sm in enumerate(dsem[ks]):
                last = min(n - ep * per, per)
                store_waits.append((sm, last * 16))

        def run_stream(e, eng):
            known = {}
            for ins in self.streams[e]:
                for d in ins.deps:
                    key = id(d.sem)
                    if known.get(key, 0) >= d.val:
                        continue
                    eng.wait_ge(d.sem, d.val)
                    known[key] = d.val
                    self.n_waits += 1
                bi = ins.fn(eng)
                if ins.dma:
                    bi.then_inc(ins.sem, 16)
                elif ins.signals:
                    bi.then_inc(ins.sem, 1)
            if e == final_wait_eng:
                for sm, v in store_waits:
                    eng.wait_ge(sm, v)

        with nc.Block() as block:
            @block.tensor
            def _(eng):
                run_stream("pe", eng)

            @block.scalar
            def _(eng):
                run_stream("act", eng)

            @block.vector
            def _(eng):
                run_stream("dve", eng)

            @block.gpsimd
            def _(eng):
                run_stream("pool", eng)

            @block.sync
            def _(eng):
                run_stream("sp", eng)
        for cm in reversed(stack):
            cm.__exit__(None, None, None)


D = 1024
S = 2048
NB = 16
DFF = 2816
NFC = 22
DEPTH = 2
EPS = 1e-6
TT = 512
NT = S // TT
FFN_GROUPS = ((0, 6), (6, 12), (12, 17), (17, 22))
WSLOT = 1024


def chunkT(W):
    K, F = W.shape
    return np.ascontiguousarray(
        W.reshape(K // 128, 128, F // 128, 128).transpose(2, 1, 0, 3)).reshape(F // 128, 128, K)


class Builder:
    def __init__(self, stages):
        self.stages = stages
        nc = bass.Bass("TRN2", target_bir_lowering=False)
        self.nc = nc
        self.P = Prog(nc)
        self.sb_off = self.SB_BASE
        self.ntile = 0
        self.wreq = []
        self.wtiles = {}
        self.wnext = 0
        self.cast_rr = 0
        self.fence_new = False

    SB_BASE = 16512
    SB_TOP = 229344

    def sb_raw(self, name, shape, dt):
        self.ntile += 1
        esz = {F32: 4, BF16: 2, I32: 4}[dt]
        nbytes = esz * int(np.prod(shape[1:]))
        nbytes = (nbytes + 31) // 32 * 32
        off = self.sb_off
        assert off + nbytes <= self.SB_TOP, "SBUF arena overflow at %s: need %d, have %d" % (name, nbytes, self.SB_TOP - off)
        self.sb_off += nbytes
        return self.nc.alloc_sbuf_tensor_at("%s_%d" % (name, self.ntile), list(shape), dt, offset=off)

    def sb(self, name, shape, dt):
        t = self.P.tile(name, self.sb_raw(name, shape, dt))
        if self.fence_new:
            t.readers = [st[-1] for st in (self.P.streams[e] for e in ("pe", "act", "dve", "pool")) if st and not st[-1].dma]
        return t

    def psum_banks(self):
        self.PS = [self.P.tile("ps%d" % i, self.nc.alloc_psum_tensor("ps%d" % i, [128, 512], F32))
                   for i in range(8)]
        for t in self.PS:
            t.psum = True

    def w_setup(self, nstage, nslots, depth):
        self.stg = [self.sb("stg", [128, WSLOT], F32) for _ in range(nstage)]
        self.wsl = [self.sb("wsl", [128, WSLOT], BF16) for _ in range(nslots)]
        self.stg_i = 0
        self.wsl_i = 0
        self.slot_owner = {}
        self.live = []
        self.wdepth = depth

    def w_declare(self, dram_ap, width):
        self.wreq.append((dram_ap, width))
        return len(self.wreq) - 1

    def w_issue_upto(self, idx):
        P = self.P
        idx = min(idx, len(self.wreq) - 1)
        while self.wnext <= idx:
            dram_ap, width = self.wreq[self.wnext]
            st = self.stg[self.stg_i % len(self.stg)]
            self.stg_i += 1
            sl = self.wsl[self.wsl_i % len(self.wsl)]
            self.wsl_i += 1
            P.op("sp", (lambda e, st=st, a=dram_ap, w=width: e.dma_start(out=st[:, 0:w], in_=a)),
                 writes=[st], dma=True)
            if self.cast_rr % 2 == 0:
                P.op("dve", (lambda e, st=st, sl=sl, w=width: e.tensor_copy(out=sl[:, 0:w], in_=st[:, 0:w])),
                     reads=[st], writes=[sl])
            else:
                P.op("act", (lambda e, st=st, sl=sl, w=width: e.activation(out=sl[:, 0:w], in_=st[:, 0:w], func=AF.Copy)),
                     reads=[st], writes=[sl])
            self.cast_rr += 1
            self.wtiles[self.wnext] = sl
            self.slot_owner[id(sl)] = self.wnext
            self.wnext += 1

    def w_get(self, idx):
        self.w_issue_upto(idx + self.wdepth)
        sl = self.wtiles[idx]
        assert self.slot_owner[id(sl)] == idx, "weight ring overrun"
        return sl

    def rmsnorm(self, gcol0):
        P = self.P
        for T in range(NT):
            ts = slice(T * TT, (T + 1) * TT)
            ss = self.PS[self.ps_rr % 8]
            self.ps_rr += 1
            for c in range(8):
                sq = self.sq[self.sq_i % len(self.sq)]
                self.sq_i += 1
                x = self.xT[c][T]
                P.op("act", (lambda e, sq=sq, x=x, c=c, ts=ts: e.activation(out=sq[:, :], in_=self.xT_t[:, c, ts], func=AF.Square)),
                     reads=[x], writes=[sq])
                P.op("pe", (lambda e, ss=ss, sq=sq, c=c: e.matmul(ss[:, :], lhsT=self.ones[:, :], rhs=sq[:, :], start=(c == 0), stop=(c == 7))),
                     reads=[self.ones, sq], writes=[ss], accum=(c != 0))
            rs = self.rstd[T]
            P.op("act", (lambda e, rs=rs, ss=ss: e.activation(out=rs[:, :], in_=ss[:, :], func=AF.Ln, scale=1.0 / D, bias=self.epsc[:, 0:1])),
                 reads=[ss, self.epsc], writes=[rs])
            P.op("act", (lambda e, rs=rs: e.activation(out=rs[:, :], in_=rs[:, :], func=AF.Exp, scale=-0.5)),
                 reads=[rs], writes=[rs])
            for c in range(8):
                h = self.hT[c][T]
                x = self.xT[c][T]
                P.op("dve", (lambda e, c=c, ts=ts, rs=rs: e.scalar_tensor_tensor(
                    out=self.hT_t[:, c, ts], in0=self.xT_t[:, c, ts], scalar=self.par[:, gcol0 + c:gcol0 + c + 1],
                    in1=rs[:, :], op0=ALU.mult, op1=ALU.mult)),
                     reads=[x, rs, self.par_t], writes=[h])

    def ffn_declare(self, li):
        req = []
        for (f0, f1) in FFN_GROUPS:
            g = {"gu": [], "d": []}
            for fc in range(f0, f1):
                g["gu"].append((self.w_declare(self.ffnw[li, fc, 0], WSLOT), self.w_declare(self.ffnw[li, fc, 1], WSLOT)))
            for fc in range(f0, f1):
                g["d"].append(self.w_declare(self.ffnw[li, fc, 2], WSLOT))
            req.append(g)
        return req

    def ffn_run(self, req):
        P = self.P
        for gi, (f0, f1) in enumerate(FFN_GROUPS):
            n = f1 - f0
            for k in range(n):
                ig, iu = req[gi]["gu"][k]
                wg = self.w_get(ig)
                wu = self.w_get(iu)
                for T in range(NT):
                    ts = slice(T * TT, (T + 1) * TT)
                    gp = self.PS[self.ps_rr % 8]
                    up = self.PS[(self.ps_rr + 1) % 8]
                    self.ps_rr += 2
                    for (pt, w) in ((gp, wg), (up, wu)):
                        for c in range(8):
                            P.op("pe", (lambda e, pt=pt, w=w, c=c, ts=ts: e.matmul(
                                pt[:, :], lhsT=w[:, c * 128:(c + 1) * 128], rhs=self.hT_t[:, c, ts],
                                start=(c == 0), stop=(c == 7))),
                                 reads=[w, self.hT[c][T]], writes=[pt], accum=(c != 0))
                    sg = self.sg[self.sg_i % len(self.sg)]
                    self.sg_i += 1
                    P.op("act", (lambda e, sg=sg, gp=gp: e.activation(out=sg[:, :], in_=gp[:, :], func=AF.Silu)),
                         reads=[gp], writes=[sg])
                    a = self.act[k]
                    P.op("dve", (lambda e, a=a, sg=sg, up=up, ts=ts: e.tensor_tensor(
                        out=a[:, ts], in0=up[:, :], in1=sg[:, :], op=ALU.mult)),
                         reads=[sg, up], writes=[self.act_T[k][T]])
            wds = [self.w_get(i) for i in req[gi]["d"]]
            for dc in range(8):
                for T in range(NT):
                    ts = slice(T * TT, (T + 1) * TT)
                    op_ = self.PS[self.ps_rr % 8]
                    self.ps_rr += 1
                    for k in range(n):
                        P.op("pe", (lambda e, op_=op_, w=wds[k], k=k, dc=dc, ts=ts, n=n: e.matmul(
                            op_[:, :], lhsT=w[:, dc * 128:(dc + 1) * 128], rhs=self.act[k][:, ts],
                            start=(k == 0), stop=(k == n - 1))),
                             reads=[wds[k], self.act_T[k][T]], writes=[op_], accum=(k != 0))
                    x = self.xT[dc][T]
                    P.op("dve", (lambda e, op_=op_, dc=dc, ts=ts: e.scalar_tensor_tensor(
                        out=self.xT_t[:, dc, ts], in0=op_[:, :], scalar=0.5, in1=self.xT_t[:, dc, ts],
                        op0=ALU.mult, op1=ALU.add)),
                         reads=[op_, x], writes=[x])

    def dump(self, name, get_ap, cols, reads, rows=128):
        nc, P = self.nc, self.P
        d = nc.dram_tensor("dbg_" + name, [rows, cols], F32, kind="ExternalOutput").ap()
        if not hasattr(self, "dbg_t"):
            save = self.sb_off
            assert self.sb_off <= self.SB_TOP - 12288 - 64, "no room for debug tile"
            self.sb_off = self.SB_TOP - 12288 - 64
            self.dbg_t = self.sb("dbg_t", [128, 3072], F32)
            self.sb_off = save
        t = self.dbg_t
        P.op("dve", lambda e: e.tensor_copy(out=t[0:rows, 0:cols], in_=get_ap()), reads=list(reads), writes=[t])
        P.op("sp", lambda e: e.dma_start(out=d[:, :], in_=t[0:rows, 0:cols]), reads=[t], dma=True)
        self.dbg_names.append("dbg_" + name)

    def setup(self):
        self.dbg_names = []
        nc, P = self.nc, self.P
        self.xT_d = nc.dram_tensor("xT", [D, S], F32, kind="ExternalInput").ap()
        self.par_d = nc.dram_tensor("par", [128, NPAR], F32, kind="ExternalInput").ap()
        if set(k for k, _ in self.stages) & {"ffn1", "ffn2"}:
            self.ffnw = nc.dram_tensor("ffnw", [DEPTH * 2, NFC, 3, 128, WSLOT], F32, kind="ExternalInput").ap()
        self.out_d = nc.dram_tensor("outT", [D, S], F32, kind="ExternalOutput").ap()
        self.psum_banks()
        self.ps_rr = 0
        self.xT_t = self.sb_raw("xTs", [128, 8, S], F32)
        self.xT = [[P.tile("x%d_%d" % (c, T), None) for T in range(NT)] for c in range(8)]
        self.hT_t = self.sb_raw("hTs", [128, 8, S], BF16)
        self.hT = [[P.tile("h%d_%d" % (c, T), None) for T in range(NT)] for c in range(8)]
        self.par_t = self.sb("par", [128, NPAR], F32)
        self.par = self.par_t
        self.ones = self.sb("ones", [128, 128], BF16)
        self.epsc = self.sb("epsc", [128, 1], F32)
        self.rstd = [self.sb("rstd", [128, TT], F32) for _ in range(NT)]
        self.sq = [self.sb("sq", [128, TT], BF16) for _ in range(2)]
        self.sq_i = 0
        self.sg = [self.sb("sg", [128, TT], F32) for _ in range(2)]
        self.sg_i = 0
        nact = 6
        self.act = [self.sb("act", [128, S], BF16) for _ in range(nact)]
        self.act_T = [[P.tile("a%d_%d" % (k, T), None) for T in range(NT)] for k in range(nact)]
        self.otx = self.sb("otx", [128, S], BF16)
        self.otx_T = [P.tile("otx", None) for _ in range(NT)]
        self.otx2 = [self.sb("otx2", [128, S], BF16) for _ in range(2)]
        self.otx2_T = [[P.tile("otx2", None) for _ in range(NT)] for _ in range(2)]
        self.w_setup(nstage=2, nslots=10, depth=4)
        self.mix_mark = None
        P.op("dve", lambda e: e.memset(self.ones[:, :], 1.0), writes=[self.ones])
        P.op("dve", lambda e: e.memset(self.epsc[:, :], EPS), writes=[self.epsc])
        P.op("sp", lambda e: e.dma_start(out=self.par_t[:, :], in_=self.par_d[:, :]), writes=[self.par_t], dma=True)
        for c in range(8):
            for T in range(NT):
                ts = slice(T * TT, (T + 1) * TT)
                P.op("sp", (lambda e, c=c, ts=ts: e.dma_start(out=self.xT_t[:, c, ts], in_=self.xT_d[c * 128:(c + 1) * 128, ts])),
                     writes=[self.xT[c][T]], dma=True)

    def finish(self):
        P = self.P
        for c in range(8):
            P.op("sp", (lambda e, c=c: e.dma_start(out=self.out_d[c * 128:(c + 1) * 128, :], in_=self.xT_t[:, c, :])),
                 reads=[self.xT[c][T] for T in range(NT)], dma=True)
        P.emit()


PC_FN1 = 0
PC_MIX = 16
PC_FN2 = 32
NPAR = 128


def build_program(stages):
    b = Builder(stages)
    b.setup()
    kinds = set(k for k, _ in stages)
    if "mixA" in kinds:
        mixA_setup(b)
        mixA_tables(b)
    if "mixB" in kinds:
        mixB_setup(b)
    reqs = {}
    for st in stages:
        kind, l = st
        if kind == "ffn1":
            reqs[st] = b.ffn_declare(l * 2 + 0)
        elif kind == "ffn2":
            reqs[st] = b.ffn_declare(l * 2 + 1)
        elif kind == "mixA":
            reqs[st] = mixA_declare(b)
        elif kind == "mixB":
            reqs[st] = mixB_declare(b)
    pending_consts = False
    for st in stages:
        kind, l = st
        if os.environ.get("STAGE_BARRIER") == "1":
            b.P.barrier()
        if kind in ("ffn1", "ffn2") and pending_consts:
            b.rmsnorm((PC_FN1 if kind == "ffn1" else PC_FN2) + 8 * l)
            mixB_consts(b)
            pending_consts = False
            b.ffn_run(reqs[st])
            continue
        if kind == "ffn1":
            b.rmsnorm(PC_FN1 + 8 * l)
            if os.environ.get("FFN1_TRUNC") != "1":
                b.ffn_run(reqs[st])
        elif kind == "ffn2":
            b.rmsnorm(PC_FN2 + 8 * l)
            if os.environ.get("FFN2_TRUNC") != "1":
                b.ffn_run(reqs[st])
        elif kind == "mixA":
            mixA_run(b, reqs[st], l)
            pending_consts = "mixB" in kinds
        elif kind == "mixB":
            mixB_run(b, reqs[st], l)
    b.finish()
    return b


def host_prep(inputs):
    f = lambda k: np.asarray(inputs[k], dtype=np.float32)
    par = np.zeros((128, NPAR), np.float32)
    for l in range(DEPTH):
        par[:, PC_FN1 + 8 * l:PC_FN1 + 8 * l + 8] = f("ffn_norm1")[l].reshape(8, 128).T
        par[:, PC_MIX + 8 * l:PC_MIX + 8 * l + 8] = f("mix_norm")[l].reshape(8, 128).T
        par[:, PC_FN2 + 8 * l:PC_FN2 + 8 * l + 8] = f("ffn_norm2")[l].reshape(8, 128).T
    ffnw = np.empty((DEPTH * 2, NFC, 3, 128, WSLOT), np.float32)
    for l in range(DEPTH):
        for i, pre in enumerate(("ffn1", "ffn2")):
            ffnw[l * 2 + i, :, 0] = chunkT(f(pre + "_wg")[l])
            ffnw[l * 2 + i, :, 1] = chunkT(f(pre + "_wu")[l])
            ffnw[l * 2 + i, :, 2] = f(pre + "_wd")[l].reshape(NFC, 128, D)
    shared = {"par": par, "ffnw": ffnw}
    gq = f("a_q_gain")[0]; gk = f("a_k_gain")[0]
    par[:, PC_AQG] = np.tile(gq, 2)
    par[:, PC_AKG] = np.tile(gk, 2)
    par[:, PC_ASINK:PC_ASINK + 16] = f("a_sinks")[0][None, :]
    win = f("a_w_in")[0]
    aw = np.empty((19, 128, WSLOT), np.float32)
    aw[A_SLOT_Q:A_SLOT_Q + 8] = chunkT(win[:, 0:1024])
    for c in range(2):
        kc = win[:, 1024 + c * 64:1024 + (c + 1) * 64]
        aw[A_SLOT_K + c] = chunkT(np.concatenate([kc, kc], axis=1))[0]
    aw[A_SLOT_V] = chunkT(win[:, 1152:1280])[0]
    aw[A_SLOT_O:A_SLOT_O + 8] = f("a_w_out")[0].reshape(8, 128, D)
    shared["aw"] = aw
    shared["rbT"] = np.ascontiguousarray(f("rel_bias").T)
    bwin = f("b_w_in")[0]
    bw = np.zeros((12, 128, WSLOT), np.float32)
    bw[B_SLOT_C:B_SLOT_C + 3] = chunkT(bwin[:, 0:384])
    krc = bwin[:, 384:416]
    kr64 = np.concatenate([krc, krc[:, 16:32], krc[:, 0:16]], axis=1)
    kr128 = np.concatenate([kr64, kr64], axis=1)
    bw[B_SLOT_KR] = kr128.reshape(8, 128, 128).transpose(1, 0, 2).reshape(128, 1024)
    bw[B_SLOT_O:B_SLOT_O + 8] = f("b_w_out")[0].reshape(8, 128, D)
    shared["bw"] = bw
    wuq = f("b_w_uq")[0].reshape(256, 16, 96)
    wukv = f("b_w_ukv")[0].reshape(128, 16, 128)
    bwh = np.empty((16, 128, 384), np.float32)
    for h in range(16):
        q128 = np.concatenate([wuq[:, h, 64:96], wuq[:, h, 80:96], wuq[:, h, 64:80], wuq[:, h, 0:64]], axis=1)
        bwh[h, :, 0:128] = q128[0:128]
        bwh[h, :, 128:256] = q128[128:256]
        bwh[h, :, 256:384] = wukv[:, h, :]
    shared["bwh"] = bwh
    par[:, PC_BQN:PC_BQN + 2] = f("b_q_norm")[0].reshape(2, 128).T
    par[:, PC_BKVN] = f("b_kv_norm")[0]
    gq = f("b_q_gain")[0]; gk = f("b_k_gain")[0]
    par[0:32, PC_BQG] = gq[64:96]
    par[32:48, PC_BQG] = gq[80:96]
    par[48:64, PC_BQG] = gq[64:80]
    par[64:128, PC_BQG] = gq[0:64]
    par[64:128, PC_BKG] = gk[0:64]
    par[0:32, PC_BKRG] = gk[64:96]
    par[32:48, PC_BKRG] = gk[80:96]
    par[48:64, PC_BKRG] = gk[64:80]
    inv_freq = (np.float32(10000.0) ** (-np.arange(0, 32, 2, dtype=np.float32) / np.float32(32))).astype(np.float32)
    cstt = np.zeros((128, 4), np.float32)
    for base in (0, 64):
        for i in range(32):
            cstt[base + i, CST_F] = inv_freq[i % 16]
            cstt[base + i, CST_PH] = np.float32(np.pi / 2)
            cstt[base + 32 + i, CST_F] = -inv_freq[i % 16] if i < 16 else inv_freq[i % 16]
            cstt[base + 32 + i, CST_PH] = 0.0
    shared["cst"] = cstt
    x = f("x")
    pos = np.asarray(inputs["positions"]).astype(np.int32)
    percore = [{"xT": np.ascontiguousarray(x[b].T), "pos": pos[b][None, :]} for b in range(x.shape[0])]
    return shared, percore


ALL_STAGES = (("ffn1", 0), ("mixA", 0), ("ffn2", 0), ("ffn1", 1), ("mixB", 1), ("ffn2", 1))


def run(inputs, stages=ALL_STAGES, ncores=8, trace=False):
    shared, percore = host_prep(inputs)
    b = build_program(stages)
    names = set(["xT", "par"])
    kinds = set(k for k, _ in stages)
    if kinds & {"ffn1", "ffn2"}:
        names |= {"ffnw"}
    if "mixA" in kinds:
        names |= {"aw", "rbT", "pos"}
    if "mixB" in kinds:
        names |= {"bw", "bwh", "pos", "cst"}
    in_maps = [{k: v for k, v in dict(shared, **percore[i]).items() if k in names} for i in range(ncores)]
    res = run_bass_kernel_spmd(b.nc, in_maps, core_ids=list(range(ncores)), trace=trace)
    out = np.stack([np.ascontiguousarray(r["outT"].T) for r in res.results], axis=0)
    res.dbg = {k: res.results[0][k] for k in b.dbg_names}
    return out, res


def kernel(**inputs):
    out, _ = run(inputs)
    return out.astype(np.float32)


NEGB = -30000.0
T5_THR = [float(j) for j in range(1, 17)] + [float(int(np.ceil(16.0 * 8.0 ** (j / 16.0) - 1e-9))) for j in range(1, 16)]
A_SLOT_Q, A_SLOT_K, A_SLOT_V, A_SLOT_O = 0, 8, 10, 11
PC_AQG, PC_AKG, PC_ASINK = 48, 49, 50


def mixA_setup(b):
    nc, P = b.nc, b.P
    b.aw = nc.dram_tensor("aw", [19, 128, WSLOT], F32, kind="ExternalInput").ap()
    b.pos_d = nc.dram_tensor("pos", [1, S], I32, kind="ExternalInput").ap()
    b.rbT_d = nc.dram_tensor("rbT", [16, 32], F32, kind="ExternalInput").ap()
    b.gscr = nc.dram_tensor("gscr", [16, 128, 384], F32, kind="Internal")
    b.blk64 = b.sb("blk64", [128, 128], BF16)
    P.op("dve", lambda e: e.memset(b.blk64[:, :], 0.0), writes=[b.blk64])
    P.op("dve", lambda e: e.memset(b.blk64[0:64, 0:64], 1.0), writes=[b.blk64])
    P.op("dve", lambda e: e.memset(b.blk64[64:128, 64:128], 1.0), writes=[b.blk64])
    b.esink = b.sb("esink", [128, 16], F32)
    P.op("act", lambda e: e.activation(out=b.esink[:, :], in_=b.par[:, PC_ASINK:PC_ASINK + 16], func=AF.Exp),
         reads=[b.par_t], writes=[b.esink])


def mixA_declare(b):
    r = {}
    r["k"] = [b.w_declare(b.aw[A_SLOT_K + c], WSLOT) for c in range(2)]
    r["v"] = b.w_declare(b.aw[A_SLOT_V], WSLOT)
    r["q"], r["o"] = [None] * 8, [None] * 8
    for kind, i in (("q", 0), ("q", 1), ("q", 2), ("q", 3), ("q", 4), ("o", 0), ("o", 1), ("o", 2), ("o", 3),
                    ("q", 5), ("q", 6), ("q", 7), ("o", 4), ("o", 5), ("o", 6), ("o", 7)):
        r[kind][i] = b.w_declare(b.aw[(A_SLOT_Q if kind == "q" else A_SLOT_O) + i], WSLOT)
    order = r["k"] + [r["v"]]
    return r


def proj_headnorm(b, w, gcol, out_t, out_tiles, nrm_lhsT, hd, sq_pool, tmp_pool):
    P = b.P
    for T in range(NT):
        ts = slice(T * TT, (T + 1) * TT)
        raw = b.PS[b.ps_rr % 8]
        ss = b.PS[(b.ps_rr + 1) % 8]
        b.ps_rr += 2
        for c in range(8):
            P.op("pe", (lambda e, raw=raw, c=c, ts=ts: e.matmul(raw[:, :], lhsT=w[:, c * 128:(c + 1) * 128], rhs=b.hT_t[:, c, ts],
                                                               start=(c == 0), stop=(c == 7))),
                 reads=[w, b.hT[c][T]], writes=[raw], accum=(c != 0))
        sq = b.sq[b.sq_i % len(b.sq)]
        b.sq_i += 1
        P.op("act", (lambda e, sq=sq, raw=raw: e.activation(out=sq[:, :], in_=raw[:, :], func=AF.Square)), reads=[raw], writes=[sq])
        P.op("pe", (lambda e, ss=ss, sq=sq: e.matmul(ss[:, :], lhsT=nrm_lhsT[:, :], rhs=sq[:, :], start=True, stop=True)),
             reads=[nrm_lhsT, sq], writes=[ss])
        rs = b.sg[b.sg_i % len(b.sg)]
        b.sg_i += 1
        P.op("act", (lambda e, rs=rs, ss=ss: e.activation(out=rs[:, :], in_=ss[:, :], func=AF.Ln, scale=1.0 / hd, bias=b.epsc[:, 0:1])),
             reads=[ss, b.epsc], writes=[rs])
        P.op("act", (lambda e, rs=rs: e.activation(out=rs[:, :], in_=rs[:, :], func=AF.Exp, scale=-0.5)), reads=[rs], writes=[rs])
        P.op("dve", (lambda e, raw=raw, rs=rs, ts=ts: e.scalar_tensor_tensor(
            out=out_t[:, ts], in0=raw[:, :], scalar=b.par[:, gcol:gcol + 1], in1=rs[:, :], op0=ALU.mult, op1=ALU.mult)),
             reads=[raw, rs, b.par_t], writes=[out_tiles[T]])


def mixA_tables(b):
    nc, P = b.nc, b.P
    if not hasattr(b, "a_alloc"):
        b.a_alloc = True
        if b.mix_mark is None:
            b.mix_mark = b.sb_off
        b.sb_off = b.mix_mark
        b.fence_new = True
        b.kdup = b.act[0:2]
        b.kdup_T = b.act_T[0:2]
        b.qn = b.act[2:4]
        b.qn_T = b.act_T[2:4]
        b.OTp = [b.act[5], b.otx, b.otx2[0], b.otx2[1]]
        b.OTp_T = [b.act_T[5], b.otx_T, b.otx2_T[0], b.otx2_T[1]]
        b.pt = [(b.act[4][:, i * 512:(i + 1) * 512], b.act_T[4][i]) for i in range(4)]
        b.vx = [b.sb("vx", [128, NB, 192], BF16) for _ in range(2)]
        b.biasm = b.sb("biasm", [128, 8, 4, 128], F32)
        b.sc = b.rstd[0:2]
        b.dn = b.rstd[2:4]
        b.sc_i = b.pt_i = b.dn_i = 0
        posr_i = b.sb("posr_i", [16, 128], I32)
        posr = b.sb("posr", [16, 128], F32)
        dist = b.sb("dist", [16, 128], F32)
        rbT = b.sb("rbT", [16, 32], F32)
        dif = b.sb("dif", [16, 32], F32)
        acc = b.sb("bacc", [16, 128], F32)
        tmp = b.sb("btmp", [16, 128], F32)
        G = b.sb("G", [16, 384], F32)
        P.op("sp", lambda e: e.dma_start(out=posr_i[:, :], in_=b.pos_d[0:1, 0:128].partition_broadcast(16)), writes=[posr_i], dma=True)
        P.op("sp", lambda e: e.dma_start(out=rbT[:, :], in_=b.rbT_d[:, :]), writes=[rbT], dma=True)
        P.op("dve", lambda e: e.tensor_copy(out=posr[:, :], in_=posr_i[:, :]), reads=[posr_i], writes=[posr])
        P.op("dve", lambda e: e.tensor_scalar(out=dist[:, :], in0=posr[:, :], scalar1=posr[:, 0:1], scalar2=None, op0=ALU.subtract),
             reads=[posr], writes=[dist])
        P.op("dve", lambda e: e.tensor_tensor(out=dif[:, 1:32], in0=rbT[:, 1:32], in1=rbT[:, 0:31], op=ALU.subtract), reads=[rbT], writes=[dif])
        P.op("dve", lambda e: e.tensor_scalar(out=acc[:, :], in0=dist[:, :], scalar1=0.0, scalar2=rbT[:, 0:1], op0=ALU.mult, op1=ALU.add),
             reads=[dist, rbT], writes=[acc])
        for j in range(1, 32):
            P.op("dve", (lambda e, j=j: e.tensor_scalar(out=tmp[:, :], in0=dist[:, :], scalar1=T5_THR[j - 1], scalar2=dif[:, j:j + 1],
                                                        op0=ALU.is_ge, op1=ALU.mult)), reads=[dist, dif], writes=[tmp])
            P.op("dve", lambda e: e.tensor_tensor(out=acc[:, :], in0=acc[:, :], in1=tmp[:, :], op=ALU.add), reads=[acc, tmp], writes=[acc])
        P.op("dve", lambda e: e.memset(G[:, :], NEGB), writes=[G])
        P.op("dve", lambda e: e.tensor_copy(out=G[:, 127:255], in_=acc[:, :]), reads=[acc], writes=[G])
        gt = P.tile("gscr", None)
        P.op("sp", lambda e: e.dma_start(out=b.gscr.ap()[:, :, :], in_=G[:, :].unsqueeze(1).broadcast_to([16, 128, 384])), reads=[G], writes=[gt], dma=True)
        for h in range(16):
            for kt in range(2):
                off = h * 128 * 384 + 127 + (128 if kt == 0 else 0)
                src = bass.AP(tensor=b.gscr, offset=off, ap=[[383, 128], [1, 128]])
                P.op("sp", (lambda e, h=h, kt=kt, src=src: e.dma_start(out=b.biasm[:, h // 2, (h % 2) * 2 + kt, :], in_=src)),
                     reads=[gt], writes=[b.biasm], dma=True)
        for c in range(2):
            P.op("pool", (lambda e, c=c: e.memset(b.vx[c][:, :, :], 1.0)), writes=[b.vx[c]])
        b.fence_new = False


def mixA_run(b, r, l):
    nc, P = b.nc, b.P
    if DBG_STOP == 1:
        return
    b.rmsnorm(PC_MIX + 8 * l)
    for c in range(2):
        proj_headnorm(b, b.w_get(r["k"][c]), PC_AKG, b.kdup[c], b.kdup_T[c], b.blk64, 64, None, None)
    wv = b.w_get(r["v"])
    for g4 in range(4):
        vp = b.PS[b.ps_rr % 8]
        b.ps_rr += 1
        for j in range(4):
            n = g4 * 4 + j
            for c in range(8):
                P.op("pe", (lambda e, vp=vp, j=j, n=n, c=c: e.matmul(vp[:, j * 128:(j + 1) * 128], lhsT=b.hT_t[:, c, n * 128:(n + 1) * 128],
                                                                      rhs=wv[:, c * 128:(c + 1) * 128], start=(c == 0), stop=(c == 7))),
                     reads=[wv, b.hT[c][n // 4]], writes=[vp], accum=not (c == 0 and j == 0))
        for c2 in range(2):
            P.op("act", (lambda e, vp=vp, g4=g4, c2=c2: e.activation(
                out=b.vx[c2][:, g4 * 4:(g4 + 1) * 4, 64:128],
                in_=vp[:, :].rearrange("p (j c d) -> p j c d", j=4, c=2)[:, :, c2, :], func=AF.Copy)),
                 reads=[vp], writes=[b.vx[c2]])
    if DBG_STOP == 2 or DBG_DUMP:
        b.dump("kdup0", lambda: b.kdup[0][:, :], S, b.kdup_T[0])
        b.dump("vx0", lambda: b.vx[0][:, :, :].rearrange("p n c -> p (n c)"), NB * 192, [b.vx[0]])
    if DBG_STOP == 2:
        return
    PACC = b.PS[0:2]
    PST4 = b.PS[2:6]
    PSR = b.PS[6:8]
    psr_i = [0]

    def psr():
        t = PSR[psr_i[0] % 2]
        psr_i[0] += 1
        return t
    prepQ, bgQ = [], []
    st_i = [0]

    def qproj_tasks(oc):
        qn, qT = b.qn[oc % 2], b.qn_T[oc % 2]
        ctx = {}
        tasks = []
        for T in range(NT):
            def t(T=T):
                if "w" not in ctx:
                    ctx["w"] = b.w_get(r["q"][oc])
                w = ctx["w"]
                ts = slice(T * TT, (T + 1) * TT)
                raw, ss = psr(), psr()
                for c in range(8):
                    P.op("pe", (lambda e, c=c: e.matmul(raw[:, :], lhsT=w[:, c * 128:(c + 1) * 128], rhs=b.hT_t[:, c, ts], start=(c == 0), stop=(c == 7))),
                         reads=[w, b.hT[c][T]], writes=[raw], accum=(c != 0))
                sq = b.sq[b.sq_i % len(b.sq)]
                b.sq_i += 1
                P.op("act", (lambda e: e.activation(out=sq[:, :], in_=raw[:, :], func=AF.Square)), reads=[raw], writes=[sq])
                P.op("pe", (lambda e: e.matmul(ss[:, :], lhsT=b.blk64[:, :], rhs=sq[:, :], start=True, stop=True)), reads=[b.blk64, sq], writes=[ss])
                rs = b.sg[b.sg_i % len(b.sg)]
                b.sg_i += 1
                P.op("act", (lambda e: e.activation(out=rs[:, :], in_=ss[:, :], func=AF.Ln, scale=1.0 / 64, bias=b.epsc[:, 0:1])),
                     reads=[ss, b.epsc], writes=[rs])
                P.op("act", (lambda e: e.activation(out=rs[:, :], in_=rs[:, :], func=AF.Exp, scale=-0.5)), reads=[rs], writes=[rs])
                P.op("dve", (lambda e: e.scalar_tensor_tensor(out=qn[:, ts], in0=raw[:, :], scalar=b.par[:, PC_AQG:PC_AQG + 1], in1=rs[:, :],
                                                              op0=ALU.mult, op1=ALU.mult)), reads=[raw, rs, b.par_t], writes=[qT[T]])
            tasks.append(t)
        return tasks

    def outproj_group_tasks(g):
        ctx = {}
        tasks = []
        for dc in range(8):
            for T in range(NT):
                def t(dc=dc, T=T):
                    if "wo" not in ctx:
                        ctx["wo"] = [b.w_get(r["o"][4 * g + i]) for i in range(4)]
                    wos = ctx["wo"]
                    ts = slice(T * TT, (T + 1) * TT)
                    op_ = psr()
                    for i in range(4):
                        P.op("pe", (lambda e, i=i: e.matmul(op_[:, :], lhsT=wos[i][:, dc * 128:(dc + 1) * 128], rhs=b.OTp[i][:, ts],
                                                            start=(i == 0), stop=(i == 3))),
                             reads=[wos[i], b.OTp_T[i][T]], writes=[op_], accum=(i != 0))
                    x = b.xT[dc][T]
                    P.op("dve", (lambda e: e.tensor_tensor(out=b.xT_t[:, dc, ts], in0=op_[:, :], in1=b.xT_t[:, dc, ts], op=ALU.add)),
                         reads=[op_, x], writes=[x])
                tasks.append(t)
        return tasks

    def pop_tasks(n_bg=2):
        if prepQ:
            prepQ.pop(0)()
        for _ in range(n_bg):
            if bgQ:
                bgQ.pop(0)()

    def scores(oc, step):
        c = oc // 4
        qn, qT = b.qn[oc % 2], b.qn_T[oc % 2]
        sts = (PST4[(st_i[0] * 2) % 4], PST4[(st_i[0] * 2 + 1) % 4])
        st_i[0] += 1
        outs = []
        for hh in range(2):
            rows = slice(hh * 64, (hh + 1) * 64)
            st = sts[hh]
            first = True
            for jj in range(2):
                n = step * 2 + jj
                qs = slice(n * 128, (n + 1) * 128)
                for kt in range(2):
                    if n == 0 and kt == 0:
                        continue
                    kb = n - 1 + kt
                    sl = (jj * 2 + kt) * 128
                    P.op("pe", (lambda e, st=st, rows=rows, kb=kb, sl=sl, qs=qs: e.matmul(
                        st[:, sl:sl + 128], lhsT=b.kdup[c][rows, kb * 128:(kb + 1) * 128], rhs=qn[rows, qs], start=True, stop=True)),
                         reads=[b.kdup_T[c][kb // 4], qT[n // 4]], writes=[st], accum=not first)
                    first = False
        for hh in range(2):
            st = sts[hh]
            sc = b.sc[b.sc_i % 2]
            b.sc_i += 1
            pt, ptT = b.pt[b.pt_i % len(b.pt)]
            b.pt_i += 1
            outs.append((pt, ptT))
            P.op("dve", (lambda e, sc=sc, st=st, hh=hh: e.scalar_tensor_tensor(
                out=sc[:, :].rearrange("p (j a) -> p j a", j=2), in0=st[:, :].rearrange("p (j a) -> p j a", j=2), scalar=0.125,
                in1=b.biasm[:, oc, hh * 2:hh * 2 + 2, :].rearrange("p a q -> p (a q)").unsqueeze(1).broadcast_to([128, 2, 256]),
                op0=ALU.mult, op1=ALU.add)), reads=[st, b.biasm], writes=[sc])
            P.op("act", (lambda e, sc=sc, pt=pt: e.activation(out=pt, in_=sc[:, :], func=AF.Exp)), reads=[sc], writes=[ptT])
        return outs

    def pv(oc, step, outs):
        c = oc // 4
        for hh, vcols in ((0, slice(64, 192)), (1, slice(0, 128))):
            acc_ = PACC[hh]
            pt, ptT = outs[hh]
            for jj in range(2):
                n = step * 2 + jj
                j = n % 4
                for kt in range(2):
                    if n == 0 and kt == 0:
                        continue
                    kb = n - 1 + kt
                    sl = (jj * 2 + kt) * 128
                    P.op("pe", (lambda e, acc_=acc_, kb=kb, sl=sl, j=j, kt=kt, vcols=vcols, pt=pt, n=n: e.matmul(
                        acc_[:, j * 128:(j + 1) * 128], lhsT=b.vx[c][:, kb, vcols], rhs=pt[:, sl:sl + 128],
                        start=(kt == 0 or n == 0), stop=(kt == 1))),
                         reads=[b.vx[c], ptT], writes=[acc_], accum=not (j == 0 and (kt == 0 or n == 0)))

    def normalise(oc, bg):
        ts = slice(bg * 512, (bg + 1) * 512)
        for hh in range(2):
            acc_ = PACC[hh]
            h = 2 * oc + hh
            orow = slice(hh * 64, (hh + 1) * 64)
            drow = slice((1 - hh) * 64, (2 - hh) * 64)
            dn = b.dn[b.dn_i % 2]
            b.dn_i += 1
            P.op("act", (lambda e, dn=dn, acc_=acc_, drow=drow, h=h: e.activation(out=dn[drow, :], in_=acc_[drow, :], func=AF.Ln,
                                                                               bias=b.esink[drow, h:h + 1], scale=1.0)),
                 reads=[acc_, b.esink], writes=[dn])
            P.op("act", (lambda e, dn=dn, drow=drow: e.activation(out=dn[drow, :], in_=dn[drow, :], func=AF.Exp, scale=-1.0)), reads=[dn], writes=[dn])
            P.op("dve", (lambda e, dn=dn, acc_=acc_, drow=drow, orow=orow: e.tensor_tensor(
                out=b.OTp[oc % 4][orow, ts], in0=acc_[orow, :], in1=dn[drow, :], op=ALU.mult)),
                 reads=[acc_, dn], writes=[b.OTp_T[oc % 4][bg]], accum=(hh == 1))

    for t in qproj_tasks(0):
        t()
    for oc in range(8):
        if (DBG_STOP == 3 or DBG_DUMP) and oc == 1:
            b.dump("qn0", lambda: b.qn[0][:, :], S, b.qn_T[0])
            b.dump("ot0", lambda: b.OTp[0][:, :], S, b.OTp_T[0])
        if DBG_STOP == 3 and oc == 1:
            return
        if oc < 7:
            prepQ.extend(qproj_tasks(oc + 1))
        nxt = scores(oc, 0)
        for step in range(8):
            cur = nxt
            if step < 7:
                nxt = scores(oc, step + 1)
            pv(oc, step, cur)
            if step % 2 == 1:
                normalise(oc, step // 2)
            pop_tasks()
        while prepQ:
            prepQ.pop(0)()
        if oc % 4 == 3:
            for t in outproj_group_tasks(oc // 4):
                t()


def attn_out_proj_pair(b, wo, ot, ot_T, ps_pool=None):
    P = b.P
    for dc in range(8):
        for T in range(NT):
            ts = slice(T * TT, (T + 1) * TT)
            if ps_pool is None:
                op_ = b.PS[b.ps_rr % 8]
            else:
                op_ = ps_pool[b.ps_rr % len(ps_pool)]
            b.ps_rr += 1
            P.op("pe", (lambda e, op_=op_, dc=dc, ts=ts: e.matmul(op_[:, :], lhsT=wo[:, dc * 128:(dc + 1) * 128], rhs=ot[:, ts], start=True, stop=True)),
                 reads=[wo, ot_T[T]], writes=[op_])
            x = b.xT[dc][T]
            P.op("dve", (lambda e, op_=op_, dc=dc, ts=ts: e.tensor_tensor(out=b.xT_t[:, dc, ts], in0=op_[:, :], in1=b.xT_t[:, dc, ts], op=ALU.add)),
                 reads=[op_, x], writes=[x])


def attn_out_proj(b, oreq):
    P = b.P
    for dc in range(8):
        wo = b.w_get(oreq[dc])
        for T in range(NT):
            ts = slice(T * TT, (T + 1) * TT)
            op_ = b.PS[b.ps_rr % 8]
            b.ps_rr += 1
            for oc in range(8):
                P.op("pe", (lambda e, op_=op_, oc=oc, ts=ts, wo=wo: e.matmul(op_[:, :], lhsT=wo[:, oc * 128:(oc + 1) * 128], rhs=b.OT_t[:, oc, ts],
                                                                          start=(oc == 0), stop=(oc == 7))),
                     reads=[wo, b.OT[oc][T]], writes=[op_], accum=(oc != 0))
            x = b.xT[dc][T]
            P.op("dve", (lambda e, op_=op_, dc=dc, ts=ts: e.tensor_tensor(out=b.xT_t[:, dc, ts], in0=op_[:, :], in1=b.xT_t[:, dc, ts], op=ALU.add)),
                 reads=[op_, x], writes=[x])


B_SLOT_C, B_SLOT_KR, B_SLOT_O = 0, 3, 4
PC_BQN, PC_BKVN, PC_BQG, PC_BKG, PC_BKRG = 66, 68, 69, 70, 71
CST_F, CST_PH = 0, 1
TWO_PI = float(2.0 * np.pi)
CW1 = 6.28125
CW2 = float(2.0 * np.pi - 6.28125)
MLA_SCALE = float(96.0 ** -0.5)
MASK_ENG = os.environ.get("MASK_ENG", "pool")
ROPEK_ENG = os.environ.get("ROPEK_ENG", "pool")
CPH = int(os.environ.get("CPH", "0"))


def mixB_setup(b):
    nc = b.nc
    b.bw = nc.dram_tensor("bw", [12, 128, WSLOT], F32, kind="ExternalInput").ap()
    b.bwh = nc.dram_tensor("bwh", [16, 128, 384], F32, kind="ExternalInput").ap()
    b.cst_d = nc.dram_tensor("cst", [128, 4], F32, kind="ExternalInput").ap()
    if not hasattr(b, "pos_d"):
        b.pos_d = nc.dram_tensor("pos", [1, S], I32, kind="ExternalInput").ap()


def mixB_declare(b):
    r = {}
    r["c"] = [b.w_declare(b.bw[B_SLOT_C + i], WSLOT) for i in range(3)]
    r["kr"] = b.w_declare(b.bw[B_SLOT_KR], WSLOT)
    r["h"], r["o"] = [], []
    for h in range(16):
        r["h"].append(b.w_declare(b.bwh[h], 384))
        if h % 2 == 1:
            r["o"].append(b.w_declare(b.bw[B_SLOT_O + h // 2], WSLOT))
    return r


def mixB_consts(b):
    nc, P = b.nc, b.P
    if b.mix_mark is None:
        b.mix_mark = b.sb_off
    b.sb_off = b.mix_mark
    b.fence_new = True
    CS = b.sb("CS", [128, S], F32)
    krfin = b.sb("krfin", [128, S], F32)
    uvs = [b.sb("uv", [128, TT], BF16) for _ in range(2)]
    cst = b.sb("cst", [128, 4], F32)
    tri = b.sb("tri", [128, 128], BF16)
    mA = b.sb("mA", [128, 128], BF16)
    sel = b.sb("sel", [128, 128], BF16)
    posi = b.sb("posi", [128, TT], I32)
    b.fence_new = False
    uv_i = [0]
    cqn = b.act[0:2]
    cqn_T = b.act_T[0:2]
    ckvn = b.act[2]
    ckvn_T = b.act_T[2]
    sqK = [(b.act[3][:, i * 512:(i + 1) * 512], b.act_T[3][i]) for i in range(4)]
    pts = [(b.act[4][:, i * 512:(i + 1) * 512], b.act_T[4][i]) for i in range(4)]
    pt_i = [0]
    OTp = [b.act[5], b.otx]
    OTp_T = [b.act_T[5], b.otx_T]
    qh = [b.hT_t[:, i, :] for i in range(2)]
    qh_T = [b.hT[i] for i in range(2)]
    kh = [b.hT_t[:, 2 + i, :] for i in range(2)]
    kh_T = [b.hT[2 + i] for i in range(2)]
    vxs, vxs_T = [], []
    for i in range(2):
        v = b.hT_t[:, 4 + 2 * i:6 + 2 * i, :].rearrange("p a s -> p (a s)")[:, 0:NB * 192].rearrange("p (n c) -> p n c", c=192)
        vxs.append(v)
        vxs_T.append(b.hT[4 + 2 * i] + b.hT[5 + 2 * i])

    P.op("sp", lambda e: e.dma_start(out=cst[:, :], in_=b.cst_d[:, :]), writes=[cst], dma=True)
    P.op("pool", lambda e: e.memset(tri[:, :], 1.0), writes=[tri])
    P.op("pool", lambda e: e.affine_select(out=tri[:, :], in_=tri[:, :], pattern=[[1, 128]], compare_op=ALU.is_ge, fill=0.0,
                                           base=0, channel_multiplier=-1), reads=[tri], writes=[tri])
    P.op("pool", lambda e: e.memset(sel[:, :], 1.0), writes=[sel])
    for r0 in (0, 32):
        P.op("pool", (lambda e, r0=r0: e.affine_select(out=sel[r0:r0 + 32, :], in_=sel[r0:r0 + 32, :], pattern=[[-1, 128]], compare_op=ALU.is_equal,
                                                      fill=0.0, base=0, channel_multiplier=1)), reads=[sel], writes=[sel])
    P.op("pool", lambda e: e.memset(sel[64:128, :], 0.0), reads=[sel], writes=[sel])
    for u_ in uvs:
        P.op("dve", (lambda e, u_=u_: e.memset(u_[:, :], 0.0)), writes=[u_])
    P.op("dve", lambda e: e.memset(mA[:, :], 1.0), writes=[mA])
    P.op("dve", lambda e: e.memset(mA[32:64, :], 0.0), writes=[mA])
    for T in range(NT):
        ts = slice(T * TT, (T + 1) * TT)
        a = b.rstd[T % 2]
        kf = b.rstd[2 + T % 2]
        P.op("sp", (lambda e, ts=ts: e.dma_start(out=posi[:, :], in_=b.pos_d[0:1, ts].partition_broadcast(128))), writes=[posi], dma=True)
        P.op("dve", (lambda e, a=a: e.tensor_copy(out=a[:, :], in_=posi[:, :])), reads=[posi], writes=[a])
        P.op("dve", (lambda e, a=a: e.tensor_scalar(out=a[:, :], in0=a[:, :], scalar1=cst[:, CST_F:CST_F + 1], scalar2=cst[:, CST_PH:CST_PH + 1],
                                                    op0=ALU.mult, op1=ALU.add)), reads=[a, cst], writes=[a])
        ki = posi
        P.op("dve", (lambda e, a=a, kf=kf: e.tensor_scalar(out=kf[:, :], in0=a[:, :], scalar1=1.0 / TWO_PI, scalar2=None, op0=ALU.mult)),
             reads=[a], writes=[kf])
        P.op("dve", (lambda e, kf=kf: e.tensor_copy(out=ki[:, :], in_=kf[:, :])), reads=[kf], writes=[ki])
        P.op("dve", (lambda e, kf=kf: e.tensor_copy(out=kf[:, :], in_=ki[:, :])), reads=[ki], writes=[kf])
        P.op("dve", (lambda e, a=a, kf=kf: e.scalar_tensor_tensor(out=a[:, :], in0=kf[:, :], scalar=-CW1, in1=a[:, :], op0=ALU.mult, op1=ALU.add)),
             reads=[a, kf], writes=[a])
        P.op("dve", (lambda e, a=a, kf=kf: e.scalar_tensor_tensor(out=a[:, :], in0=kf[:, :], scalar=-CW2, in1=a[:, :], op0=ALU.mult, op1=ALU.add)),
             reads=[a, kf], writes=[a])
        P.op("dve", (lambda e, a=a, kf=kf: e.tensor_scalar(out=kf[:, :], in0=a[:, :], scalar1=float(np.pi), scalar2=-TWO_PI, op0=ALU.is_gt, op1=ALU.mult)),
             reads=[a], writes=[kf])
        P.op("dve", (lambda e, a=a, kf=kf: e.tensor_tensor(out=a[:, :], in0=a[:, :], in1=kf[:, :], op=ALU.add)), reads=[a, kf], writes=[a])
        P.op("dve", (lambda e, a=a, kf=kf: e.tensor_scalar(out=kf[:, :], in0=a[:, :], scalar1=-float(np.pi), scalar2=TWO_PI, op0=ALU.is_lt, op1=ALU.mult)),
             reads=[a], writes=[kf])
        P.op("dve", (lambda e, a=a, kf=kf: e.tensor_tensor(out=a[:, :], in0=a[:, :], in1=kf[:, :], op=ALU.add)), reads=[a, kf], writes=[a])
        P.op("dve", (lambda e, a=a: e.tensor_scalar(out=a[:, :], in0=a[:, :], scalar1=float(np.pi), scalar2=-float(np.pi), op0=ALU.min, op1=ALU.max)),
             reads=[a], writes=[a])
        P.op("act", (lambda e, a=a, ts=ts: e.activation(out=CS[0:64, ts], in_=a[0:64, :], func=AF.Sin)), reads=[a], writes=[CS])

    b.mla = dict(CS=CS, krfin=krfin, uvs=uvs, tri=tri, mA=mA, sel=sel, uv_i=uv_i, cqn=cqn, cqn_T=cqn_T, ckvn=ckvn, ckvn_T=ckvn_T,
                 sqK=sqK, pts=pts, pt_i=pt_i, OTp=OTp, OTp_T=OTp_T, qh=qh, qh_T=qh_T, kh=kh, kh_T=kh_T, vxs=vxs, vxs_T=vxs_T)


def mixB_run(b, r, l):
    nc, P = b.nc, b.P
    PSA = b.PS[0:2]
    PST = b.PS[2:4]
    PSH = b.PS[4:6]
    PSP = b.PS[6:8]
    st_i = [0]
    pp_i = [0]

    def psp():
        t = PSP[pp_i[0] % 2]
        pp_i[0] += 1
        return t

    if not hasattr(b, "mla"):
        mixB_consts(b)
    L = b.mla
    CS, krfin, uvs, tri, mA, sel, uv_i = L["CS"], L["krfin"], L["uvs"], L["tri"], L["mA"], L["sel"], L["uv_i"]
    cqn, cqn_T, ckvn, ckvn_T, sqK, pts, pt_i = L["cqn"], L["cqn_T"], L["ckvn"], L["ckvn_T"], L["sqK"], L["pts"], L["pt_i"]
    OTp, OTp_T, qh, qh_T, kh, kh_T, vxs, vxs_T = L["OTp"], L["OTp_T"], L["qh"], L["qh_T"], L["kh"], L["kh_T"], L["vxs"], L["vxs_T"]
    if DBG_STOP == 10:
        return
    b.rmsnorm(PC_MIX + 8 * l)

    def proj8(w, wcols, out_ps, T, mrows=128):
        ts = slice(T * TT, (T + 1) * TT)
        for c in range(8):
            P.op("pe", (lambda e, c=c, ts=ts: e.matmul(out_ps[0:mrows, :], lhsT=w[:, c * wcols:(c + 1) * wcols], rhs=b.hT_t[:, c, ts],
                                                       start=(c == 0), stop=(c == 7))),
                 reads=[w, b.hT[c][T]], writes=[out_ps], accum=(c != 0))

    def rstd_from(ss_ps, n, rs):
        P.op("act", (lambda e: e.activation(out=rs[:, :], in_=ss_ps[:, :], func=AF.Ln, scale=1.0 / n, bias=b.epsc[:, 0:1])),
             reads=[ss_ps, b.epsc], writes=[rs])
        P.op("act", (lambda e: e.activation(out=rs[:, :], in_=rs[:, :], func=AF.Exp, scale=-0.5)), reads=[rs], writes=[rs])

    def rope_sum(uv, out_ps):
        P.op("pe", (lambda e: e.matmul(out_ps[:, :], lhsT=sel[:, :], rhs=uv[:, :], start=True, stop=True)), reads=[sel, uv], writes=[out_ps])

    wc = [b.w_get(i) for i in r["c"]]
    wkr = b.w_get(r["kr"])
    for T in range(NT):
        ts = slice(T * TT, (T + 1) * TT)
        raws = [PSH[0], PSH[1]]
        ss = psp()
        for c2 in range(2):
            proj8(wc[c2], 128, raws[c2], T)
            sq = b.sq[b.sq_i % len(b.sq)]
            b.sq_i += 1
            P.op("act", (lambda e, sq=sq, raw=raws[c2]: e.activation(out=sq[:, :], in_=raw[:, :], func=AF.Square)), reads=[raws[c2]], writes=[sq])
            P.op("pe", (lambda e, sq=sq, c2=c2, ss=ss: e.matmul(ss[:, :], lhsT=b.ones[:, :], rhs=sq[:, :], start=(c2 == 0), stop=(c2 == 1))),
                 reads=[b.ones, sq], writes=[ss], accum=(c2 != 0))
        rs = b.sg[b.sg_i % len(b.sg)]
        b.sg_i += 1
        rstd_from(ss, 256.0, rs)
        for c2 in range(2):
            P.op("dve", (lambda e, c2=c2, rs=rs, ts=ts, raw=raws[c2]: e.scalar_tensor_tensor(
                out=cqn[c2][:, ts], in0=raw[:, :], scalar=b.par[:, PC_BQN + c2:PC_BQN + c2 + 1], in1=rs[:, :], op0=ALU.mult, op1=ALU.mult)),
                 reads=[raws[c2], rs, b.par_t], writes=[cqn_T[c2][T]])
        if CPH == 1:
            continue
        raw = PSH[0]
        ss = psp()
        proj8(wc[2], 128, raw, T)
        sq = b.sq[b.sq_i % len(b.sq)]
        b.sq_i += 1
        P.op("act", (lambda e, sq=sq, raw=raw: e.activation(out=sq[:, :], in_=raw[:, :], func=AF.Square)), reads=[raw], writes=[sq])
        P.op("pe", (lambda e, sq=sq, ss=ss: e.matmul(ss[:, :], lhsT=b.ones[:, :], rhs=sq[:, :], start=True, stop=True)), reads=[b.ones, sq], writes=[ss])
        rs = b.sg[b.sg_i % len(b.sg)]
        b.sg_i += 1
        rstd_from(ss, 128.0, rs)
        P.op("dve", (lambda e, rs=rs, ts=ts, raw=raw: e.scalar_tensor_tensor(
            out=ckvn[:, ts], in0=raw[:, :], scalar=b.par[:, PC_BKVN:PC_BKVN + 1], in1=rs[:, :], op0=ALU.mult, op1=ALU.mult)),
             reads=[raw, rs, b.par_t], writes=[ckvn_T[T]])
        if CPH == 2:
            continue
        raw = PSH[1]
        proj8(wkr, 128, raw, T)
        sqk, sqk_T = sqK[T]
        if CPH != 6:
            P.op("dve", (lambda e, sqk=sqk: e.memset(sqk[32:64, :], 0.0)), writes=[sqk_T])
            if CPH != 7:
                P.op("act", (lambda e, sqk=sqk, raw=raw: e.activation(out=sqk[0:32, :], in_=raw[0:32, :], func=AF.Square)), reads=[raw], writes=[sqk_T], accum=True)
        if CPH == 8:
            continue
        uv = uvs[uv_i[0] % 2]
        uv_i[0] += 1
        P.op("dve", (lambda e, uv=uv, raw=raw, ts=ts: e.scalar_tensor_tensor(
            out=uv[0:64, :], in0=raw[0:64, :], scalar=b.par[0:64, PC_BKRG:PC_BKRG + 1], in1=CS[0:64, ts], op0=ALU.mult, op1=ALU.mult)),
             reads=[raw, CS, b.par_t], writes=[uv])
        if CPH == 4:
            continue
        kps = psp()
        rope_sum(uv, kps)
        if CPH == 5:
            continue
        P.op("act", (lambda e, kps=kps, ts=ts: e.activation(out=krfin[0:32, ts], in_=kps[0:32, :], func=AF.Copy)), reads=[kps], writes=[krfin])

    if DBG_STOP == 11:
        return
    for i in range(2):
        P.op(MASK_ENG, (lambda e, i=i: e.memset(vxs[i], 1.0)), writes=vxs_T[i])
        P.op(MASK_ENG, (lambda e, i=i: e.memset(qh[i][32:64, :], 0.0)), writes=qh_T[i])
        P.op(MASK_ENG, (lambda e, i=i: e.memset(kh[i][32:64, :], 0.0)), writes=kh_T[i])

    prepQ, bgQ, normQ = [], [], []

    def prep_tasks(h):
        tasks = []
        q_t, q_T = qh[h % 2], qh_T[h % 2]
        k_t, k_T = kh[h % 2], kh_T[h % 2]
        vx, vx_T = vxs[h % 2], vxs_T[h % 2]
        for T in range(NT):
            ts = slice(T * TT, (T + 1) * TT)
            ctx = {}

            def tA(T=T, ts=ts, ctx=ctx):
                wh = b.w_get(r["h"][h])
                ctx["wh"] = wh
                qraw, kraw = PSH[0], PSH[1]
                for c2 in range(2):
                    P.op("pe", (lambda e, c2=c2: e.matmul(qraw[:, :], lhsT=wh[:, c2 * 128:(c2 + 1) * 128], rhs=cqn[c2][:, ts],
                                                          start=(c2 == 0), stop=(c2 == 1))),
                         reads=[wh, cqn_T[c2][T]], writes=[qraw], accum=(c2 != 0))
                sq = b.sq[b.sq_i % len(b.sq)]
                b.sq_i += 1
                ctx["sq"] = sq
                P.op("act", (lambda e: e.activation(out=sq[:, :], in_=qraw[:, :], func=AF.Square)), reads=[qraw], writes=[sq])
                P.op("pe", (lambda e: e.matmul(kraw[:, :], lhsT=wh[:, 192:320], rhs=ckvn[:, ts], start=True, stop=True)),
                     reads=[wh, ckvn_T[T]], writes=[kraw])
                sqk, sqk_T = sqK[T]
                P.op("act", (lambda e: e.activation(out=sqk[64:128, :], in_=kraw[64:128, :], func=AF.Square)), reads=[kraw], writes=[sqk_T])
            tasks.append(tA)
            if T == 0:
                for half in range(2):
                    def tV(half=half, ctx=ctx):
                        wh = ctx["wh"]
                        vp = psp()
                        for j in range(8):
                            n = half * 8 + j
                            P.op("pe", (lambda e, j=j, n=n: e.matmul(vp[:, j * 64:(j + 1) * 64], lhsT=ckvn[:, n * 128:(n + 1) * 128], rhs=wh[:, 320:384],
                                                                 start=True, stop=True)),
                                 reads=[wh, ckvn_T[n // 4]], writes=[vp], accum=(j != 0))
                        P.op("act", (lambda e: e.activation(out=vx[:, half * 8:(half + 1) * 8, 64:128],
                                                            in_=vp[:, :].rearrange("p (j d) -> p j d", j=8), func=AF.Copy)),
                             reads=[vp], writes=vx_T)
                    tasks.append(tV)

            def t1(ctx=ctx):
                ss = psp()
                sq = ctx["sq"]
                P.op("pe", (lambda e: e.matmul(ss[:, :], lhsT=mA[:, :], rhs=sq[:, :], start=True, stop=True)), reads=[mA, sq], writes=[ss])
                rs = b.sg[b.sg_i % len(b.sg)]
                b.sg_i += 1
                ctx["rs"] = rs
                rstd_from(ss, 96.0, rs)
            tasks.append(t1)

            def t2(T=T, ts=ts, ctx=ctx):
                rs = ctx["rs"]
                qraw = PSH[0]
                P.op("dve", (lambda e: e.scalar_tensor_tensor(out=q_t[64:128, ts], in0=qraw[64:128, :], scalar=b.par[64:128, PC_BQG:PC_BQG + 1],
                                                              in1=rs[64:128, :], op0=ALU.mult, op1=ALU.mult)), reads=[qraw, rs, b.par_t], writes=[q_T[T]])
                uv = uvs[uv_i[0] % 2]
                uv_i[0] += 1
                ctx["uv"] = uv
                P.op("dve", (lambda e: e.scalar_tensor_tensor(out=uv[0:64, :], in0=qraw[0:64, :], scalar=b.par[0:64, PC_BQG:PC_BQG + 1], in1=CS[0:64, ts],
                                                              op0=ALU.mult, op1=ALU.mult)), reads=[qraw, CS, b.par_t], writes=[uv])
            tasks.append(t2)

            def t3(T=T, ts=ts, ctx=ctx):
                rs = ctx["rs"]
                qps = psp()
                rope_sum(ctx["uv"], qps)
                P.op("dve", (lambda e: e.tensor_tensor(out=q_t[0:32, ts], in0=qps[0:32, :], in1=rs[0:32, :], op=ALU.mult)),
                     reads=[qps, rs], writes=[q_T[T]], accum=True)
            tasks.append(t3)

            def t4(T=T, ctx=ctx):
                sqk, sqk_T = sqK[T]
                ssk = psp()
                P.op("pe", (lambda e: e.matmul(ssk[:, :], lhsT=mA[:, :], rhs=sqk, start=True, stop=True)), reads=[mA, sqk_T], writes=[ssk])
                rsk = b.sg[b.sg_i % len(b.sg)]
                b.sg_i += 1
                ctx["rsk"] = rsk
                rstd_from(ssk, 96.0, rsk)
            tasks.append(t4)

            def t5(T=T, ts=ts, ctx=ctx):
                rsk = ctx["rsk"]
                kraw = PSH[1]
                P.op("dve", (lambda e: e.scalar_tensor_tensor(out=k_t[64:128, ts], in0=kraw[64:128, :], scalar=b.par[64:128, PC_BKG:PC_BKG + 1],
                                                              in1=rsk[64:128, :], op0=ALU.mult, op1=ALU.mult)), reads=[kraw, rsk, b.par_t], writes=[k_T[T]])
                P.op(ROPEK_ENG, (lambda e: e.tensor_tensor(out=k_t[0:32, ts], in0=krfin[0:32, ts], in1=rsk[0:32, :], op=ALU.mult)),
                     reads=[krfin, rsk], writes=[k_T[T]], accum=True)
            tasks.append(t5)
        return tasks

    def norm_task(h, Qc, acc):
        def t():
            ts = slice(Qc * 512, (Qc + 1) * 512)
            hh = h % 2
            orow = slice(hh * 64, (hh + 1) * 64)
            drow = slice((1 - hh) * 64, (2 - hh) * 64)
            dn = b.rstd[2 + (h * 4 + Qc) % 2]
            P.op("dve", (lambda e: e.reciprocal(out=dn[drow, :], in_=acc[drow, :])), reads=[acc], writes=[dn])
            ot, ot_T = OTp[(h // 2) % 2], OTp_T[(h // 2) % 2]
            P.op("dve", (lambda e: e.tensor_tensor(out=ot[orow, ts], in0=acc[orow, :], in1=dn[drow, :], op=ALU.mult)),
                 reads=[acc, dn], writes=[ot_T[Qc]], accum=(hh == 1))
        return t

    def outproj_tasks(p):
        ot, ot_T = OTp[p % 2], OTp_T[p % 2]
        tasks = []
        ctx = {}
        for dc in range(8):
            for T in range(NT):
                def t(dc=dc, T=T):
                    if "wo" not in ctx:
                        ctx["wo"] = b.w_get(r["o"][p])
                    wo = ctx["wo"]
                    ts = slice(T * TT, (T + 1) * TT)
                    op_ = psp()
                    P.op("pe", (lambda e: e.matmul(op_[:, :], lhsT=wo[:, dc * 128:(dc + 1) * 128], rhs=ot[:, ts], start=True, stop=True)),
                         reads=[wo, ot_T[T]], writes=[op_])
                    x = b.xT[dc][T]
                    P.op("dve", (lambda e: e.tensor_tensor(out=b.xT_t[:, dc, ts], in0=op_[:, :], in1=b.xT_t[:, dc, ts], op=ALU.add)),
                         reads=[op_, x], writes=[x])
                tasks.append(t)
        return tasks

    def pop_tasks():
        if normQ:
            normQ.pop(0)()
        if prepQ:
            prepQ.pop(0)()
        if bgQ:
            bgQ.pop(0)()

    def attend(h, Qc):
        q_t, q_T = qh[h % 2], qh_T[h % 2]
        k_t, k_T = kh[h % 2], kh_T[h % 2]
        vx, vx_T = vxs[h % 2], vxs_T[h % 2]
        vcols = slice(64, 192) if h % 2 == 0 else slice(0, 128)
        acc = PSA[(h * 4 + Qc) % 2]
        jmax = 4 * Qc + 3
        while len(normQ) > 1:
            normQ.pop(0)()

        def scores(j):
            c0 = max(0, j - 4 * Qc) * 128
            st = PST[st_i[0] % 2]
            st_i[0] += 1
            P.op("pe", (lambda e, st=st, j=j, c0=c0: e.matmul(st[:, c0:512], lhsT=k_t[:, j * 128:(j + 1) * 128], rhs=q_t[:, Qc * 512 + c0:(Qc + 1) * 512],
                                                         start=True, stop=True)), reads=[k_T[j // 4], q_T[Qc]], writes=[st])
            pt, ptT = pts[pt_i[0] % 4]
            pt_i[0] += 1
            P.op("act", (lambda e, st=st, pt=pt, c0=c0: e.activation(out=pt[:, c0:512], in_=st[:, c0:512], func=AF.Exp, scale=MLA_SCALE)),
                 reads=[st], writes=[ptT])
            if j >= 4 * Qc:
                P.op(MASK_ENG, (lambda e, pt=pt, c0=c0: e.tensor_tensor(out=pt[:, c0:c0 + 128], in0=pt[:, c0:c0 + 128], in1=tri[:, :], op=ALU.mult)),
                     reads=[ptT, tri], writes=[ptT])
            return pt, ptT, c0

        nxt = scores(0)
        for j in range(jmax + 1):
            pt, ptT, c0 = nxt
            if j < jmax:
                nxt = scores(j + 1)
            P.op("pe", (lambda e, acc=acc, j=j, c0=c0, pt=pt: e.matmul(acc[:, c0:512], lhsT=vx[:, j, vcols], rhs=pt[:, c0:512], start=(j == 0), stop=(j == jmax))),
                 reads=vx_T + [ptT], writes=[acc], accum=(j != 0))
            pop_tasks()
        normQ.append(norm_task(h, Qc, acc))

    for t in prep_tasks(0):
        t()
    if DBG_STOP == 12:
        return
    if DBG_DUMP:
        b.dump("CS", lambda: CS[:, :], S, [CS])
        b.dump("krfin", lambda: krfin[:, :], S, [krfin])
        b.dump("cqn0", lambda: cqn[0][:, :], S, cqn_T[0])
        b.dump("ckvn", lambda: ckvn[:, :], S, ckvn_T)
        b.dump("qh0", lambda: qh[0], S, qh_T[0])
        b.dump("kh0", lambda: kh[0], S, kh_T[0])
        b.dump("vx0", lambda: vxs[0].rearrange("p n c -> p (n c)"), NB * 192, vxs_T[0])
    for h in range(16):
        if DBG_STOP == 7 and h == 2:
            while normQ:
                normQ.pop(0)()
            while bgQ:
                bgQ.pop(0)()
            return
        if h < 15:
            prepQ.extend(prep_tasks(h + 1))
        for Qc in range(4):
            attend(h, Qc)
        while prepQ:
            prepQ.pop(0)()
        while normQ:
            normQ.pop(0)()
        if h % 2 == 1:
            bgQ.extend(outproj_tasks(h // 2))
        if h % 2 == 0 or h == 15:
            while bgQ:
                bgQ.pop(0)()
```

```python
import numpy as np
import concourse.bass as bass
import concourse.mybir as mybir
from concourse.bass_utils import run_bass_kernel_spmd

import os
DBG_STOP = int(os.environ.get("DBG_STOP", "0"))
DBG_DUMP = int(os.environ.get("DBG_DUMP", "0"))
F32 = mybir.dt.float32
BF16 = mybir.dt.bfloat16
I32 = mybir.dt.int32
ALU = mybir.AluOpType
AF = mybir.ActivationFunctionType
AX = mybir.AxisListType


class Tile:
    __slots__ = ("name", "t", "writer", "readers", "dma_sem", "psum")

    def __init__(self, name, t):
        self.psum = False
        self.name = name
        self.t = t
        self.writer = None
        self.readers = []
        self.dma_sem = None

    def __getitem__(self, k):
        return self.t[k]


class Ins:
    __slots__ = ("eng", "fn", "dma", "deps", "signals", "sem", "val", "idx", "dsem_tile")

    def __init__(self, eng, fn, dma):
        self.eng = eng
        self.fn = fn
        self.dma = dma
        self.deps = []
        self.signals = False
        self.sem = None
        self.val = None
        self.idx = None
        self.dsem_tile = None


ENGS = ("pe", "act", "dve", "pool", "sp")


class Prog:
    def __init__(self, nc):
        self.nc = nc
        self.streams = {e: [] for e in ENGS}
        self.n = 0
        self.store_tile = Tile("__store__", None)
        self.deferring = False
        self.deferred = []

    def tile(self, name, t):
        return Tile(name, t)

    def pop_deferred(self, n=1):
        for _ in range(n):
            if not self.deferred:
                return
            a = self.deferred.pop(0)
            self.op(*a)

    def op(self, eng, fn, reads=(), writes=(), dma=False, accum=False):
        if self.deferring:
            self.deferred.append((eng, fn, list(reads), list(writes), dma, accum))
            return None
        ins = Ins(eng, fn, dma)
        ins.idx = self.n
        self.n += 1
        deps = {}

        def add(d, kind):
            if d is None:
                return
            if (not d.dma) and (not dma) and d.eng == eng:
                if eng == "pe" or kind == "war":
                    return
            deps[d.idx] = d

        for t in reads:
            add(t.writer, "raw")
            if t.psum:
                for r in t.readers:
                    if r.eng != eng:
                        add(r, "rr")
        if not accum:
            for t in writes:
                add(t.writer, "waw")
                for r in t.readers:
                    add(r, "war")
        ins.deps = list(deps.values())
        for d in ins.deps:
            d.signals = True
        for t in reads:
            if not dma:
                t.readers = [r for r in t.readers if r.dma or r.eng != eng]
            t.readers.append(ins)
        if not accum:
            for t in writes:
                t.writer = ins
                t.readers = []
        else:
            for t in writes:
                t.writer = ins
        if dma:
            ins.dsem_tile = writes[0] if writes else self.store_tile
        self.streams[eng].append(ins)
        return ins

    def barrier(self, engs=("pe", "act", "dve", "pool")):
        toks = {}
        for e in engs:
            t = Tile("bar_" + e, None)
            last = None
            for i in reversed(self.streams[e]):
                if not i.dma:
                    last = i
                    break
            t.writer = last
            toks[e] = t
        for e in engs:
            self.op(e, (lambda en: en.nop()), reads=[toks[x] for x in engs if x != e and toks[x].writer is not None])

    def emit(self, final_wait_eng="sp"):
        nc = self.nc
        engobj = {"pe": nc.tensor, "act": nc.scalar, "dve": nc.vector, "pool": nc.gpsimd, "sp": nc.sync}
        SEM_EPOCH = 1024
        stack = []
        nsem = [0]

        def new_sem(tag):
            cm = nc.semaphore("%s_%d" % (tag, nsem[0]))
            nsem[0] += 1
            h = cm.__enter__()
            stack.append(cm)
            return h

        esem = {e: [] for e in ENGS}
        dsem = {}
        all_ins = sorted((i for e in ENGS for i in self.streams[e]), key=lambda i: i.idx)
        cnt = {e: 0 for e in ENGS}
        dcnt = {}
        for ins in all_ins:
            if ins.dma:
                t = ins.dsem_tile
                k = id(t)
                if k not in dsem:
                    dsem[k] = []
                    dcnt[k] = 0
                n = dcnt[k]
                dcnt[k] += 1
                ep = n // (SEM_EPOCH // 16)
                if ep >= len(dsem[k]):
                    dsem[k].append(new_sem("d"))
                ins.sem = dsem[k][ep]
                ins.val = (n % (SEM_EPOCH // 16) + 1) * 16
            elif ins.signals:
                e = ins.eng
                r = cnt[e]
                cnt[e] += 1
                ep = r // SEM_EPOCH
                if ep >= len(esem[e]):
                    esem[e].append(new_sem("s_" + e))
                ins.sem = esem[e][ep]
                ins.val = r % SEM_EPOCH + 1
        self.n_sems = nsem[0]
        self.sig_counts = dict(cnt)
        self.ins_counts = {e: len(self.streams[e]) for e in ENGS}
        if os.environ.get("PROG_STATS"):
            print("PROG_STATS sems", self.n_sems, "signals", cnt, "instrs", self.ins_counts)
        self.n_waits = 0
        store_waits = []
        ks = id(self.store_tile)
        if ks in dsem:
            n = dcnt[ks]
            per = SEM_EPOCH // 16
            for ep, sm in enumerate(dsem[ks]):
                last = min(n - ep * per, per)
                store_waits.append((sm, last * 16))

        def run_stream(e, eng):
            known = {}
            for ins in self.streams[e]:
                for d in ins.deps:
                    key = id(d.sem)
                    if known.get(key, 0) >= d.val:
                        continue
                    eng.wait_ge(d.sem, d.val)
                    known[key] = d.val
                    self.n_waits += 1
                bi = ins.fn(eng)
                if ins.dma:
                    bi.then_inc(ins.sem, 16)
                elif ins.signals:
                    bi.then_inc(ins.sem, 1)
            if e == final_wait_eng:
                for sm, v in store_waits:
                    eng.wait_ge(sm, v)

        with nc.Block() as block:
            @block.tensor
            def _(eng):
                run_stream("pe", eng)

            @block.scalar
            def _(eng):
                run_stream("act", eng)

            @block.vector
            def _(eng):
                run_stream("dve", eng)

            @block.gpsimd
            def _(eng):
                run_stream("pool", eng)

            @block.sync
            def _(eng):
                run_stream("sp", eng)
        for cm in reversed(stack):
            cm.__exit__(None, None, None)


D = 1024
S = 2048
NB = 16
DFF = 2816
NFC = 22
DEPTH = 2
EPS = 1e-6
TT = 512
NT = S // TT
FFN_GROUPS = ((0, 6), (6, 12), (12, 17), (17, 22))
WSLOT = 1024


def chunkT(W):
    K, F = W.shape
    return np.ascontiguousarray(
        W.reshape(K // 128, 128, F // 128, 128).transpose(2, 1, 0, 3)).reshape(F // 128, 128, K)


class Builder:
    def __init__(self, stages):
        self.stages = stages
        nc = bass.Bass("TRN2", target_bir_lowering=False)
        self.nc = nc
        self.P = Prog(nc)
        self.sb_off = self.SB_BASE
        self.ntile = 0
        self.wreq = []
        self.wtiles = {}
        self.wnext = 0
        self.cast_rr = 0
        self.fence_new = False
        self.norm_done = False

    SB_BASE = 16512
    SB_TOP = 229344

    def sb_raw(self, name, shape, dt):
        self.ntile += 1
        esz = {F32: 4, BF16: 2, I32: 4}[dt]
        nbytes = esz * int(np.prod(shape[1:]))
        nbytes = (nbytes + 31) // 32 * 32
        off = self.sb_off
        assert off + nbytes <= self.SB_TOP, "SBUF arena overflow at %s: need %d, have %d" % (name, nbytes, self.SB_TOP - off)
        self.sb_off += nbytes
        return self.nc.alloc_sbuf_tensor_at("%s_%d" % (name, self.ntile), list(shape), dt, offset=off)

    def sb(self, name, shape, dt):
        t = self.P.tile(name, self.sb_raw(name, shape, dt))
        if self.fence_new:
            t.readers = [st[-1] for st in (self.P.streams[e] for e in ("pe", "act", "dve", "pool")) if st and not st[-1].dma]
        return t

    def psum_banks(self):
        self.PS = [self.P.tile("ps%d" % i, self.nc.alloc_psum_tensor("ps%d" % i, [128, 512], F32))
                   for i in range(8)]
        for t in self.PS:
            t.psum = True

    def w_setup(self, nstage, nslots, depth):
        self.stg = [self.sb("stg", [128, WSLOT], F32) for _ in range(nstage)]
        self.wsl = [self.sb("wsl", [128, WSLOT], BF16) for _ in range(nslots)]
        self.stg_i = 0
        self.wsl_i = 0
        self.slot_owner = {}
        self.live = []
        self.wdepth = depth

    def w_declare(self, dram_ap, width):
        self.wreq.append((dram_ap, width))
        return len(self.wreq) - 1

    def w_issue_upto(self, idx):
        P = self.P
        idx = min(idx, len(self.wreq) - 1)
        while self.wnext <= idx:
            dram_ap, width = self.wreq[self.wnext]
            st = self.stg[self.stg_i % len(self.stg)]
            self.stg_i += 1
            sl = self.wsl[self.wsl_i % len(self.wsl)]
            self.wsl_i += 1
            P.op("sp", (lambda e, st=st, a=dram_ap, w=width: e.dma_start(out=st[:, 0:w], in_=a)),
                 writes=[st], dma=True)
            if self.cast_rr % 2 == 0:
                P.op("dve", (lambda e, st=st, sl=sl, w=width: e.tensor_copy(out=sl[:, 0:w], in_=st[:, 0:w])),
                     reads=[st], writes=[sl])
            else:
                P.op("act", (lambda e, st=st, sl=sl, w=width: e.activation(out=sl[:, 0:w], in_=st[:, 0:w], func=AF.Copy)),
                     reads=[st], writes=[sl])
            self.cast_rr += 1
            self.wtiles[self.wnext] = sl
            self.slot_owner[id(sl)] = self.wnext
            self.wnext += 1

    def w_get(self, idx):
        self.w_issue_upto(idx + self.wdepth)
        sl = self.wtiles[idx]
        assert self.slot_owner[id(sl)] == idx, "weight ring overrun"
        return sl

    def norm_sq(self, T):
        P = self.P
        ts = slice(T * TT, (T + 1) * TT)
        for c in range(8):
            P.op("act", (lambda e, c=c, ts=ts: e.activation(out=self.sqn_v[c], in_=self.xT_t[:, c, ts], func=AF.Square)),
                 reads=[self.xT[c][T]], writes=[self.sqn_T[c]])

    def norm_finish(self, T, gcol0):
        P = self.P
        ts = slice(T * TT, (T + 1) * TT)
        ss = self.PS[self.ps_rr % 8]
        self.ps_rr += 1
        for c in range(8):
            P.op("pe", (lambda e, ss=ss, c=c: e.matmul(ss[:, :], lhsT=self.ones[:, :], rhs=self.sqn_v[c], start=(c == 0), stop=(c == 7))),
                 reads=[self.ones, self.sqn_T[c]], writes=[ss], accum=(c != 0))
        rs = self.rstd[T]
        P.op("act", (lambda e, rs=rs, ss=ss: e.activation(out=rs[:, :], in_=ss[:, :], func=AF.Ln, scale=1.0 / D, bias=self.epsc[:, 0:1])),
             reads=[ss, self.epsc], writes=[rs])
        P.op("act", (lambda e, rs=rs: e.activation(out=rs[:, :], in_=rs[:, :], func=AF.Exp, scale=-0.5)), reads=[rs], writes=[rs])
        for c in range(8):
            P.op("dve", (lambda e, c=c, ts=ts, rs=rs: e.scalar_tensor_tensor(
                out=self.hT_t[:, c, ts], in0=self.xT_t[:, c, ts], scalar=self.par[:, gcol0 + c:gcol0 + c + 1],
                in1=rs[:, :], op0=ALU.mult, op1=ALU.mult)),
                 reads=[self.xT[c][T], rs, self.par_t], writes=[self.hT[c][T]])

    def rmsnorm(self, gcol0):
        if self.norm_done:
            self.norm_done = False
            return
        P = self.P
        for T in range(NT):
            ts = slice(T * TT, (T + 1) * TT)
            ss = self.PS[self.ps_rr % 8]
            self.ps_rr += 1
            for c in range(8):
                sq = self.sq[self.sq_i % len(self.sq)]
                self.sq_i += 1
                x = self.xT[c][T]
                P.op("act", (lambda e, sq=sq, x=x, c=c, ts=ts: e.activation(out=sq[:, :], in_=self.xT_t[:, c, ts], func=AF.Square)),
                     reads=[x], writes=[sq])
                P.op("pe", (lambda e, ss=ss, sq=sq, c=c: e.matmul(ss[:, :], lhsT=self.ones[:, :], rhs=sq[:, :], start=(c == 0), stop=(c == 7))),
                     reads=[self.ones, sq], writes=[ss], accum=(c != 0))
            rs = self.rstd[T]
            P.op("act", (lambda e, rs=rs, ss=ss: e.activation(out=rs[:, :], in_=ss[:, :], func=AF.Ln, scale=1.0 / D, bias=self.epsc[:, 0:1])),
                 reads=[ss, self.epsc], writes=[rs])
            P.op("act", (lambda e, rs=rs: e.activation(out=rs[:, :], in_=rs[:, :], func=AF.Exp, scale=-0.5)),
                 reads=[rs], writes=[rs])
            for c in range(8):
                h = self.hT[c][T]
                x = self.xT[c][T]
                P.op("dve", (lambda e, c=c, ts=ts, rs=rs: e.scalar_tensor_tensor(
                    out=self.hT_t[:, c, ts], in0=self.xT_t[:, c, ts], scalar=self.par[:, gcol0 + c:gcol0 + c + 1],
                    in1=rs[:, :], op0=ALU.mult, op1=ALU.mult)),
                     reads=[x, rs, self.par_t], writes=[h])

    def ffn_declare(self, li):
        req = []
        for (f0, f1) in FFN_GROUPS:
            g = {"gu": [], "d": []}
            for fc in range(f0, f1):
                g["gu"].append((self.w_declare(self.ffnw[li, fc, 0], WSLOT), self.w_declare(self.ffnw[li, fc, 1], WSLOT)))
            for fc in range(f0, f1):
                g["d"].append(self.w_declare(self.ffnw[li, fc, 2], WSLOT))
            req.append(g)
        return req

    def ffn_run(self, req, next_gcol=None):
        P = self.P
        for gi, (f0, f1) in enumerate(FFN_GROUPS):
            n = f1 - f0
            for k in range(n):
                ig, iu = req[gi]["gu"][k]
                wg = self.w_get(ig)
                wu = self.w_get(iu)
                for T in range(NT):
                    ts = slice(T * TT, (T + 1) * TT)
                    gp = self.PS[self.ps_rr % 8]
                    up = self.PS[(self.ps_rr + 1) % 8]
                    self.ps_rr += 2
                    for (pt, w) in ((gp, wg), (up, wu)):
                        for c in range(8):
                            P.op("pe", (lambda e, pt=pt, w=w, c=c, ts=ts: e.matmul(
                                pt[:, :], lhsT=w[:, c * 128:(c + 1) * 128], rhs=self.hT_t[:, c, ts],
                                start=(c == 0), stop=(c == 7))),
                                 reads=[w, self.hT[c][T]], writes=[pt], accum=(c != 0))
                    sg = self.sg[self.sg_i % len(self.sg)]
                    self.sg_i += 1
                    P.op("act", (lambda e, sg=sg, gp=gp: e.activation(out=sg[:, :], in_=gp[:, :], func=AF.Silu)),
                         reads=[gp], writes=[sg])
                    a = self.act[k]
                    P.op("dve", (lambda e, a=a, sg=sg, up=up, ts=ts: e.tensor_tensor(
                        out=a[:, ts], in0=up[:, :], in1=sg[:, :], op=ALU.mult)),
                         reads=[sg, up], writes=[self.act_T[k][T]])
                    P.pop_deferred(1)
            wds = [self.w_get(i) for i in req[gi]["d"]]
            fuse = (next_gcol is not None) and gi == len(FFN_GROUPS) - 1
            order = [(dc, T) for T in range(NT) for dc in range(8)] if fuse else [(dc, T) for dc in range(8) for T in range(NT)]
            for (dc, T) in order:
                if True:
                    ts = slice(T * TT, (T + 1) * TT)
                    op_ = self.PS[self.ps_rr % 8]
                    self.ps_rr += 1
                    for k in range(n):
                        P.op("pe", (lambda e, op_=op_, w=wds[k], k=k, dc=dc, ts=ts, n=n: e.matmul(
                            op_[:, :], lhsT=w[:, dc * 128:(dc + 1) * 128], rhs=self.act[k][:, ts],
                            start=(k == 0), stop=(k == n - 1))),
                             reads=[wds[k], self.act_T[k][T]], writes=[op_], accum=(k != 0))
                    x = self.xT[dc][T]
                    P.op("dve", (lambda e, op_=op_, dc=dc, ts=ts: e.scalar_tensor_tensor(
                        out=self.xT_t[:, dc, ts], in0=op_[:, :], scalar=0.5, in1=self.xT_t[:, dc, ts],
                        op0=ALU.mult, op1=ALU.add)),
                         reads=[op_, x], writes=[x])
                    if fuse and dc == 7:
                        if T > 0:
                            self.norm_finish(T - 1, next_gcol)
                        self.norm_sq(T)
            if fuse:
                self.norm_finish(NT - 1, next_gcol)
                self.norm_done = True

    def dump(self, name, get_ap, cols, reads, rows=128):
        nc, P = self.nc, self.P
        d = nc.dram_tensor("dbg_" + name, [rows, cols], F32, kind="ExternalOutput").ap()
        if not hasattr(self, "dbg_t"):
            save = self.sb_off
            assert self.sb_off <= self.SB_TOP - 12288 - 64, "no room for debug tile"
            self.sb_off = self.SB_TOP - 12288 - 64
            self.dbg_t = self.sb("dbg_t", [128, 3072], F32)
            self.sb_off = save
        t = self.dbg_t
        P.op("dve", lambda e: e.tensor_copy(out=t[0:rows, 0:cols], in_=get_ap()), reads=list(reads), writes=[t])
        P.op("sp", lambda e: e.dma_start(out=d[:, :], in_=t[0:rows, 0:cols]), reads=[t], dma=True)
        self.dbg_names.append("dbg_" + name)

    def setup(self):
        self.dbg_names = []
        nc, P = self.nc, self.P
        self.xT_d = nc.dram_tensor("xT", [D, S], F32, kind="ExternalInput").ap()
        self.par_d = nc.dram_tensor("par", [128, NPAR], F32, kind="ExternalInput").ap()
        if set(k for k, _ in self.stages) & {"ffn1", "ffn2"}:
            self.ffnw = nc.dram_tensor("ffnw", [DEPTH * 2, NFC, 3, 128, WSLOT], F32, kind="ExternalInput").ap()
        self.out_d = nc.dram_tensor("outT", [D, S], F32, kind="ExternalOutput").ap()
        self.psum_banks()
        self.ps_rr = 0
        self.xT_t = self.sb_raw("xTs", [128, 8, S], F32)
        self.xT = [[P.tile("x%d_%d" % (c, T), None) for T in range(NT)] for c in range(8)]
        self.hT_t = self.sb_raw("hTs", [128, 8, S], BF16)
        self.hT = [[P.tile("h%d_%d" % (c, T), None) for T in range(NT)] for c in range(8)]
        self.par_t = self.sb("par", [128, NPAR], F32)
        self.par = self.par_t
        self.ones = self.sb("ones", [128, 128], BF16)
        self.epsc = self.sb("epsc", [128, 1], F32)
        self.rstd = [self.sb("rstd", [128, TT], F32) for _ in range(NT)]
        self.sq = [self.sb("sq", [128, TT], BF16) for _ in range(2)]
        self.sq_i = 0
        self.sg = [self.sb("sg", [128, TT], F32) for _ in range(2)]
        self.sg_i = 0
        nact = 6
        self.act = [self.sb("act", [128, S], BF16) for _ in range(nact)]
        self.act_T = [[P.tile("a%d_%d" % (k, T), None) for T in range(NT)] for k in range(nact)]
        self.otx = self.sb("otx", [128, S], BF16)
        self.otx_T = [P.tile("otx", None) for _ in range(NT)]
        self.otx2 = [self.sb("otx2", [128, S], BF16) for _ in range(2)]
        self.sqn_v = [self.act[5][:, c * TT:(c + 1) * TT] for c in range(4)] + [self.otx[:, c * TT:(c + 1) * TT] for c in range(4)]
        self.sqn_T = [self.act_T[5][c] for c in range(4)] + [self.otx_T[c] for c in range(4)]
        self.otx2_T = [[P.tile("otx2", None) for _ in range(NT)] for _ in range(2)]
        self.w_setup(nstage=2, nslots=10, depth=4)
        self.mix_mark = None
        P.op("dve", lambda e: e.memset(self.ones[:, :], 1.0), writes=[self.ones])
        P.op("dve", lambda e: e.memset(self.epsc[:, :], EPS), writes=[self.epsc])
        P.op("sp", lambda e: e.dma_start(out=self.par_t[:, :], in_=self.par_d[:, :]), writes=[self.par_t], dma=True)
        for c in range(8):
            for T in range(NT):
                ts = slice(T * TT, (T + 1) * TT)
                P.op("sp", (lambda e, c=c, ts=ts: e.dma_start(out=self.xT_t[:, c, ts], in_=self.xT_d[c * 128:(c + 1) * 128, ts])),
                     writes=[self.xT[c][T]], dma=True)

    def finish(self):
        P = self.P
        for c in range(8):
            P.op("sp", (lambda e, c=c: e.dma_start(out=self.out_d[c * 128:(c + 1) * 128, :], in_=self.xT_t[:, c, :])),
                 reads=[self.xT[c][T] for T in range(NT)], dma=True)
        P.emit()


PC_FN1 = 0
PC_MIX = 16
PC_FN2 = 32
NPAR = 128


def build_program(stages):
    b = Builder(stages)
    b.setup()
    kinds = set(k for k, _ in stages)
    if "mixA" in kinds:
        mixA_setup(b)
        mixA_tables(b)
    if "mixB" in kinds:
        mixB_setup(b)
    reqs = {}
    for st in stages:
        kind, l = st
        if kind == "ffn1":
            reqs[st] = b.ffn_declare(l * 2 + 0)
        elif kind == "ffn2":
            reqs[st] = b.ffn_declare(l * 2 + 1)
        elif kind == "mixA":
            reqs[st] = mixA_declare(b)
        elif kind == "mixB":
            reqs[st] = mixB_declare(b)
    pending_consts = False

    def norm_col(st2):
        k2, l2 = st2
        return {"ffn1": PC_FN1, "ffn2": PC_FN2, "mixA": PC_MIX, "mixB": PC_MIX}[k2] + 8 * l2

    for si, st in enumerate(stages):
        kind, l = st
        nxt_col = norm_col(stages[si + 1]) if (si + 1 < len(stages) and FUSE_NORM) else None
        if os.environ.get("STAGE_BARRIER") == "1":
            b.P.barrier()
        if kind in ("ffn1", "ffn2") and pending_consts:
            b.rmsnorm((PC_FN1 if kind == "ffn1" else PC_FN2) + 8 * l)
            b.P.deferring = True
            mixB_consts(b)
            b.P.deferring = False
            pending_consts = False
            b.ffn_run(reqs[st], nxt_col)
            b.P.pop_deferred(10 ** 6)
            continue
        if kind == "ffn1":
            b.rmsnorm(PC_FN1 + 8 * l)
            b.ffn_run(reqs[st], nxt_col)
        elif kind == "ffn2":
            b.rmsnorm(PC_FN2 + 8 * l)
            b.ffn_run(reqs[st], nxt_col)
        elif kind == "mixA":
            mixA_run(b, reqs[st], l)
            pending_consts = "mixB" in kinds
        elif kind == "mixB":
            mixB_run(b, reqs[st], l)
    b.finish()
    return b


def host_prep(inputs):
    f = lambda k: np.asarray(inputs[k], dtype=np.float32)
    par = np.zeros((128, NPAR), np.float32)
    for l in range(DEPTH):
        par[:, PC_FN1 + 8 * l:PC_FN1 + 8 * l + 8] = f("ffn_norm1")[l].reshape(8, 128).T
        par[:, PC_MIX + 8 * l:PC_MIX + 8 * l + 8] = f("mix_norm")[l].reshape(8, 128).T
        par[:, PC_FN2 + 8 * l:PC_FN2 + 8 * l + 8] = f("ffn_norm2")[l].reshape(8, 128).T
    ffnw = np.empty((DEPTH * 2, NFC, 3, 128, WSLOT), np.float32)
    for l in range(DEPTH):
        for i, pre in enumerate(("ffn1", "ffn2")):
            ffnw[l * 2 + i, :, 0] = chunkT(f(pre + "_wg")[l])
            ffnw[l * 2 + i, :, 1] = chunkT(f(pre + "_wu")[l])
            ffnw[l * 2 + i, :, 2] = f(pre + "_wd")[l].reshape(NFC, 128, D)
    shared = {"par": par, "ffnw": ffnw}
    gq = f("a_q_gain")[0]; gk = f("a_k_gain")[0]
    par[:, PC_AQG] = np.tile(gq, 2)
    par[:, PC_AKG] = np.tile(gk, 2)
    par[:, PC_ASINK:PC_ASINK + 16] = f("a_sinks")[0][None, :]
    win = f("a_w_in")[0]
    aw = np.empty((19, 128, WSLOT), np.float32)
    aw[A_SLOT_Q:A_SLOT_Q + 8] = chunkT(win[:, 0:1024])
    for c in range(2):
        kc = win[:, 1024 + c * 64:1024 + (c + 1) * 64]
        aw[A_SLOT_K + c] = chunkT(np.concatenate([kc, kc], axis=1))[0]
    aw[A_SLOT_V] = chunkT(win[:, 1152:1280])[0]
    aw[A_SLOT_O:A_SLOT_O + 8] = f("a_w_out")[0].reshape(8, 128, D)
    shared["aw"] = aw
    shared["rbT"] = np.ascontiguousarray(f("rel_bias").T)
    bwin = f("b_w_in")[0]
    bw = np.zeros((12, 128, WSLOT), np.float32)
    bw[B_SLOT_C:B_SLOT_C + 3] = chunkT(bwin[:, 0:384])
    krc = bwin[:, 384:416]
    kr64 = np.concatenate([krc, krc[:, 16:32], krc[:, 0:16]], axis=1)
    kr128 = np.concatenate([kr64, kr64], axis=1)
    bw[B_SLOT_KR] = kr128.reshape(8, 128, 128).transpose(1, 0, 2).reshape(128, 1024)
    bw[B_SLOT_O:B_SLOT_O + 8] = f("b_w_out")[0].reshape(8, 128, D)
    shared["bw"] = bw
    wuq = f("b_w_uq")[0].reshape(256, 16, 96)
    wukv = f("b_w_ukv")[0].reshape(128, 16, 128)
    bwh = np.empty((16, 128, 384), np.float32)
    for h in range(16):
        q128 = np.concatenate([wuq[:, h, 64:96], wuq[:, h, 80:96], wuq[:, h, 64:80], wuq[:, h, 0:64]], axis=1)
        bwh[h, :, 0:128] = q128[0:128]
        bwh[h, :, 128:256] = q128[128:256]
        bwh[h, :, 256:384] = wukv[:, h, :]
    shared["bwh"] = bwh
    par[:, PC_BQN:PC_BQN + 2] = f("b_q_norm")[0].reshape(2, 128).T
    par[:, PC_BKVN] = f("b_kv_norm")[0]
    gq = f("b_q_gain")[0]; gk = f("b_k_gain")[0]
    par[0:32, PC_BQG] = gq[64:96]
    par[32:48, PC_BQG] = gq[80:96]
    par[48:64, PC_BQG] = gq[64:80]
    par[64:128, PC_BQG] = gq[0:64]
    par[64:128, PC_BKG] = gk[0:64]
    par[0:32, PC_BKRG] = gk[64:96]
    par[32:48, PC_BKRG] = gk[80:96]
    par[48:64, PC_BKRG] = gk[64:80]
    inv_freq = (np.float32(10000.0) ** (-np.arange(0, 32, 2, dtype=np.float32) / np.float32(32))).astype(np.float32)
    cstt = np.zeros((128, 4), np.float32)
    for base in (0, 64):
        for i in range(32):
            cstt[base + i, CST_F] = inv_freq[i % 16]
            cstt[base + i, CST_PH] = np.float32(np.pi / 2)
            cstt[base + 32 + i, CST_F] = -inv_freq[i % 16] if i < 16 else inv_freq[i % 16]
            cstt[base + 32 + i, CST_PH] = 0.0
    shared["cst"] = cstt
    x = f("x")
    pos = np.asarray(inputs["positions"]).astype(np.int32)
    percore = [{"xT": np.ascontiguousarray(x[b].T), "pos": pos[b][None, :]} for b in range(x.shape[0])]
    return shared, percore


ALL_STAGES = (("ffn1", 0), ("mixA", 0), ("ffn2", 0), ("ffn1", 1), ("mixB", 1), ("ffn2", 1))


def run(inputs, stages=ALL_STAGES, ncores=8, trace=False):
    shared, percore = host_prep(inputs)
    b = build_program(stages)
    names = set(["xT", "par"])
    kinds = set(k for k, _ in stages)
    if kinds & {"ffn1", "ffn2"}:
        names |= {"ffnw"}
    if "mixA" in kinds:
        names |= {"aw", "rbT", "pos"}
    if "mixB" in kinds:
        names |= {"bw", "bwh", "pos", "cst"}
    in_maps = [{k: v for k, v in dict(shared, **percore[i]).items() if k in names} for i in range(ncores)]
    res = run_bass_kernel_spmd(b.nc, in_maps, core_ids=list(range(ncores)), trace=trace)
    out = np.stack([np.ascontiguousarray(r["outT"].T) for r in res.results], axis=0)
    res.dbg = {k: res.results[0][k] for k in b.dbg_names}
    return out, res


def kernel(**inputs):
    out, _ = run(inputs)
    return out.astype(np.float32)


NEGB = -30000.0
T5_THR = [float(j) for j in range(1, 17)] + [float(int(np.ceil(16.0 * 8.0 ** (j / 16.0) - 1e-9))) for j in range(1, 16)]
A_SLOT_Q, A_SLOT_K, A_SLOT_V, A_SLOT_O = 0, 8, 10, 11
PC_AQG, PC_AKG, PC_ASINK = 48, 49, 50


def mixA_setup(b):
    nc, P = b.nc, b.P
    b.aw = nc.dram_tensor("aw", [19, 128, WSLOT], F32, kind="ExternalInput").ap()
    b.pos_d = nc.dram_tensor("pos", [1, S], I32, kind="ExternalInput").ap()
    b.rbT_d = nc.dram_tensor("rbT", [16, 32], F32, kind="ExternalInput").ap()
    b.gscr = nc.dram_tensor("gscr", [16, 128, 384], F32, kind="Internal")
    b.blk64 = b.sb("blk64", [128, 128], BF16)
    P.op("dve", lambda e: e.memset(b.blk64[:, :], 0.0), writes=[b.blk64])
    P.op("dve", lambda e: e.memset(b.blk64[0:64, 0:64], 1.0), writes=[b.blk64])
    P.op("dve", lambda e: e.memset(b.blk64[64:128, 64:128], 1.0), writes=[b.blk64])
    b.esink = b.sb("esink", [128, 16], F32)
    P.op("act", lambda e: e.activation(out=b.esink[:, :], in_=b.par[:, PC_ASINK:PC_ASINK + 16], func=AF.Exp),
         reads=[b.par_t], writes=[b.esink])


def mixA_declare(b):
    r = {}
    r["k"] = [b.w_declare(b.aw[A_SLOT_K + c], WSLOT) for c in range(2)]
    r["v"] = b.w_declare(b.aw[A_SLOT_V], WSLOT)
    r["q"], r["o"] = [None] * 8, [None] * 8
    for kind, i in (("q", 0), ("q", 1), ("q", 2), ("q", 3), ("q", 4), ("o", 0), ("o", 1), ("o", 2), ("o", 3),
                    ("q", 5), ("q", 6), ("q", 7), ("o", 4), ("o", 5), ("o", 6), ("o", 7)):
        r[kind][i] = b.w_declare(b.aw[(A_SLOT_Q if kind == "q" else A_SLOT_O) + i], WSLOT)
    order = r["k"] + [r["v"]]
    return r


def proj_headnorm(b, w, gcol, out_t, out_tiles, nrm_lhsT, hd, sq_pool, tmp_pool):
    P = b.P
    for T in range(NT):
        ts = slice(T * TT, (T + 1) * TT)
        raw = b.PS[b.ps_rr % 8]
        ss = b.PS[(b.ps_rr + 1) % 8]
        b.ps_rr += 2
        for c in range(8):
            P.op("pe", (lambda e, raw=raw, c=c, ts=ts: e.matmul(raw[:, :], lhsT=w[:, c * 128:(c + 1) * 128], rhs=b.hT_t[:, c, ts],
                                                               start=(c == 0), stop=(c == 7))),
                 reads=[w, b.hT[c][T]], writes=[raw], accum=(c != 0))
        sq = b.sq[b.sq_i % len(b.sq)]
        b.sq_i += 1
        P.op("act", (lambda e, sq=sq, raw=raw: e.activation(out=sq[:, :], in_=raw[:, :], func=AF.Square)), reads=[raw], writes=[sq])
        P.op("pe", (lambda e, ss=ss, sq=sq: e.matmul(ss[:, :], lhsT=nrm_lhsT[:, :], rhs=sq[:, :], start=True, stop=True)),
             reads=[nrm_lhsT, sq], writes=[ss])
        rs = b.sg[b.sg_i % len(b.sg)]
        b.sg_i += 1
        P.op("act", (lambda e, rs=rs, ss=ss: e.activation(out=rs[:, :], in_=ss[:, :], func=AF.Ln, scale=1.0 / hd, bias=b.epsc[:, 0:1])),
             reads=[ss, b.epsc], writes=[rs])
        P.op("act", (lambda e, rs=rs: e.activation(out=rs[:, :], in_=rs[:, :], func=AF.Exp, scale=-0.5)), reads=[rs], writes=[rs])
        P.op("dve", (lambda e, raw=raw, rs=rs, ts=ts: e.scalar_tensor_tensor(
            out=out_t[:, ts], in0=raw[:, :], scalar=b.par[:, gcol:gcol + 1], in1=rs[:, :], op0=ALU.mult, op1=ALU.mult)),
             reads=[raw, rs, b.par_t], writes=[out_tiles[T]])


def mixA_tables(b):
    nc, P = b.nc, b.P
    if not hasattr(b, "a_alloc"):
        b.a_alloc = True
        if b.mix_mark is None:
            b.mix_mark = b.sb_off
        b.sb_off = b.mix_mark
        b.fence_new = True
        b.kdup = b.act[0:2]
        b.kdup_T = b.act_T[0:2]
        b.qn = b.act[2:4]
        b.qn_T = b.act_T[2:4]
        b.OTp = [b.act[5], b.otx, b.otx2[0], b.otx2[1]]
        b.OTp_T = [b.act_T[5], b.otx_T, b.otx2_T[0], b.otx2_T[1]]
        b.pt = [(b.act[4][:, i * 512:(i + 1) * 512], b.act_T[4][i]) for i in range(4)]
        b.vx = [b.sb("vx", [128, NB, 192], BF16) for _ in range(2)]
        b.biasm = b.sb("biasm", [128, 8, 4, 128], F32)
        b.sc = b.rstd[0:2]
        b.dn = b.rstd[2:4]
        b.sc_i = b.pt_i = b.dn_i = 0
        posr_i = b.sb("posr_i", [16, 128], I32)
        posr = b.sb("posr", [16, 128], F32)
        dist = b.sb("dist", [16, 128], F32)
        rbT = b.sb("rbT", [16, 32], F32)
        dif = b.sb("dif", [16, 32], F32)
        acc = b.sb("bacc", [16, 128], F32)
        tmp = b.sb("btmp", [16, 128], F32)
        G = b.sb("G", [16, 384], F32)
        P.op("sp", lambda e: e.dma_start(out=posr_i[:, :], in_=b.pos_d[0:1, 0:128].partition_broadcast(16)), writes=[posr_i], dma=True)
        P.op("sp", lambda e: e.dma_start(out=rbT[:, :], in_=b.rbT_d[:, :]), writes=[rbT], dma=True)
        P.op("dve", lambda e: e.tensor_copy(out=posr[:, :], in_=posr_i[:, :]), reads=[posr_i], writes=[posr])
        P.op("dve", lambda e: e.tensor_scalar(out=dist[:, :], in0=posr[:, :], scalar1=posr[:, 0:1], scalar2=None, op0=ALU.subtract),
             reads=[posr], writes=[dist])
        P.op("dve", lambda e: e.tensor_tensor(out=dif[:, 1:32], in0=rbT[:, 1:32], in1=rbT[:, 0:31], op=ALU.subtract), reads=[rbT], writes=[dif])
        P.op("dve", lambda e: e.tensor_scalar(out=acc[:, :], in0=dist[:, :], scalar1=0.0, scalar2=rbT[:, 0:1], op0=ALU.mult, op1=ALU.add),
             reads=[dist, rbT], writes=[acc])
        for j in range(1, 32):
            P.op("dve", (lambda e, j=j: e.tensor_scalar(out=tmp[:, :], in0=dist[:, :], scalar1=T5_THR[j - 1], scalar2=dif[:, j:j + 1],
                                                        op0=ALU.is_ge, op1=ALU.mult)), reads=[dist, dif], writes=[tmp])
            P.op("dve", lambda e: e.tensor_tensor(out=acc[:, :], in0=acc[:, :], in1=tmp[:, :], op=ALU.add), reads=[acc, tmp], writes=[acc])
        P.op("dve", lambda e: e.memset(G[:, :], NEGB), writes=[G])
        P.op("dve", lambda e: e.tensor_copy(out=G[:, 127:255], in_=acc[:, :]), reads=[acc], writes=[G])
        gt = P.tile("gscr", None)
        P.op("sp", lambda e: e.dma_start(out=b.gscr.ap()[:, :, :], in_=G[:, :].unsqueeze(1).broadcast_to([16, 128, 384])), reads=[G], writes=[gt], dma=True)
        for h in range(16):
            for kt in range(2):
                off = h * 128 * 384 + 127 + (128 if kt == 0 else 0)
                src = bass.AP(tensor=b.gscr, offset=off, ap=[[383, 128], [1, 128]])
                P.op("sp", (lambda e, h=h, kt=kt, src=src: e.dma_start(out=b.biasm[:, h // 2, (h % 2) * 2 + kt, :], in_=src)),
                     reads=[gt], writes=[b.biasm], dma=True)
        for c in range(2):
            P.op("pool", (lambda e, c=c: e.memset(b.vx[c][:, :, :], 1.0)), writes=[b.vx[c]])
        b.fence_new = False


def mixA_run(b, r, l):
    nc, P = b.nc, b.P
    if DBG_STOP == 1:
        return
    b.rmsnorm(PC_MIX + 8 * l)
    for c in range(2):
        proj_headnorm(b, b.w_get(r["k"][c]), PC_AKG, b.kdup[c], b.kdup_T[c], b.blk64, 64, None, None)
    wv = b.w_get(r["v"])
    for g4 in range(4):
        vp = b.PS[b.ps_rr % 8]
        b.ps_rr += 1
        for j in range(4):
            n = g4 * 4 + j
            for c in range(8):
                P.op("pe", (lambda e, vp=vp, j=j, n=n, c=c: e.matmul(vp[:, j * 128:(j + 1) * 128], lhsT=b.hT_t[:, c, n * 128:(n + 1) * 128],
                                                                      rhs=wv[:, c * 128:(c + 1) * 128], start=(c == 0), stop=(c == 7))),
                     reads=[wv, b.hT[c][n // 4]], writes=[vp], accum=not (c == 0 and j == 0))
        for c2 in range(2):
            P.op("act", (lambda e, vp=vp, g4=g4, c2=c2: e.activation(
                out=b.vx[c2][:, g4 * 4:(g4 + 1) * 4, 64:128],
                in_=vp[:, :].rearrange("p (j c d) -> p j c d", j=4, c=2)[:, :, c2, :], func=AF.Copy)),
                 reads=[vp], writes=[b.vx[c2]])
    if DBG_STOP == 2 or DBG_DUMP:
        b.dump("kdup0", lambda: b.kdup[0][:, :], S, b.kdup_T[0])
        b.dump("vx0", lambda: b.vx[0][:, :, :].rearrange("p n c -> p (n c)"), NB * 192, [b.vx[0]])
    if DBG_STOP == 2:
        return
    PACC = b.PS[0:2]
    PST4 = b.PS[2:6]
    PSR = b.PS[6:8]
    psr_i = [0]

    def psr():
        t = PSR[psr_i[0] % 2]
        psr_i[0] += 1
        return t
    prepQ, bgQ = [], []
    st_i = [0]

    def qproj_tasks(oc):
        qn, qT = b.qn[oc % 2], b.qn_T[oc % 2]
        ctx = {}
        tasks = []
        for T in range(NT):
            def t(T=T):
                if "w" not in ctx:
                    ctx["w"] = b.w_get(r["q"][oc])
                w = ctx["w"]
                ts = slice(T * TT, (T + 1) * TT)
                raw, ss = psr(), psr()
                for c in range(8):
                    P.op("pe", (lambda e, c=c: e.matmul(raw[:, :], lhsT=w[:, c * 128:(c + 1) * 128], rhs=b.hT_t[:, c, ts], start=(c == 0), stop=(c == 7))),
                         reads=[w, b.hT[c][T]], writes=[raw], accum=(c != 0))
                sq = b.sq[b.sq_i % len(b.sq)]
                b.sq_i += 1
                P.op("act", (lambda e: e.activation(out=sq[:, :], in_=raw[:, :], func=AF.Square)), reads=[raw], writes=[sq])
                P.op("pe", (lambda e: e.matmul(ss[:, :], lhsT=b.blk64[:, :], rhs=sq[:, :], start=True, stop=True)), reads=[b.blk64, sq], writes=[ss])
                rs = b.sg[b.sg_i % len(b.sg)]
                b.sg_i += 1
                P.op("act", (lambda e: e.activation(out=rs[:, :], in_=ss[:, :], func=AF.Ln, scale=1.0 / 64, bias=b.epsc[:, 0:1])),
                     reads=[ss, b.epsc], writes=[rs])
                P.op("act", (lambda e: e.activation(out=rs[:, :], in_=rs[:, :], func=AF.Exp, scale=-0.5)), reads=[rs], writes=[rs])
                P.op("dve", (lambda e: e.scalar_tensor_tensor(out=qn[:, ts], in0=raw[:, :], scalar=b.par[:, PC_AQG:PC_AQG + 1], in1=rs[:, :],
                                                              op0=ALU.mult, op1=ALU.mult)), reads=[raw, rs, b.par_t], writes=[qT[T]])
            tasks.append(t)
        return tasks

    def outproj_group_tasks(g):
        ctx = {}
        tasks = []
        for dc in range(8):
            for T in range(NT):
                def t(dc=dc, T=T):
                    if "wo" not in ctx:
                        ctx["wo"] = [b.w_get(r["o"][4 * g + i]) for i in range(4)]
                    wos = ctx["wo"]
                    ts = slice(T * TT, (T + 1) * TT)
                    op_ = psr()
                    for i in range(4):
                        P.op("pe", (lambda e, i=i: e.matmul(op_[:, :], lhsT=wos[i][:, dc * 128:(dc + 1) * 128], rhs=b.OTp[i][:, ts],
                                                            start=(i == 0), stop=(i == 3))),
                             reads=[wos[i], b.OTp_T[i][T]], writes=[op_], accum=(i != 0))
                    x = b.xT[dc][T]
                    P.op("dve", (lambda e: e.tensor_tensor(out=b.xT_t[:, dc, ts], in0=op_[:, :], in1=b.xT_t[:, dc, ts], op=ALU.add)),
                         reads=[op_, x], writes=[x])
                tasks.append(t)
        return tasks

    def pop_tasks(n_bg=2):
        if prepQ:
            prepQ.pop(0)()
        for _ in range(n_bg):
            if bgQ:
                bgQ.pop(0)()

    def scores(oc, step):
        c = oc // 4
        qn, qT = b.qn[oc % 2], b.qn_T[oc % 2]
        sts = (PST4[(st_i[0] * 2) % 4], PST4[(st_i[0] * 2 + 1) % 4])
        st_i[0] += 1
        outs = []
        for hh in range(2):
            rows = slice(hh * 64, (hh + 1) * 64)
            st = sts[hh]
            first = True
            for jj in range(2):
                n = step * 2 + jj
                qs = slice(n * 128, (n + 1) * 128)
                for kt in range(2):
                    if n == 0 and kt == 0:
                        continue
                    kb = n - 1 + kt
                    sl = (jj * 2 + kt) * 128
                    P.op("pe", (lambda e, st=st, rows=rows, kb=kb, sl=sl, qs=qs: e.matmul(
                        st[:, sl:sl + 128], lhsT=b.kdup[c][rows, kb * 128:(kb + 1) * 128], rhs=qn[rows, qs], start=True, stop=True)),
                         reads=[b.kdup_T[c][kb // 4], qT[n // 4]], writes=[st], accum=not first)
                    first = False
        for hh in range(2):
            st = sts[hh]
            sc = b.sc[b.sc_i % 2]
            b.sc_i += 1
            pt, ptT = b.pt[b.pt_i % len(b.pt)]
            b.pt_i += 1
            outs.append((pt, ptT))
            P.op("dve", (lambda e, sc=sc, st=st, hh=hh: e.scalar_tensor_tensor(
                out=sc[:, :].rearrange("p (j a) -> p j a", j=2), in0=st[:, :].rearrange("p (j a) -> p j a", j=2), scalar=0.125,
                in1=b.biasm[:, oc, hh * 2:hh * 2 + 2, :].rearrange("p a q -> p (a q)").unsqueeze(1).broadcast_to([128, 2, 256]),
                op0=ALU.mult, op1=ALU.add)), reads=[st, b.biasm], writes=[sc])
            P.op("act", (lambda e, sc=sc, pt=pt: e.activation(out=pt, in_=sc[:, :], func=AF.Exp)), reads=[sc], writes=[ptT])
        return outs

    def pv(oc, step, outs):
        c = oc // 4
        for hh, vcols in ((0, slice(64, 192)), (1, slice(0, 128))):
            acc_ = PACC[hh]
            pt, ptT = outs[hh]
            for jj in range(2):
                n = step * 2 + jj
                j = n % 4
                for kt in range(2):
                    if n == 0 and kt == 0:
                        continue
                    kb = n - 1 + kt
                    sl = (jj * 2 + kt) * 128
                    P.op("pe", (lambda e, acc_=acc_, kb=kb, sl=sl, j=j, kt=kt, vcols=vcols, pt=pt, n=n: e.matmul(
                        acc_[:, j * 128:(j + 1) * 128], lhsT=b.vx[c][:, kb, vcols], rhs=pt[:, sl:sl + 128],
                        start=(kt == 0 or n == 0), stop=(kt == 1))),
                         reads=[b.vx[c], ptT], writes=[acc_], accum=not (j == 0 and (kt == 0 or n == 0)))

    def normalise(oc, bg):
        ts = slice(bg * 512, (bg + 1) * 512)
        for hh in range(2):
            acc_ = PACC[hh]
            h = 2 * oc + hh
            orow = slice(hh * 64, (hh + 1) * 64)
            drow = slice((1 - hh) * 64, (2 - hh) * 64)
            dn = b.dn[b.dn_i % 2]
            b.dn_i += 1
            P.op("act", (lambda e, dn=dn, acc_=acc_, drow=drow, h=h: e.activation(out=dn[drow, :], in_=acc_[drow, :], func=AF.Ln,
                                                                               bias=b.esink[drow, h:h + 1], scale=1.0)),
                 reads=[acc_, b.esink], writes=[dn])
            P.op("act", (lambda e, dn=dn, drow=drow: e.activation(out=dn[drow, :], in_=dn[drow, :], func=AF.Exp, scale=-1.0)), reads=[dn], writes=[dn])
            P.op("dve", (lambda e, dn=dn, acc_=acc_, drow=drow, orow=orow: e.tensor_tensor(
                out=b.OTp[oc % 4][orow, ts], in0=acc_[orow, :], in1=dn[drow, :], op=ALU.mult)),
                 reads=[acc_, dn], writes=[b.OTp_T[oc % 4][bg]], accum=(hh == 1))

    for t in qproj_tasks(0):
        t()
    for oc in range(8):
        if (DBG_STOP == 3 or DBG_DUMP) and oc == 1:
            b.dump("qn0", lambda: b.qn[0][:, :], S, b.qn_T[0])
            b.dump("ot0", lambda: b.OTp[0][:, :], S, b.OTp_T[0])
        if DBG_STOP == 3 and oc == 1:
            return
        if oc < 7:
            prepQ.extend(qproj_tasks(oc + 1))
        nxt = scores(oc, 0)
        for step in range(8):
            cur = nxt
            if step < 7:
                nxt = scores(oc, step + 1)
            pv(oc, step, cur)
            if step % 2 == 1:
                normalise(oc, step // 2)
            pop_tasks()
        while prepQ:
            prepQ.pop(0)()
        if oc % 4 == 3:
            for t in outproj_group_tasks(oc // 4):
                t()


def attn_out_proj_pair(b, wo, ot, ot_T, ps_pool=None):
    P = b.P
    for dc in range(8):
        for T in range(NT):
            ts = slice(T * TT, (T + 1) * TT)
            if ps_pool is None:
                op_ = b.PS[b.ps_rr % 8]
            else:
                op_ = ps_pool[b.ps_rr % len(ps_pool)]
            b.ps_rr += 1
            P.op("pe", (lambda e, op_=op_, dc=dc, ts=ts: e.matmul(op_[:, :], lhsT=wo[:, dc * 128:(dc + 1) * 128], rhs=ot[:, ts], start=True, stop=True)),
                 reads=[wo, ot_T[T]], writes=[op_])
            x = b.xT[dc][T]
            P.op("dve", (lambda e, op_=op_, dc=dc, ts=ts: e.tensor_tensor(out=b.xT_t[:, dc, ts], in0=op_[:, :], in1=b.xT_t[:, dc, ts], op=ALU.add)),
                 reads=[op_, x], writes=[x])


def attn_out_proj(b, oreq):
    P = b.P
    for dc in range(8):
        wo = b.w_get(oreq[dc])
        for T in range(NT):
            ts = slice(T * TT, (T + 1) * TT)
            op_ = b.PS[b.ps_rr % 8]
            b.ps_rr += 1
            for oc in range(8):
                P.op("pe", (lambda e, op_=op_, oc=oc, ts=ts, wo=wo: e.matmul(op_[:, :], lhsT=wo[:, oc * 128:(oc + 1) * 128], rhs=b.OT_t[:, oc, ts],
                                                                          start=(oc == 0), stop=(oc == 7))),
                     reads=[wo, b.OT[oc][T]], writes=[op_], accum=(oc != 0))
            x = b.xT[dc][T]
            P.op("dve", (lambda e, op_=op_, dc=dc, ts=ts: e.tensor_tensor(out=b.xT_t[:, dc, ts], in0=op_[:, :], in1=b.xT_t[:, dc, ts], op=ALU.add)),
                 reads=[op_, x], writes=[x])


B_SLOT_C, B_SLOT_KR, B_SLOT_O = 0, 3, 4
PC_BQN, PC_BKVN, PC_BQG, PC_BKG, PC_BKRG = 66, 68, 69, 70, 71
CST_F, CST_PH = 0, 1
TWO_PI = float(2.0 * np.pi)
CW1 = 6.28125
CW2 = float(2.0 * np.pi - 6.28125)
MLA_SCALE = float(96.0 ** -0.5)
MASK_ENG = os.environ.get("MASK_ENG", "pool")
FUSE_NORM = os.environ.get("FUSE_NORM", "1") == "1"
ROPEK_ENG = os.environ.get("ROPEK_ENG", "pool")
CPH = int(os.environ.get("CPH", "0"))


def mixB_setup(b):
    nc = b.nc
    b.bw = nc.dram_tensor("bw", [12, 128, WSLOT], F32, kind="ExternalInput").ap()
    b.bwh = nc.dram_tensor("bwh", [16, 128, 384], F32, kind="ExternalInput").ap()
    b.cst_d = nc.dram_tensor("cst", [128, 4], F32, kind="ExternalInput").ap()
    if not hasattr(b, "pos_d"):
        b.pos_d = nc.dram_tensor("pos", [1, S], I32, kind="ExternalInput").ap()


def mixB_declare(b):
    r = {}
    r["c"] = [b.w_declare(b.bw[B_SLOT_C + i], WSLOT) for i in range(3)]
    r["kr"] = b.w_declare(b.bw[B_SLOT_KR], WSLOT)
    r["h"], r["o"] = [], []
    for h in range(16):
        r["h"].append(b.w_declare(b.bwh[h], 384))
        if h % 2 == 1:
            r["o"].append(b.w_declare(b.bw[B_SLOT_O + h // 2], WSLOT))
    return r


def mixB_consts(b):
    nc, P = b.nc, b.P
    if b.mix_mark is None:
        b.mix_mark = b.sb_off
    b.sb_off = b.mix_mark
    b.fence_new = True
    CS = b.sb("CS", [128, S], F32)
    krfin = b.sb("krfin", [128, S], F32)
    uvs = [b.sb("uv", [128, TT], BF16) for _ in range(2)]
    cst = b.sb("cst", [128, 4], F32)
    tri = b.sb("tri", [128, 128], BF16)
    mA = b.sb("mA", [128, 128], BF16)
    sel = b.sb("sel", [128, 128], BF16)
    posi = b.sb("posi", [128, TT], I32)
    b.fence_new = False
    uv_i = [0]
    cqn = b.act[0:2]
    cqn_T = b.act_T[0:2]
    ckvn = b.act[2]
    ckvn_T = b.act_T[2]
    sqK = [(b.act[3][:, i * 512:(i + 1) * 512], b.act_T[3][i]) for i in range(4)]
    pts = [(b.act[4][:, i * 512:(i + 1) * 512], b.act_T[4][i]) for i in range(4)]
    pt_i = [0]
    OTp = [b.act[5], b.otx]
    OTp_T = [b.act_T[5], b.otx_T]
    qh = [b.hT_t[:, i, :] for i in range(2)]
    qh_T = [b.hT[i] for i in range(2)]
    kh = [b.hT_t[:, 2 + i, :] for i in range(2)]
    kh_T = [b.hT[2 + i] for i in range(2)]
    vxs, vxs_T = [], []
    for i in range(2):
        v = b.hT_t[:, 4 + 2 * i:6 + 2 * i, :].rearrange("p a s -> p (a s)")[:, 0:NB * 192].rearrange("p (n c) -> p n c", c=192)
        vxs.append(v)
        vxs_T.append(b.hT[4 + 2 * i] + b.hT[5 + 2 * i])

    P.op("sp", lambda e: e.dma_start(out=cst[:, :], in_=b.cst_d[:, :]), writes=[cst], dma=True)
    P.op("pool", lambda e: e.memset(tri[:, :], 1.0), writes=[tri])
    P.op("pool", lambda e: e.affine_select(out=tri[:, :], in_=tri[:, :], pattern=[[1, 128]], compare_op=ALU.is_ge, fill=0.0,
                                           base=0, channel_multiplier=-1), reads=[tri], writes=[tri])
    P.op("pool", lambda e: e.memset(sel[:, :], 1.0), writes=[sel])
    for r0 in (0, 32):
        P.op("pool", (lambda e, r0=r0: e.affine_select(out=sel[r0:r0 + 32, :], in_=sel[r0:r0 + 32, :], pattern=[[-1, 128]], compare_op=ALU.is_equal,
                                                      fill=0.0, base=0, channel_multiplier=1)), reads=[sel], writes=[sel])
    P.op("pool", lambda e: e.memset(sel[64:128, :], 0.0), reads=[sel], writes=[sel])
    for u_ in uvs:
        P.op("dve", (lambda e, u_=u_: e.memset(u_[:, :], 0.0)), writes=[u_])
    P.op("dve", lambda e: e.memset(mA[:, :], 1.0), writes=[mA])
    P.op("dve", lambda e: e.memset(mA[32:64, :], 0.0), writes=[mA])
    for T in range(NT):
        ts = slice(T * TT, (T + 1) * TT)
        a = b.rstd[T % 2]
        kf = b.rstd[2 + T % 2]
        P.op("sp", (lambda e, ts=ts: e.dma_start(out=posi[:, :], in_=b.pos_d[0:1, ts].partition_broadcast(128))), writes=[posi], dma=True)
        P.op("dve", (lambda e, a=a: e.tensor_copy(out=a[:, :], in_=posi[:, :])), reads=[posi], writes=[a])
        P.op("dve", (lambda e, a=a: e.tensor_scalar(out=a[:, :], in0=a[:, :], scalar1=cst[:, CST_F:CST_F + 1], scalar2=cst[:, CST_PH:CST_PH + 1],
                                                    op0=ALU.mult, op1=ALU.add)), reads=[a, cst], writes=[a])
        ki = posi
        P.op("dve", (lambda e, a=a, kf=kf: e.tensor_scalar(out=kf[:, :], in0=a[:, :], scalar1=1.0 / TWO_PI, scalar2=None, op0=ALU.mult)),
             reads=[a], writes=[kf])
        P.op("dve", (lambda e, kf=kf: e.tensor_copy(out=ki[:, :], in_=kf[:, :])), reads=[kf], writes=[ki])
        P.op("dve", (lambda e, kf=kf: e.tensor_copy(out=kf[:, :], in_=ki[:, :])), reads=[ki], writes=[kf])
        P.op("dve", (lambda e, a=a, kf=kf: e.scalar_tensor_tensor(out=a[:, :], in0=kf[:, :], scalar=-CW1, in1=a[:, :], op0=ALU.mult, op1=ALU.add)),
             reads=[a, kf], writes=[a])
        P.op("dve", (lambda e, a=a, kf=kf: e.scalar_tensor_tensor(out=a[:, :], in0=kf[:, :], scalar=-CW2, in1=a[:, :], op0=ALU.mult, op1=ALU.add)),
             reads=[a, kf], writes=[a])
        P.op("dve", (lambda e, a=a, kf=kf: e.tensor_scalar(out=kf[:, :], in0=a[:, :], scalar1=float(np.pi), scalar2=-TWO_PI, op0=ALU.is_gt, op1=ALU.mult)),
             reads=[a], writes=[kf])
        P.op("dve", (lambda e, a=a, kf=kf: e.tensor_tensor(out=a[:, :], in0=a[:, :], in1=kf[:, :], op=ALU.add)), reads=[a, kf], writes=[a])
        P.op("dve", (lambda e, a=a, kf=kf: e.tensor_scalar(out=kf[:, :], in0=a[:, :], scalar1=-float(np.pi), scalar2=TWO_PI, op0=ALU.is_lt, op1=ALU.mult)),
             reads=[a], writes=[kf])
        P.op("dve", (lambda e, a=a, kf=kf: e.tensor_tensor(out=a[:, :], in0=a[:, :], in1=kf[:, :], op=ALU.add)), reads=[a, kf], writes=[a])
        P.op("dve", (lambda e, a=a: e.tensor_scalar(out=a[:, :], in0=a[:, :], scalar1=float(np.pi), scalar2=-float(np.pi), op0=ALU.min, op1=ALU.max)),
             reads=[a], writes=[a])
        P.op("act", (lambda e, a=a, ts=ts: e.activation(out=CS[0:64, ts], in_=a[0:64, :], func=AF.Sin)), reads=[a], writes=[CS])

    b.mla = dict(CS=CS, krfin=krfin, uvs=uvs, tri=tri, mA=mA, sel=sel, uv_i=uv_i, cqn=cqn, cqn_T=cqn_T, ckvn=ckvn, ckvn_T=ckvn_T,
                 sqK=sqK, pts=pts, pt_i=pt_i, OTp=OTp, OTp_T=OTp_T, qh=qh, qh_T=qh_T, kh=kh, kh_T=kh_T, vxs=vxs, vxs_T=vxs_T)


def mixB_run(b, r, l):
    nc, P = b.nc, b.P
    PSA = b.PS[0:2]
    PST = b.PS[2:4]
    PSH = b.PS[4:6]
    PSP = b.PS[6:8]
    st_i = [0]
    pp_i = [0]

    def psp():
        t = PSP[pp_i[0] % 2]
        pp_i[0] += 1
        return t

    if not hasattr(b, "mla"):
        mixB_consts(b)
    L = b.mla
    CS, krfin, uvs, tri, mA, sel, uv_i = L["CS"], L["krfin"], L["uvs"], L["tri"], L["mA"], L["sel"], L["uv_i"]
    cqn, cqn_T, ckvn, ckvn_T, sqK, pts, pt_i = L["cqn"], L["cqn_T"], L["ckvn"], L["ckvn_T"], L["sqK"], L["pts"], L["pt_i"]
    OTp, OTp_T, qh, qh_T, kh, kh_T, vxs, vxs_T = L["OTp"], L["OTp_T"], L["qh"], L["qh_T"], L["kh"], L["kh_T"], L["vxs"], L["vxs_T"]
    if DBG_STOP == 10:
        return
    b.rmsnorm(PC_MIX + 8 * l)

    def proj8(w, wcols, out_ps, T, mrows=128):
        ts = slice(T * TT, (T + 1) * TT)
        for c in range(8):
            P.op("pe", (lambda e, c=c, ts=ts: e.matmul(out_ps[0:mrows, :], lhsT=w[:, c * wcols:(c + 1) * wcols], rhs=b.hT_t[:, c, ts],
                                                       start=(c == 0), stop=(c == 7))),
                 reads=[w, b.hT[c][T]], writes=[out_ps], accum=(c != 0))

    def rstd_from(ss_ps, n, rs):
        P.op("act", (lambda e: e.activation(out=rs[:, :], in_=ss_ps[:, :], func=AF.Ln, scale=1.0 / n, bias=b.epsc[:, 0:1])),
             reads=[ss_ps, b.epsc], writes=[rs])
        P.op("act", (lambda e: e.activation(out=rs[:, :], in_=rs[:, :], func=AF.Exp, scale=-0.5)), reads=[rs], writes=[rs])

    def rope_sum(uv, out_ps):
        P.op("pe", (lambda e: e.matmul(out_ps[:, :], lhsT=sel[:, :], rhs=uv[:, :], start=True, stop=True)), reads=[sel, uv], writes=[out_ps])

    wc = [b.w_get(i) for i in r["c"]]
    wkr = b.w_get(r["kr"])
    for T in range(NT):
        ts = slice(T * TT, (T + 1) * TT)
        raws = [PSH[0], PSH[1]]
        ss = psp()
        for c2 in range(2):
            proj8(wc[c2], 128, raws[c2], T)
            sq = b.sq[b.sq_i % len(b.sq)]
            b.sq_i += 1
            P.op("act", (lambda e, sq=sq, raw=raws[c2]: e.activation(out=sq[:, :], in_=raw[:, :], func=AF.Square)), reads=[raws[c2]], writes=[sq])
            P.op("pe", (lambda e, sq=sq, c2=c2, ss=ss: e.matmul(ss[:, :], lhsT=b.ones[:, :], rhs=sq[:, :], start=(c2 == 0), stop=(c2 == 1))),
                 reads=[b.ones, sq], writes=[ss], accum=(c2 != 0))
        rs = b.sg[b.sg_i % len(b.sg)]
        b.sg_i += 1
        rstd_from(ss, 256.0, rs)
        for c2 in range(2):
            P.op("dve", (lambda e, c2=c2, rs=rs, ts=ts, raw=raws[c2]: e.scalar_tensor_tensor(
                out=cqn[c2][:, ts], in0=raw[:, :], scalar=b.par[:, PC_BQN + c2:PC_BQN + c2 + 1], in1=rs[:, :], op0=ALU.mult, op1=ALU.mult)),
                 reads=[raws[c2], rs, b.par_t], writes=[cqn_T[c2][T]])
        if CPH == 1:
            continue
        raw = PSH[0]
        ss = psp()
        proj8(wc[2], 128, raw, T)
        sq = b.sq[b.sq_i % len(b.sq)]
        b.sq_i += 1
        P.op("act", (lambda e, sq=sq, raw=raw: e.activation(out=sq[:, :], in_=raw[:, :], func=AF.Square)), reads=[raw], writes=[sq])
        P.op("pe", (lambda e, sq=sq, ss=ss: e.matmul(ss[:, :], lhsT=b.ones[:, :], rhs=sq[:, :], start=True, stop=True)), reads=[b.ones, sq], writes=[ss])
        rs = b.sg[b.sg_i % len(b.sg)]
        b.sg_i += 1
        rstd_from(ss, 128.0, rs)
        P.op("dve", (lambda e, rs=rs, ts=ts, raw=raw: e.scalar_tensor_tensor(
            out=ckvn[:, ts], in0=raw[:, :], scalar=b.par[:, PC_BKVN:PC_BKVN + 1], in1=rs[:, :], op0=ALU.mult, op1=ALU.mult)),
             reads=[raw, rs, b.par_t], writes=[ckvn_T[T]])
        if CPH == 2:
            continue
        raw = PSH[1]
        proj8(wkr, 128, raw, T)
        sqk, sqk_T = sqK[T]
        if CPH != 6:
            P.op("dve", (lambda e, sqk=sqk: e.memset(sqk[32:64, :], 0.0)), writes=[sqk_T])
            if CPH != 7:
                P.op("act", (lambda e, sqk=sqk, raw=raw: e.activation(out=sqk[0:32, :], in_=raw[0:32, :], func=AF.Square)), reads=[raw], writes=[sqk_T], accum=True)
        if CPH == 8:
            continue
        uv = uvs[uv_i[0] % 2]
        uv_i[0] += 1
        P.op("dve", (lambda e, uv=uv, raw=raw, ts=ts: e.scalar_tensor_tensor(
            out=uv[0:64, :], in0=raw[0:64, :], scalar=b.par[0:64, PC_BKRG:PC_BKRG + 1], in1=CS[0:64, ts], op0=ALU.mult, op1=ALU.mult)),
             reads=[raw, CS, b.par_t], writes=[uv])
        if CPH == 4:
            continue
        kps = psp()
        rope_sum(uv, kps)
        if CPH == 5:
            continue
        P.op("act", (lambda e, kps=kps, ts=ts: e.activation(out=krfin[0:32, ts], in_=kps[0:32, :], func=AF.Copy)), reads=[kps], writes=[krfin])

    if DBG_STOP == 11:
        return
    for i in range(2):
        P.op(MASK_ENG, (lambda e, i=i: e.memset(vxs[i], 1.0)), writes=vxs_T[i])
        P.op(MASK_ENG, (lambda e, i=i: e.memset(qh[i][32:64, :], 0.0)), writes=qh_T[i])
        P.op(MASK_ENG, (lambda e, i=i: e.memset(kh[i][32:64, :], 0.0)), writes=kh_T[i])

    prepQ, bgQ, normQ = [], [], []

    def prep_tasks(h):
        tasks = []
        q_t, q_T = qh[h % 2], qh_T[h % 2]
        k_t, k_T = kh[h % 2], kh_T[h % 2]
        vx, vx_T = vxs[h % 2], vxs_T[h % 2]
        for T in range(NT):
            ts = slice(T * TT, (T + 1) * TT)
            ctx = {}

            def tA(T=T, ts=ts, ctx=ctx):
                wh = b.w_get(r["h"][h])
                ctx["wh"] = wh
                qraw, kraw = PSH[0], PSH[1]
                for c2 in range(2):
                    P.op("pe", (lambda e, c2=c2: e.matmul(qraw[:, :], lhsT=wh[:, c2 * 128:(c2 + 1) * 128], rhs=cqn[c2][:, ts],
                                                          start=(c2 == 0), stop=(c2 == 1))),
                         reads=[wh, cqn_T[c2][T]], writes=[qraw], accum=(c2 != 0))
                sq = b.sq[b.sq_i % len(b.sq)]
                b.sq_i += 1
                ctx["sq"] = sq
                P.op("act", (lambda e: e.activation(out=sq[:, :], in_=qraw[:, :], func=AF.Square)), reads=[qraw], writes=[sq])
                P.op("pe", (lambda e: e.matmul(kraw[:, :], lhsT=wh[:, 192:320], rhs=ckvn[:, ts], start=True, stop=True)),
                     reads=[wh, ckvn_T[T]], writes=[kraw])
                sqk, sqk_T = sqK[T]
                P.op("act", (lambda e: e.activation(out=sqk[64:128, :], in_=kraw[64:128, :], func=AF.Square)), reads=[kraw], writes=[sqk_T])
            tasks.append(tA)
            if T == 0:
                for half in range(2):
                    def tV(half=half, ctx=ctx):
                        wh = ctx["wh"]
                        vp = psp()
                        for j in range(8):
                            n = half * 8 + j
                            P.op("pe", (lambda e, j=j, n=n: e.matmul(vp[:, j * 64:(j + 1) * 64], lhsT=ckvn[:, n * 128:(n + 1) * 128], rhs=wh[:, 320:384],
                                                                 start=True, stop=True)),
                                 reads=[wh, ckvn_T[n // 4]], writes=[vp], accum=(j != 0))
                        P.op("act", (lambda e: e.activation(out=vx[:, half * 8:(half + 1) * 8, 64:128],
                                                            in_=vp[:, :].rearrange("p (j d) -> p j d", j=8), func=AF.Copy)),
                             reads=[vp], writes=vx_T)
                    tasks.append(tV)

            def t1(ctx=ctx):
                ss = psp()
                sq = ctx["sq"]
                P.op("pe", (lambda e: e.matmul(ss[:, :], lhsT=mA[:, :], rhs=sq[:, :], start=True, stop=True)), reads=[mA, sq], writes=[ss])
                rs = b.sg[b.sg_i % len(b.sg)]
                b.sg_i += 1
                ctx["rs"] = rs
                rstd_from(ss, 96.0, rs)
            tasks.append(t1)

            def t2(T=T, ts=ts, ctx=ctx):
                rs = ctx["rs"]
                qraw = PSH[0]
                P.op("dve", (lambda e: e.scalar_tensor_tensor(out=q_t[64:128, ts], in0=qraw[64:128, :], scalar=b.par[64:128, PC_BQG:PC_BQG + 1],
                                                              in1=rs[64:128, :], op0=ALU.mult, op1=ALU.mult)), reads=[qraw, rs, b.par_t], writes=[q_T[T]])
                uv = uvs[uv_i[0] % 2]
                uv_i[0] += 1
                ctx["uv"] = uv
                P.op("dve", (lambda e: e.scalar_tensor_tensor(out=uv[0:64, :], in0=qraw[0:64, :], scalar=b.par[0:64, PC_BQG:PC_BQG + 1], in1=CS[0:64, ts],
                                                              op0=ALU.mult, op1=ALU.mult)), reads=[qraw, CS, b.par_t], writes=[uv])
            tasks.append(t2)

            def t3(T=T, ts=ts, ctx=ctx):
                rs = ctx["rs"]
                qps = psp()
                rope_sum(ctx["uv"], qps)
                P.op("dve", (lambda e: e.tensor_tensor(out=q_t[0:32, ts], in0=qps[0:32, :], in1=rs[0:32, :], op=ALU.mult)),
                     reads=[qps, rs], writes=[q_T[T]], accum=True)
            tasks.append(t3)

            def t4(T=T, ctx=ctx):
                sqk, sqk_T = sqK[T]
                ssk = psp()
                P.op("pe", (lambda e: e.matmul(ssk[:, :], lhsT=mA[:, :], rhs=sqk, start=True, stop=True)), reads=[mA, sqk_T], writes=[ssk])
                rsk = b.sg[b.sg_i % len(b.sg)]
                b.sg_i += 1
                ctx["rsk"] = rsk
                rstd_from(ssk, 96.0, rsk)
            tasks.append(t4)

            def t5(T=T, ts=ts, ctx=ctx):
                rsk = ctx["rsk"]
                kraw = PSH[1]
                P.op("dve", (lambda e: e.scalar_tensor_tensor(out=k_t[64:128, ts], in0=kraw[64:128, :], scalar=b.par[64:128, PC_BKG:PC_BKG + 1],
                                                              in1=rsk[64:128, :], op0=ALU.mult, op1=ALU.mult)), reads=[kraw, rsk, b.par_t], writes=[k_T[T]])
                P.op(ROPEK_ENG, (lambda e: e.tensor_tensor(out=k_t[0:32, ts], in0=krfin[0:32, ts], in1=rsk[0:32, :], op=ALU.mult)),
                     reads=[krfin, rsk], writes=[k_T[T]], accum=True)
            tasks.append(t5)
        return tasks

    def norm_task(h, Qc, acc):
        def t():
            ts = slice(Qc * 512, (Qc + 1) * 512)
            hh = h % 2
            orow = slice(hh * 64, (hh + 1) * 64)
            drow = slice((1 - hh) * 64, (2 - hh) * 64)
            dn = b.rstd[2 + (h * 4 + Qc) % 2]
            P.op("dve", (lambda e: e.reciprocal(out=dn[drow, :], in_=acc[drow, :])), reads=[acc], writes=[dn])
            ot, ot_T = OTp[(h // 2) % 2], OTp_T[(h // 2) % 2]
            P.op("dve", (lambda e: e.tensor_tensor(out=ot[orow, ts], in0=acc[orow, :], in1=dn[drow, :], op=ALU.mult)),
                 reads=[acc, dn], writes=[ot_T[Qc]], accum=(hh == 1))
        return t

    def outproj_tasks(p):
        ot, ot_T = OTp[p % 2], OTp_T[p % 2]
        tasks = []
        ctx = {}
        for dc in range(8):
            for T in range(NT):
                def t(dc=dc, T=T):
                    if "wo" not in ctx:
                        ctx["wo"] = b.w_get(r["o"][p])
                    wo = ctx["wo"]
                    ts = slice(T * TT, (T + 1) * TT)
                    op_ = psp()
                    P.op("pe", (lambda e: e.matmul(op_[:, :], lhsT=wo[:, dc * 128:(dc + 1) * 128], rhs=ot[:, ts], start=True, stop=True)),
                         reads=[wo, ot_T[T]], writes=[op_])
                    x = b.xT[dc][T]
                    P.op("dve", (lambda e: e.tensor_tensor(out=b.xT_t[:, dc, ts], in0=op_[:, :], in1=b.xT_t[:, dc, ts], op=ALU.add)),
                         reads=[op_, x], writes=[x])
                tasks.append(t)
        return tasks

    def pop_tasks():
        if normQ:
            normQ.pop(0)()
        if prepQ:
            prepQ.pop(0)()
        if bgQ:
            bgQ.pop(0)()

    def attend(h, Qc):
        q_t, q_T = qh[h % 2], qh_T[h % 2]
        k_t, k_T = kh[h % 2], kh_T[h % 2]
        vx, vx_T = vxs[h % 2], vxs_T[h % 2]
        vcols = slice(64, 192) if h % 2 == 0 else slice(0, 128)
        acc = PSA[(h * 4 + Qc) % 2]
        jmax = 4 * Qc + 3
        while len(normQ) > 1:
            normQ.pop(0)()

        def scores(j):
            c0 = max(0, j - 4 * Qc) * 128
            st = PST[st_i[0] % 2]
            st_i[0] += 1
            P.op("pe", (lambda e, st=st, j=j, c0=c0: e.matmul(st[:, c0:512], lhsT=k_t[:, j * 128:(j + 1) * 128], rhs=q_t[:, Qc * 512 + c0:(Qc + 1) * 512],
                                                         start=True, stop=True)), reads=[k_T[j // 4], q_T[Qc]], writes=[st])
            pt, ptT = pts[pt_i[0] % 4]
            pt_i[0] += 1
            P.op("act", (lambda e, st=st, pt=pt, c0=c0: e.activation(out=pt[:, c0:512], in_=st[:, c0:512], func=AF.Exp, scale=MLA_SCALE)),
                 reads=[st], writes=[ptT])
            if j >= 4 * Qc:
                P.op(MASK_ENG, (lambda e, pt=pt, c0=c0: e.tensor_tensor(out=pt[:, c0:c0 + 128], in0=pt[:, c0:c0 + 128], in1=tri[:, :], op=ALU.mult)),
                     reads=[ptT, tri], writes=[ptT])
            return pt, ptT, c0

        nxt = scores(0)
        for j in range(jmax + 1):
            pt, ptT, c0 = nxt
            if j < jmax:
                nxt = scores(j + 1)
            P.op("pe", (lambda e, acc=acc, j=j, c0=c0, pt=pt: e.matmul(acc[:, c0:512], lhsT=vx[:, j, vcols], rhs=pt[:, c0:512], start=(j == 0), stop=(j == jmax))),
                 reads=vx_T + [ptT], writes=[acc], accum=(j != 0))
            pop_tasks()
        normQ.append(norm_task(h, Qc, acc))

    for t in prep_tasks(0):
        t()
    if DBG_STOP == 12:
        return
    if DBG_DUMP:
        b.dump("CS", lambda: CS[:, :], S, [CS])
        b.dump("krfin", lambda: krfin[:, :], S, [krfin])
        b.dump("cqn0", lambda: cqn[0][:, :], S, cqn_T[0])
        b.dump("ckvn", lambda: ckvn[:, :], S, ckvn_T)
        b.dump("qh0", lambda: qh[0], S, qh_T[0])
        b.dump("kh0", lambda: kh[0], S, kh_T[0])
        b.dump("vx0", lambda: vxs[0].rearrange("p n c -> p (n c)"), NB * 192, vxs_T[0])
    for h in range(16):
        if DBG_STOP == 7 and h == 2:
            while normQ:
                normQ.pop(0)()
            while bgQ:
                bgQ.pop(0)()
            return
        if h < 15:
            prepQ.extend(prep_tasks(h + 1))
        for Qc in range(4):
            attend(h, Qc)
        while prepQ:
            prepQ.pop(0)()
        while normQ:
            normQ.pop(0)()
        if h % 2 == 1:
            bgQ.extend(outproj_tasks(h // 2))
        if h % 2 == 0 or h == 15:
            while bgQ:
                bgQ.pop(0)()
```

```python
import numpy as np
import concourse.bass as bass
import concourse.mybir as mybir
from concourse.bass_utils import run_bass_kernel_spmd

import os
DBG_STOP = int(os.environ.get("DBG_STOP", "0"))
DBG_DUMP = int(os.environ.get("DBG_DUMP", "0"))
F32 = mybir.dt.float32
BF16 = mybir.dt.bfloat16
I32 = mybir.dt.int32
ALU = mybir.AluOpType
AF = mybir.ActivationFunctionType
AX = mybir.AxisListType


class Tile:
    __slots__ = ("name", "t", "writer", "readers", "dma_sem", "psum")

    def __init__(self, name, t):
        self.psum = False
        self.name = name
        self.t = t
        self.writer = None
        self.readers = []
        self.dma_sem = None

    def __getitem__(self, k):
        return self.t[k]


class Ins:
    __slots__ = ("eng", "fn", "dma", "deps", "signals", "sem", "val", "idx", "dsem_tile")

    def __init__(self, eng, fn, dma):
        self.eng = eng
        self.fn = fn
        self.dma = dma
        self.deps = []
        self.signals = False
        self.sem = None
        self.val = None
        self.idx = None
        self.dsem_tile = None


ENGS = ("pe", "act", "dve", "pool", "sp")


class Prog:
    def __init__(self, nc):
        self.nc = nc
        self.streams = {e: [] for e in ENGS}
        self.n = 0
        self.store_tile = Tile("__store__", None)
        self.deferring = False
        self.deferred = []

    def tile(self, name, t):
        return Tile(name, t)

    def pop_deferred(self, n=1):
        for _ in range(n):
            if not self.deferred:
                return
            a = self.deferred.pop(0)
            self.op(*a)

    def op(self, eng, fn, reads=(), writes=(), dma=False, accum=False):
        if self.deferring:
            self.deferred.append((eng, fn, list(reads), list(writes), dma, accum))
            return None
        ins = Ins(eng, fn, dma)
        ins.idx = self.n
        self.n += 1
        deps = {}

        def add(d, kind):
            if d is None:
                return
            if (not d.dma) and (not dma) and d.eng == eng:
                if eng == "pe" or kind == "war":
                    return
            deps[d.idx] = d

        for t in reads:
            add(t.writer, "raw")
            if t.psum:
                for r in t.readers:
                    if r.eng != eng:
                        add(r, "rr")
        if not accum:
            for t in writes:
                add(t.writer, "waw")
                for r in t.readers:
                    add(r, "war")
        ins.deps = list(deps.values())
        for d in ins.deps:
            d.signals = True
        for t in reads:
            if not dma:
                t.readers = [r for r in t.readers if r.dma or r.eng != eng]
            t.readers.append(ins)
        if not accum:
            for t in writes:
                t.writer = ins
                t.readers = []
        else:
            for t in writes:
                t.writer = ins
        if dma:
            ins.dsem_tile = writes[0] if writes else self.store_tile
        self.streams[eng].append(ins)
        return ins

    def barrier(self, engs=("pe", "act", "dve", "pool")):
        toks = {}
        for e in engs:
            t = Tile("bar_" + e, None)
            last = None
            for i in reversed(self.streams[e]):
                if not i.dma:
                    last = i
                    break
            t.writer = last
            toks[e] = t
        for e in engs:
            self.op(e, (lambda en: en.nop()), reads=[toks[x] for x in engs if x != e and toks[x].writer is not None])

    def emit(self, final_wait_eng="sp"):
        nc = self.nc
        engobj = {"pe": nc.tensor, "act": nc.scalar, "dve": nc.vector, "pool": nc.gpsimd, "sp": nc.sync}
        SEM_EPOCH = 1024
        stack = []
        nsem = [0]

        def new_sem(tag):
            cm = nc.semaphore("%s_%d" % (tag, nsem[0]))
            nsem[0] += 1
            h = cm.__enter__()
            stack.append(cm)
            return h

        esem = {e: [] for e in ENGS}
        dsem = {}
        all_ins = sorted((i for e in ENGS for i in self.streams[e]), key=lambda i: i.idx)
        cnt = {e: 0 for e in ENGS}
        dcnt = {}
        for ins in all_ins:
            if ins.dma:
                t = ins.dsem_tile
                k = id(t)
                if k not in dsem:
                    dsem[k] = []
                    dcnt[k] = 0
                n = dcnt[k]
                dcnt[k] += 1
                ep = n // (SEM_EPOCH // 16)
                if ep >= len(dsem[k]):
                    dsem[k].append(new_sem("d"))
                ins.sem = dsem[k][ep]
                ins.val = (n % (SEM_EPOCH // 16) + 1) * 16
            elif ins.signals:
                e = ins.eng
                r = cnt[e]
                cnt[e] += 1
                ep = r // SEM_EPOCH
                if ep >= len(esem[e]):
                    esem[e].append(new_sem("s_" + e))
                ins.sem = esem[e][ep]
                ins.val = r % SEM_EPOCH + 1
        self.n_sems = nsem[0]
        self.sig_counts = dict(cnt)
        self.ins_counts = {e: len(self.streams[e]) for e in ENGS}
        if os.environ.get("PROG_STATS"):
            print("PROG_STATS sems", self.n_sems, "signals", cnt, "instrs", self.ins_counts)
        self.n_waits = 0
        store_waits = []
        ks = id(self.store_tile)
        if ks in dsem:
            n = dcnt[ks]
            per = SEM_EPOCH // 16
            for ep, sm in enumerate(dsem[ks]):
                last = min(n - ep * per, per)
                store_waits.append((sm, last * 16))

        def run_stream(e, eng):
            known = {}
            for ins in self.streams[e]:
                for d in ins.deps:
                    key = id(d.sem)
                    if known.get(key, 0) >= d.val:
                        continue
                    eng.wait_ge(d.sem, d.val)
                    known[key] = d.val
                    self.n_waits += 1
                bi = ins.fn(eng)
                if ins.dma:
                    bi.then_inc(ins.sem, 16)
                elif ins.signals:
                    bi.then_inc(ins.sem, 1)
            if e == final_wait_eng:
                for sm, v in store_waits:
                    eng.wait_ge(sm, v)

        with nc.Block() as block:
            @block.tensor
            def _(eng):
                run_stream("pe", eng)

            @block.scalar
            def _(eng):
                run_stream("act", eng)

            @block.vector
            def _(eng):
                run_stream("dve", eng)

            @block.gpsimd
            def _(eng):
                run_stream("pool", eng)

            @block.sync
            def _(eng):
                run_stream("sp", eng)
        for cm in reversed(stack):
            cm.__exit__(None, None, None)


D = 1024
S = 2048
NB = 16
DFF = 2816
NFC = 22
DEPTH = 2
EPS = 1e-6
TT = 512
NT = S // TT
FFN_GROUPS = ((0, 6), (6, 12), (12, 17), (17, 22))
WSLOT = 1024


def chunkT(W):
    K, F = W.shape
    return np.ascontiguousarray(
        W.reshape(K // 128, 128, F // 128, 128).transpose(2, 1, 0, 3)).reshape(F // 128, 128, K)


class Builder:
    def __init__(self, stages):
        self.stages = stages
        nc = bass.Bass("TRN2", target_bir_lowering=False)
        self.nc = nc
        self.P = Prog(nc)
        self.sb_off = self.SB_BASE
        self.ntile = 0
        self.wreq = []
        self.wtiles = {}
        self.wnext = 0
        self.cast_rr = 0
        self.fence_new = False
        self.norm_done = False

    SB_BASE = 16512
    SB_TOP = 229344

    def sb_raw(self, name, shape, dt):
        self.ntile += 1
        esz = {F32: 4, BF16: 2, I32: 4}[dt]
        nbytes = esz * int(np.prod(shape[1:]))
        nbytes = (nbytes + 31) // 32 * 32
        off = self.sb_off
        assert off + nbytes <= self.SB_TOP, "SBUF arena overflow at %s: need %d, have %d" % (name, nbytes, self.SB_TOP - off)
        self.sb_off += nbytes
        return self.nc.alloc_sbuf_tensor_at("%s_%d" % (name, self.ntile), list(shape), dt, offset=off)

    def sb(self, name, shape, dt):
        t = self.P.tile(name, self.sb_raw(name, shape, dt))
        if self.fence_new:
            t.readers = [st[-1] for st in (self.P.streams[e] for e in ("pe", "act", "dve", "pool")) if st and not st[-1].dma]
        return t

    def psum_banks(self):
        self.PS = [self.P.tile("ps%d" % i, self.nc.alloc_psum_tensor("ps%d" % i, [128, 512], F32))
                   for i in range(8)]
        for t in self.PS:
            t.psum = True

    def w_setup(self, nstage, nslots, depth):
        self.stg = [self.sb("stg", [128, WSLOT], F32) for _ in range(nstage)]
        self.wsl = [self.sb("wsl", [128, WSLOT], BF16) for _ in range(nslots)]
        self.stg_i = 0
        self.wsl_i = 0
        self.slot_owner = {}
        self.live = []
        self.wdepth = depth

    def w_declare(self, dram_ap, width):
        self.wreq.append((dram_ap, width))
        return len(self.wreq) - 1

    def w_issue_upto(self, idx):
        P = self.P
        idx = min(idx, len(self.wreq) - 1)
        while self.wnext <= idx:
            dram_ap, width = self.wreq[self.wnext]
            st = self.stg[self.stg_i % len(self.stg)]
            self.stg_i += 1
            sl = self.wsl[self.wsl_i % len(self.wsl)]
            self.wsl_i += 1
            P.op("sp", (lambda e, st=st, a=dram_ap, w=width: e.dma_start(out=st[:, 0:w], in_=a)),
                 writes=[st], dma=True)
            if self.cast_rr % 2 == 0:
                P.op("dve", (lambda e, st=st, sl=sl, w=width: e.tensor_copy(out=sl[:, 0:w], in_=st[:, 0:w])),
                     reads=[st], writes=[sl])
            else:
                P.op("act", (lambda e, st=st, sl=sl, w=width: e.activation(out=sl[:, 0:w], in_=st[:, 0:w], func=AF.Copy)),
                     reads=[st], writes=[sl])
            self.cast_rr += 1
            self.wtiles[self.wnext] = sl
            self.slot_owner[id(sl)] = self.wnext
            self.wnext += 1

    def w_get(self, idx):
        self.w_issue_upto(idx + self.wdepth)
        sl = self.wtiles[idx]
        assert self.slot_owner[id(sl)] == idx, "weight ring overrun"
        return sl

    def set_sqn(self, ia, ib):
        ta, ta_T = self.act[ia], self.act_T[ia]
        tb, tb_T = (self.otx, self.otx_T) if ib is None else (self.act[ib], self.act_T[ib])
        self.sqn_v = [ta[:, c * TT:(c + 1) * TT] for c in range(4)] + [tb[:, c * TT:(c + 1) * TT] for c in range(4)]
        self.sqn_T = [ta_T[c] for c in range(4)] + [tb_T[c] for c in range(4)]

    def norm_sq(self, T):
        P = self.P
        ts = slice(T * TT, (T + 1) * TT)
        for c in range(8):
            P.op("act", (lambda e, c=c, ts=ts, v=self.sqn_v[c]: e.activation(out=v, in_=self.xT_t[:, c, ts], func=AF.Square)),
                 reads=[self.xT[c][T]], writes=[self.sqn_T[c]])

    def norm_finish(self, T, gcol0):
        P = self.P
        ts = slice(T * TT, (T + 1) * TT)
        ss = self.PS[self.ps_rr % 8]
        self.ps_rr += 1
        for c in range(8):
            P.op("pe", (lambda e, ss=ss, c=c, v=self.sqn_v[c]: e.matmul(ss[:, :], lhsT=self.ones[:, :], rhs=v, start=(c == 0), stop=(c == 7))),
                 reads=[self.ones, self.sqn_T[c]], writes=[ss], accum=(c != 0))
        rs = self.rstd[T]
        P.op("act", (lambda e, rs=rs, ss=ss: e.activation(out=rs[:, :], in_=ss[:, :], func=AF.Ln, scale=1.0 / D, bias=self.epsc[:, 0:1])),
             reads=[ss, self.epsc], writes=[rs])
        P.op("act", (lambda e, rs=rs: e.activation(out=rs[:, :], in_=rs[:, :], func=AF.Exp, scale=-0.5)), reads=[rs], writes=[rs])
        for c in range(8):
            P.op("dve", (lambda e, c=c, ts=ts, rs=rs: e.scalar_tensor_tensor(
                out=self.hT_t[:, c, ts], in0=self.xT_t[:, c, ts], scalar=self.par[:, gcol0 + c:gcol0 + c + 1],
                in1=rs[:, :], op0=ALU.mult, op1=ALU.mult)),
                 reads=[self.xT[c][T], rs, self.par_t], writes=[self.hT[c][T]])

    def rmsnorm(self, gcol0):
        if self.norm_done:
            self.norm_done = False
            return
        P = self.P
        for T in range(NT):
            ts = slice(T * TT, (T + 1) * TT)
            ss = self.PS[self.ps_rr % 8]
            self.ps_rr += 1
            for c in range(8):
                sq = self.sq[self.sq_i % len(self.sq)]
                self.sq_i += 1
                x = self.xT[c][T]
                P.op("act", (lambda e, sq=sq, x=x, c=c, ts=ts: e.activation(out=sq[:, :], in_=self.xT_t[:, c, ts], func=AF.Square)),
                     reads=[x], writes=[sq])
                P.op("pe", (lambda e, ss=ss, sq=sq, c=c: e.matmul(ss[:, :], lhsT=self.ones[:, :], rhs=sq[:, :], start=(c == 0), stop=(c == 7))),
                     reads=[self.ones, sq], writes=[ss], accum=(c != 0))
            rs = self.rstd[T]
            P.op("act", (lambda e, rs=rs, ss=ss: e.activation(out=rs[:, :], in_=ss[:, :], func=AF.Ln, scale=1.0 / D, bias=self.epsc[:, 0:1])),
                 reads=[ss, self.epsc], writes=[rs])
            P.op("act", (lambda e, rs=rs: e.activation(out=rs[:, :], in_=rs[:, :], func=AF.Exp, scale=-0.5)),
                 reads=[rs], writes=[rs])
            for c in range(8):
                h = self.hT[c][T]
                x = self.xT[c][T]
                P.op("dve", (lambda e, c=c, ts=ts, rs=rs: e.scalar_tensor_tensor(
                    out=self.hT_t[:, c, ts], in0=self.xT_t[:, c, ts], scalar=self.par[:, gcol0 + c:gcol0 + c + 1],
                    in1=rs[:, :], op0=ALU.mult, op1=ALU.mult)),
                     reads=[x, rs, self.par_t], writes=[h])

    def ffn_declare(self, li):
        req = []
        for (f0, f1) in FFN_GROUPS:
            g = {"gu": [], "d": []}
            for fc in range(f0, f1):
                g["gu"].append((self.w_declare(self.ffnw[li, fc, 0], WSLOT), self.w_declare(self.ffnw[li, fc, 1], WSLOT)))
            for fc in range(f0, f1):
                g["d"].append(self.w_declare(self.ffnw[li, fc, 2], WSLOT))
            req.append(g)
        return req

    def ffn_run(self, req, next_gcol=None):
        P = self.P
        for gi, (f0, f1) in enumerate(FFN_GROUPS):
            n = f1 - f0
            for k in range(n):
                ig, iu = req[gi]["gu"][k]
                wg = self.w_get(ig)
                wu = self.w_get(iu)
                for T in range(NT):
                    ts = slice(T * TT, (T + 1) * TT)
                    gp = self.PS[self.ps_rr % 8]
                    up = self.PS[(self.ps_rr + 1) % 8]
                    self.ps_rr += 2
                    for (pt, w) in ((gp, wg), (up, wu)):
                        for c in range(8):
                            P.op("pe", (lambda e, pt=pt, w=w, c=c, ts=ts: e.matmul(
                                pt[:, :], lhsT=w[:, c * 128:(c + 1) * 128], rhs=self.hT_t[:, c, ts],
                                start=(c == 0), stop=(c == 7))),
                                 reads=[w, self.hT[c][T]], writes=[pt], accum=(c != 0))
                    sg = self.sg[self.sg_i % len(self.sg)]
                    self.sg_i += 1
                    P.op("act", (lambda e, sg=sg, gp=gp: e.activation(out=sg[:, :], in_=gp[:, :], func=AF.Silu)),
                         reads=[gp], writes=[sg])
                    a = self.act[k]
                    P.op("dve", (lambda e, a=a, sg=sg, up=up, ts=ts: e.tensor_tensor(
                        out=a[:, ts], in0=up[:, :], in1=sg[:, :], op=ALU.mult)),
                         reads=[sg, up], writes=[self.act_T[k][T]])
                    P.pop_deferred(1)
            wds = [self.w_get(i) for i in req[gi]["d"]]
            fuse = (next_gcol is not None) and gi == len(FFN_GROUPS) - 1
            if fuse:
                self.set_sqn(5, None)
            order = [(dc, T) for T in range(NT) for dc in range(8)] if fuse else [(dc, T) for dc in range(8) for T in range(NT)]
            for (dc, T) in order:
                if True:
                    ts = slice(T * TT, (T + 1) * TT)
                    op_ = self.PS[self.ps_rr % 8]
                    self.ps_rr += 1
                    for k in range(n):
                        P.op("pe", (lambda e, op_=op_, w=wds[k], k=k, dc=dc, ts=ts, n=n: e.matmul(
                            op_[:, :], lhsT=w[:, dc * 128:(dc + 1) * 128], rhs=self.act[k][:, ts],
                            start=(k == 0), stop=(k == n - 1))),
                             reads=[wds[k], self.act_T[k][T]], writes=[op_], accum=(k != 0))
                    x = self.xT[dc][T]
                    P.op("dve", (lambda e, op_=op_, dc=dc, ts=ts: e.scalar_tensor_tensor(
                        out=self.xT_t[:, dc, ts], in0=op_[:, :], scalar=0.5, in1=self.xT_t[:, dc, ts],
                        op0=ALU.mult, op1=ALU.add)),
                         reads=[op_, x], writes=[x])
                    if fuse and dc == 7:
                        if T > 0:
                            self.norm_finish(T - 1, next_gcol)
                        self.norm_sq(T)
            if fuse:
                self.norm_finish(NT - 1, next_gcol)
                self.norm_done = True

    def dump(self, name, get_ap, cols, reads, rows=128):
        nc, P = self.nc, self.P
        d = nc.dram_tensor("dbg_" + name, [rows, cols], F32, kind="ExternalOutput").ap()
        if not hasattr(self, "dbg_t"):
            save = self.sb_off
            assert self.sb_off <= self.SB_TOP - 12288 - 64, "no room for debug tile"
            self.sb_off = self.SB_TOP - 12288 - 64
            self.dbg_t = self.sb("dbg_t", [128, 3072], F32)
            self.sb_off = save
        t = self.dbg_t
        P.op("dve", lambda e: e.tensor_copy(out=t[0:rows, 0:cols], in_=get_ap()), reads=list(reads), writes=[t])
        P.op("sp", lambda e: e.dma_start(out=d[:, :], in_=t[0:rows, 0:cols]), reads=[t], dma=True)
        self.dbg_names.append("dbg_" + name)

    def setup(self):
        self.dbg_names = []
        nc, P = self.nc, self.P
        self.xT_d = nc.dram_tensor("xT", [D, S], F32, kind="ExternalInput").ap()
        self.par_d = nc.dram_tensor("par", [128, NPAR], F32, kind="ExternalInput").ap()
        if set(k for k, _ in self.stages) & {"ffn1", "ffn2"}:
            self.ffnw = nc.dram_tensor("ffnw", [DEPTH * 2, NFC, 3, 128, WSLOT], F32, kind="ExternalInput").ap()
        self.out_d = nc.dram_tensor("outT", [D, S], F32, kind="ExternalOutput").ap()
        self.psum_banks()
        self.ps_rr = 0
        self.xT_t = self.sb_raw("xTs", [128, 8, S], F32)
        self.xT = [[P.tile("x%d_%d" % (c, T), None) for T in range(NT)] for c in range(8)]
        self.hT_t = self.sb_raw("hTs", [128, 8, S], BF16)
        self.hT = [[P.tile("h%d_%d" % (c, T), None) for T in range(NT)] for c in range(8)]
        self.par_t = self.sb("par", [128, NPAR], F32)
        self.par = self.par_t
        self.ones = self.sb("ones", [128, 128], BF16)
        self.epsc = self.sb("epsc", [128, 1], F32)
        self.rstd = [self.sb("rstd", [128, TT], F32) for _ in range(NT)]
        self.sq = [self.sb("sq", [128, TT], BF16) for _ in range(2)]
        self.sq_i = 0
        self.sg = [self.sb("sg", [128, TT], F32) for _ in range(2)]
        self.sg_i = 0
        nact = 6
        self.act = [self.sb("act", [128, S], BF16) for _ in range(nact)]
        self.act_T = [[P.tile("a%d_%d" % (k, T), None) for T in range(NT)] for k in range(nact)]
        self.otx = self.sb("otx", [128, S], BF16)
        self.otx_T = [P.tile("otx", None) for _ in range(NT)]
        self.otx2 = [self.sb("otx2", [128, S], BF16) for _ in range(2)]
        self.set_sqn(5, None)
        self.otx2_T = [[P.tile("otx2", None) for _ in range(NT)] for _ in range(2)]
        self.w_setup(nstage=2, nslots=10, depth=4)
        self.mix_mark = None
        P.op("dve", lambda e: e.memset(self.ones[:, :], 1.0), writes=[self.ones])
        P.op("dve", lambda e: e.memset(self.epsc[:, :], EPS), writes=[self.epsc])
        P.op("sp", lambda e: e.dma_start(out=self.par_t[:, :], in_=self.par_d[:, :]), writes=[self.par_t], dma=True)
        for T in range(NT):
            for c in range(8):
                ts = slice(T * TT, (T + 1) * TT)
                P.op("sp", (lambda e, c=c, ts=ts: e.dma_start(out=self.xT_t[:, c, ts], in_=self.xT_d[c * 128:(c + 1) * 128, ts])),
                     writes=[self.xT[c][T]], dma=True)

    def finish(self):
        P = self.P
        for c in range(8):
            P.op("sp", (lambda e, c=c: e.dma_start(out=self.out_d[c * 128:(c + 1) * 128, :], in_=self.xT_t[:, c, :])),
                 reads=[self.xT[c][T] for T in range(NT)], dma=True)
        P.emit()


PC_FN1 = 0
PC_MIX = 16
PC_FN2 = 32
NPAR = 128


def build_program(stages):
    b = Builder(stages)
    b.setup()
    kinds = set(k for k, _ in stages)
    if "mixA" in kinds:
        mixA_setup(b)
        mixA_tables(b)
    if "mixB" in kinds:
        mixB_setup(b)
    reqs = {}
    for st in stages:
        kind, l = st
        if kind == "ffn1":
            reqs[st] = b.ffn_declare(l * 2 + 0)
        elif kind == "ffn2":
            reqs[st] = b.ffn_declare(l * 2 + 1)
        elif kind == "mixA":
            reqs[st] = mixA_declare(b)
        elif kind == "mixB":
            reqs[st] = mixB_declare(b)
    pending_consts = False

    def norm_col(st2):
        k2, l2 = st2
        return {"ffn1": PC_FN1, "ffn2": PC_FN2, "mixA": PC_MIX, "mixB": PC_MIX}[k2] + 8 * l2

    for si, st in enumerate(stages):
        kind, l = st
        nxt_col = norm_col(stages[si + 1]) if (si + 1 < len(stages) and FUSE_NORM) else None
        if os.environ.get("STAGE_BARRIER") == "1":
            b.P.barrier()
        if kind in ("ffn1", "ffn2") and pending_consts:
            b.rmsnorm((PC_FN1 if kind == "ffn1" else PC_FN2) + 8 * l)
            b.P.deferring = True
            mixB_consts(b)
            b.P.deferring = False
            pending_consts = False
            b.ffn_run(reqs[st], nxt_col)
            b.P.pop_deferred(10 ** 6)
            continue
        if kind == "ffn1":
            b.rmsnorm(PC_FN1 + 8 * l)
            b.ffn_run(reqs[st], nxt_col)
        elif kind == "ffn2":
            b.rmsnorm(PC_FN2 + 8 * l)
            b.ffn_run(reqs[st], nxt_col)
        elif kind == "mixA":
            mixA_run(b, reqs[st], l, nxt_col)
            pending_consts = "mixB" in kinds
        elif kind == "mixB":
            mixB_run(b, reqs[st], l, nxt_col)
    b.finish()
    return b


def host_prep(inputs):
    f = lambda k: np.asarray(inputs[k], dtype=np.float32)
    par = np.zeros((128, NPAR), np.float32)
    for l in range(DEPTH):
        par[:, PC_FN1 + 8 * l:PC_FN1 + 8 * l + 8] = f("ffn_norm1")[l].reshape(8, 128).T
        par[:, PC_MIX + 8 * l:PC_MIX + 8 * l + 8] = f("mix_norm")[l].reshape(8, 128).T
        par[:, PC_FN2 + 8 * l:PC_FN2 + 8 * l + 8] = f("ffn_norm2")[l].reshape(8, 128).T
    ffnw = np.empty((DEPTH * 2, NFC, 3, 128, WSLOT), np.float32)
    for l in range(DEPTH):
        for i, pre in enumerate(("ffn1", "ffn2")):
            ffnw[l * 2 + i, :, 0] = chunkT(f(pre + "_wg")[l])
            ffnw[l * 2 + i, :, 1] = chunkT(f(pre + "_wu")[l])
            ffnw[l * 2 + i, :, 2] = f(pre + "_wd")[l].reshape(NFC, 128, D)
    shared = {"par": par, "ffnw": ffnw}
    gq = f("a_q_gain")[0]; gk = f("a_k_gain")[0]
    par[:, PC_AQG] = np.tile(gq, 2)
    par[:, PC_AKG] = np.tile(gk, 2)
    par[:, PC_ASINK:PC_ASINK + 16] = f("a_sinks")[0][None, :]
    win = f("a_w_in")[0]
    aw = np.empty((19, 128, WSLOT), np.float32)
    aw[A_SLOT_Q:A_SLOT_Q + 8] = chunkT(win[:, 0:1024])
    for c in range(2):
        kc = win[:, 1024 + c * 64:1024 + (c + 1) * 64]
        aw[A_SLOT_K + c] = chunkT(np.concatenate([kc, kc], axis=1))[0]
    aw[A_SLOT_V] = chunkT(win[:, 1152:1280])[0]
    aw[A_SLOT_O:A_SLOT_O + 8] = f("a_w_out")[0].reshape(8, 128, D)
    shared["aw"] = aw
    shared["rbT"] = np.ascontiguousarray(f("rel_bias").T)
    bwin = f("b_w_in")[0]
    bw = np.zeros((12, 128, WSLOT), np.float32)
    bw[B_SLOT_C:B_SLOT_C + 3] = chunkT(bwin[:, 0:384])
    krc = bwin[:, 384:416]
    kr64 = np.concatenate([krc, krc[:, 16:32], krc[:, 0:16]], axis=1)
    kr128 = np.concatenate([kr64, kr64], axis=1)
    bw[B_SLOT_KR] = kr128.reshape(8, 128, 128).transpose(1, 0, 2).reshape(128, 1024)
    bw[B_SLOT_O:B_SLOT_O + 8] = f("b_w_out")[0].reshape(8, 128, D)
    shared["bw"] = bw
    wuq = f("b_w_uq")[0].reshape(256, 16, 96)
    wukv = f("b_w_ukv")[0].reshape(128, 16, 128)
    bwh = np.empty((16, 128, 384), np.float32)
    for h in range(16):
        q128 = np.concatenate([wuq[:, h, 64:96], wuq[:, h, 80:96], wuq[:, h, 64:80], wuq[:, h, 0:64]], axis=1)
        bwh[h, :, 0:128] = q128[0:128]
        bwh[h, :, 128:256] = q128[128:256]
        bwh[h, :, 256:384] = wukv[:, h, :]
    shared["bwh"] = bwh
    par[:, PC_BQN:PC_BQN + 2] = f("b_q_norm")[0].reshape(2, 128).T
    par[:, PC_BKVN] = f("b_kv_norm")[0]
    gq = f("b_q_gain")[0]; gk = f("b_k_gain")[0]
    par[0:32, PC_BQG] = gq[64:96]
    par[32:48, PC_BQG] = gq[80:96]
    par[48:64, PC_BQG] = gq[64:80]
    par[64:128, PC_BQG] = gq[0:64]
    par[64:128, PC_BKG] = gk[0:64]
    par[0:32, PC_BKRG] = gk[64:96]
    par[32:48, PC_BKRG] = gk[80:96]
    par[48:64, PC_BKRG] = gk[64:80]
    inv_freq = (np.float32(10000.0) ** (-np.arange(0, 32, 2, dtype=np.float32) / np.float32(32))).astype(np.float32)
    cstt = np.zeros((128, 4), np.float32)
    for base in (0, 64):
        for i in range(32):
            cstt[base + i, CST_F] = inv_freq[i % 16]
            cstt[base + i, CST_PH] = np.float32(np.pi / 2)
            cstt[base + 32 + i, CST_F] = -inv_freq[i % 16] if i < 16 else inv_freq[i % 16]
            cstt[base + 32 + i, CST_PH] = 0.0
    shared["cst"] = cstt
    x = f("x")
    pos = np.asarray(inputs["positions"]).astype(np.int32)
    percore = [{"xT": np.ascontiguousarray(x[b].T), "pos": pos[b][None, :]} for b in range(x.shape[0])]
    return shared, percore


ALL_STAGES = (("ffn1", 0), ("mixA", 0), ("ffn2", 0), ("ffn1", 1), ("mixB", 1), ("ffn2", 1))


def run(inputs, stages=ALL_STAGES, ncores=8, trace=False):
    shared, percore = host_prep(inputs)
    b = build_program(stages)
    names = set(["xT", "par"])
    kinds = set(k for k, _ in stages)
    if kinds & {"ffn1", "ffn2"}:
        names |= {"ffnw"}
    if "mixA" in kinds:
        names |= {"aw", "rbT", "pos"}
    if "mixB" in kinds:
        names |= {"bw", "bwh", "pos", "cst"}
    in_maps = [{k: v for k, v in dict(shared, **percore[i]).items() if k in names} for i in range(ncores)]
    res = run_bass_kernel_spmd(b.nc, in_maps, core_ids=list(range(ncores)), trace=trace)
    out = np.stack([np.ascontiguousarray(r["outT"].T) for r in res.results], axis=0)
    res.dbg = {k: res.results[0][k] for k in b.dbg_names}
    return out, res


def kernel(**inputs):
    out, _ = run(inputs)
    return out.astype(np.float32)


NEGB = -30000.0
T5_THR = [float(j) for j in range(1, 17)] + [float(int(np.ceil(16.0 * 8.0 ** (j / 16.0) - 1e-9))) for j in range(1, 16)]
A_SLOT_Q, A_SLOT_K, A_SLOT_V, A_SLOT_O = 0, 8, 10, 11
PC_AQG, PC_AKG, PC_ASINK = 48, 49, 50


def mixA_setup(b):
    nc, P = b.nc, b.P
    b.aw = nc.dram_tensor("aw", [19, 128, WSLOT], F32, kind="ExternalInput").ap()
    b.pos_d = nc.dram_tensor("pos", [1, S], I32, kind="ExternalInput").ap()
    b.rbT_d = nc.dram_tensor("rbT", [16, 32], F32, kind="ExternalInput").ap()
    b.gscr = nc.dram_tensor("gscr", [16, 128, 384], F32, kind="Internal")
    b.blk64 = b.sb("blk64", [128, 128], BF16)
    P.op("dve", lambda e: e.memset(b.blk64[:, :], 0.0), writes=[b.blk64])
    P.op("dve", lambda e: e.memset(b.blk64[0:64, 0:64], 1.0), writes=[b.blk64])
    P.op("dve", lambda e: e.memset(b.blk64[64:128, 64:128], 1.0), writes=[b.blk64])
    b.esink = b.sb("esink", [128, 16], F32)
    P.op("act", lambda e: e.activation(out=b.esink[:, :], in_=b.par[:, PC_ASINK:PC_ASINK + 16], func=AF.Exp),
         reads=[b.par_t], writes=[b.esink])


def mixA_declare(b):
    r = {}
    r["k"] = [b.w_declare(b.aw[A_SLOT_K + c], WSLOT) for c in range(2)]
    r["v"] = b.w_declare(b.aw[A_SLOT_V], WSLOT)
    r["q"], r["o"] = [None] * 8, [None] * 8
    for kind, i in (("q", 0), ("q", 1), ("q", 2), ("q", 3), ("q", 4), ("o", 0), ("o", 1), ("o", 2), ("o", 3),
                    ("q", 5), ("q", 6), ("q", 7), ("o", 4), ("o", 5), ("o", 6), ("o", 7)):
        r[kind][i] = b.w_declare(b.aw[(A_SLOT_Q if kind == "q" else A_SLOT_O) + i], WSLOT)
    order = r["k"] + [r["v"]]
    return r


def proj_headnorm(b, w, gcol, out_t, out_tiles, nrm_lhsT, hd, sq_pool, tmp_pool):
    P = b.P
    for T in range(NT):
        ts = slice(T * TT, (T + 1) * TT)
        raw = b.PS[b.ps_rr % 8]
        ss = b.PS[(b.ps_rr + 1) % 8]
        b.ps_rr += 2
        for c in range(8):
            P.op("pe", (lambda e, raw=raw, c=c, ts=ts: e.matmul(raw[:, :], lhsT=w[:, c * 128:(c + 1) * 128], rhs=b.hT_t[:, c, ts],
                                                               start=(c == 0), stop=(c == 7))),
                 reads=[w, b.hT[c][T]], writes=[raw], accum=(c != 0))
        sq = b.sq[b.sq_i % len(b.sq)]
        b.sq_i += 1
        P.op("act", (lambda e, sq=sq, raw=raw: e.activation(out=sq[:, :], in_=raw[:, :], func=AF.Square)), reads=[raw], writes=[sq])
        P.op("pe", (lambda e, ss=ss, sq=sq: e.matmul(ss[:, :], lhsT=nrm_lhsT[:, :], rhs=sq[:, :], start=True, stop=True)),
             reads=[nrm_lhsT, sq], writes=[ss])
        rs = b.sg[b.sg_i % len(b.sg)]
        b.sg_i += 1
        P.op("act", (lambda e, rs=rs, ss=ss: e.activation(out=rs[:, :], in_=ss[:, :], func=AF.Ln, scale=1.0 / hd, bias=b.epsc[:, 0:1])),
             reads=[ss, b.epsc], writes=[rs])
        P.op("act", (lambda e, rs=rs: e.activation(out=rs[:, :], in_=rs[:, :], func=AF.Exp, scale=-0.5)), reads=[rs], writes=[rs])
        P.op("dve", (lambda e, raw=raw, rs=rs, ts=ts: e.scalar_tensor_tensor(
            out=out_t[:, ts], in0=raw[:, :], scalar=b.par[:, gcol:gcol + 1], in1=rs[:, :], op0=ALU.mult, op1=ALU.mult)),
             reads=[raw, rs, b.par_t], writes=[out_tiles[T]])


def mixA_tables(b):
    nc, P = b.nc, b.P
    if not hasattr(b, "a_alloc"):
        b.a_alloc = True
        if b.mix_mark is None:
            b.mix_mark = b.sb_off
        b.sb_off = b.mix_mark
        b.fence_new = True
        b.kdup = b.act[0:2]
        b.kdup_T = b.act_T[0:2]
        b.qn = b.act[2:4]
        b.qn_T = b.act_T[2:4]
        b.OTp = [b.act[5], b.otx, b.otx2[0], b.otx2[1]]
        b.OTp_T = [b.act_T[5], b.otx_T, b.otx2_T[0], b.otx2_T[1]]
        b.pt = [(b.act[4][:, i * 512:(i + 1) * 512], b.act_T[4][i]) for i in range(4)]
        b.vx = [b.sb("vx", [128, NB, 192], BF16) for _ in range(2)]
        b.biasm = b.sb("biasm", [128, 8, 4, 128], F32)
        b.sc = b.rstd[0:2]
        b.dn = b.rstd[2:4]
        b.sc_i = b.pt_i = b.dn_i = 0
        posr_i = b.sb("posr_i", [16, 128], I32)
        posr = b.sb("posr", [16, 128], F32)
        dist = b.sb("dist", [16, 128], F32)
        rbT = b.sb("rbT", [16, 32], F32)
        dif = b.sb("dif", [16, 32], F32)
        acc = b.sb("bacc", [16, 128], F32)
        tmp = b.sb("btmp", [16, 128], F32)
        G = b.sb("G", [16, 384], F32)
        P.op("sp", lambda e: e.dma_start(out=posr_i[:, :], in_=b.pos_d[0:1, 0:128].partition_broadcast(16)), writes=[posr_i], dma=True)
        P.op("sp", lambda e: e.dma_start(out=rbT[:, :], in_=b.rbT_d[:, :]), writes=[rbT], dma=True)
        P.op("dve", lambda e: e.tensor_copy(out=posr[:, :], in_=posr_i[:, :]), reads=[posr_i], writes=[posr])
        P.op("dve", lambda e: e.tensor_scalar(out=dist[:, :], in0=posr[:, :], scalar1=posr[:, 0:1], scalar2=None, op0=ALU.subtract),
             reads=[posr], writes=[dist])
        P.op("dve", lambda e: e.tensor_tensor(out=dif[:, 1:32], in0=rbT[:, 1:32], in1=rbT[:, 0:31], op=ALU.subtract), reads=[rbT], writes=[dif])
        P.op("dve", lambda e: e.tensor_scalar(out=acc[:, :], in0=dist[:, :], scalar1=0.0, scalar2=rbT[:, 0:1], op0=ALU.mult, op1=ALU.add),
             reads=[dist, rbT], writes=[acc])
        for j in range(1, 32):
            P.op("dve", (lambda e, j=j: e.tensor_scalar(out=tmp[:, :], in0=dist[:, :], scalar1=T5_THR[j - 1], scalar2=dif[:, j:j + 1],
                                                        op0=ALU.is_ge, op1=ALU.mult)), reads=[dist, dif], writes=[tmp])
            P.op("dve", lambda e: e.tensor_tensor(out=acc[:, :], in0=acc[:, :], in1=tmp[:, :], op=ALU.add), reads=[acc, tmp], writes=[acc])
        P.op("dve", lambda e: e.memset(G[:, :], NEGB), writes=[G])
        P.op("dve", lambda e: e.tensor_copy(out=G[:, 127:255], in_=acc[:, :]), reads=[acc], writes=[G])
        gt = P.tile("gscr", None)
        P.op("sp", lambda e: e.dma_start(out=b.gscr.ap()[:, :, :], in_=G[:, :].unsqueeze(1).broadcast_to([16, 128, 384])), reads=[G], writes=[gt], dma=True)
        for h in range(16):
            for kt in range(2):
                off = h * 128 * 384 + 127 + (128 if kt == 0 else 0)
                src = bass.AP(tensor=b.gscr, offset=off, ap=[[383, 128], [1, 128]])
                P.op("sp", (lambda e, h=h, kt=kt, src=src: e.dma_start(out=b.biasm[:, h // 2, (h % 2) * 2 + kt, :], in_=src)),
                     reads=[gt], writes=[b.biasm], dma=True)
        for c in range(2):
            P.op("pool", (lambda e, c=c: e.memset(b.vx[c][:, :, :], 1.0)), writes=[b.vx[c]])
        b.fence_new = False


def mixA_run(b, r, l, next_gcol=None):
    nc, P = b.nc, b.P
    if DBG_STOP == 1:
        return
    b.rmsnorm(PC_MIX + 8 * l)
    for c in range(2):
        proj_headnorm(b, b.w_get(r["k"][c]), PC_AKG, b.kdup[c], b.kdup_T[c], b.blk64, 64, None, None)
    wv = b.w_get(r["v"])
    for g4 in range(4):
        vp = b.PS[b.ps_rr % 8]
        b.ps_rr += 1
        for j in range(4):
            n = g4 * 4 + j
            for c in range(8):
                P.op("pe", (lambda e, vp=vp, j=j, n=n, c=c: e.matmul(vp[:, j * 128:(j + 1) * 128], lhsT=b.hT_t[:, c, n * 128:(n + 1) * 128],
                                                                      rhs=wv[:, c * 128:(c + 1) * 128], start=(c == 0), stop=(c == 7))),
                     reads=[wv, b.hT[c][n // 4]], writes=[vp], accum=not (c == 0 and j == 0))
        for c2 in range(2):
            P.op("act", (lambda e, vp=vp, g4=g4, c2=c2: e.activation(
                out=b.vx[c2][:, g4 * 4:(g4 + 1) * 4, 64:128],
                in_=vp[:, :].rearrange("p (j c d) -> p j c d", j=4, c=2)[:, :, c2, :], func=AF.Copy)),
                 reads=[vp], writes=[b.vx[c2]])
    if DBG_STOP == 2 or DBG_DUMP:
        b.dump("kdup0", lambda: b.kdup[0][:, :], S, b.kdup_T[0])
        b.dump("vx0", lambda: b.vx[0][:, :, :].rearrange("p n c -> p (n c)"), NB * 192, [b.vx[0]])
    if DBG_STOP == 2:
        return
    PACC = b.PS[0:2]
    PST4 = b.PS[2:6]
    PSR = b.PS[6:8]
    psr_i = [0]

    def psr():
        t = PSR[psr_i[0] % 2]
        psr_i[0] += 1
        return t
    prepQ, bgQ = [], []
    st_i = [0]

    def qproj_tasks(oc):
        qn, qT = b.qn[oc % 2], b.qn_T[oc % 2]
        ctx = {}
        tasks = []
        for T in range(NT):
            def t(T=T):
                if "w" not in ctx:
                    ctx["w"] = b.w_get(r["q"][oc])
                w = ctx["w"]
                ts = slice(T * TT, (T + 1) * TT)
                raw, ss = psr(), psr()
                for c in range(8):
                    P.op("pe", (lambda e, c=c: e.matmul(raw[:, :], lhsT=w[:, c * 128:(c + 1) * 128], rhs=b.hT_t[:, c, ts], start=(c == 0), stop=(c == 7))),
                         reads=[w, b.hT[c][T]], writes=[raw], accum=(c != 0))
                sq = b.sq[b.sq_i % len(b.sq)]
                b.sq_i += 1
                P.op("act", (lambda e: e.activation(out=sq[:, :], in_=raw[:, :], func=AF.Square)), reads=[raw], writes=[sq])
                P.op("pe", (lambda e: e.matmul(ss[:, :], lhsT=b.blk64[:, :], rhs=sq[:, :], start=True, stop=True)), reads=[b.blk64, sq], writes=[ss])
                rs = b.sg[b.sg_i % len(b.sg)]
                b.sg_i += 1
                P.op("act", (lambda e: e.activation(out=rs[:, :], in_=ss[:, :], func=AF.Ln, scale=1.0 / 64, bias=b.epsc[:, 0:1])),
                     reads=[ss, b.epsc], writes=[rs])
                P.op("act", (lambda e: e.activation(out=rs[:, :], in_=rs[:, :], func=AF.Exp, scale=-0.5)), reads=[rs], writes=[rs])
                P.op("dve", (lambda e: e.scalar_tensor_tensor(out=qn[:, ts], in0=raw[:, :], scalar=b.par[:, PC_AQG:PC_AQG + 1], in1=rs[:, :],
                                                              op0=ALU.mult, op1=ALU.mult)), reads=[raw, rs, b.par_t], writes=[qT[T]])
            tasks.append(t)
        return tasks

    def outproj_group_tasks(g):
        ctx = {}
        tasks = []
        fuse = (next_gcol is not None) and g == 1
        order = [(dc, T) for T in range(NT) for dc in range(8)] if fuse else [(dc, T) for dc in range(8) for T in range(NT)]
        if fuse:
            b.set_sqn(0, 1)
        for (dc, T) in order:
            if True:
                def t(dc=dc, T=T):
                    if "wo" not in ctx:
                        ctx["wo"] = [b.w_get(r["o"][4 * g + i]) for i in range(4)]
                    wos = ctx["wo"]
                    ts = slice(T * TT, (T + 1) * TT)
                    op_ = psr()
                    for i in range(4):
                        P.op("pe", (lambda e, i=i: e.matmul(op_[:, :], lhsT=wos[i][:, dc * 128:(dc + 1) * 128], rhs=b.OTp[i][:, ts],
                                                            start=(i == 0), stop=(i == 3))),
                             reads=[wos[i], b.OTp_T[i][T]], writes=[op_], accum=(i != 0))
                    x = b.xT[dc][T]
                    P.op("dve", (lambda e: e.tensor_tensor(out=b.xT_t[:, dc, ts], in0=op_[:, :], in1=b.xT_t[:, dc, ts], op=ALU.add)),
                         reads=[op_, x], writes=[x])
                    if fuse and dc == 7:
                        if T > 0:
                            b.norm_finish(T - 1, next_gcol)
                        b.norm_sq(T)
                        if T == NT - 1:
                            b.norm_finish(T, next_gcol)
                            b.norm_done = True
                tasks.append(t)
        return tasks

    def pop_tasks(n_bg=2):
        if prepQ:
            prepQ.pop(0)()
        for _ in range(n_bg):
            if bgQ:
                bgQ.pop(0)()

    def scores(oc, step):
        c = oc // 4
        qn, qT = b.qn[oc % 2], b.qn_T[oc % 2]
        sts = (PST4[(st_i[0] * 2) % 4], PST4[(st_i[0] * 2 + 1) % 4])
        st_i[0] += 1
        outs = []
        for hh in range(2):
            rows = slice(hh * 64, (hh + 1) * 64)
            st = sts[hh]
            first = True
            for jj in range(2):
                n = step * 2 + jj
                qs = slice(n * 128, (n + 1) * 128)
                for kt in range(2):
                    if n == 0 and kt == 0:
                        continue
                    kb = n - 1 + kt
                    sl = (jj * 2 + kt) * 128
                    P.op("pe", (lambda e, st=st, rows=rows, kb=kb, sl=sl, qs=qs: e.matmul(
                        st[:, sl:sl + 128], lhsT=b.kdup[c][rows, kb * 128:(kb + 1) * 128], rhs=qn[rows, qs], start=True, stop=True)),
                         reads=[b.kdup_T[c][kb // 4], qT[n // 4]], writes=[st], accum=not first)
                    first = False
        for hh in range(2):
            st = sts[hh]
            sc = b.sc[b.sc_i % 2]
            b.sc_i += 1
            pt, ptT = b.pt[b.pt_i % len(b.pt)]
            b.pt_i += 1
            outs.append((pt, ptT))
            P.op("dve", (lambda e, sc=sc, st=st, hh=hh: e.scalar_tensor_tensor(
                out=sc[:, :].rearrange("p (j a) -> p j a", j=2), in0=st[:, :].rearrange("p (j a) -> p j a", j=2), scalar=0.125,
                in1=b.biasm[:, oc, hh * 2:hh * 2 + 2, :].rearrange("p a q -> p (a q)").unsqueeze(1).broadcast_to([128, 2, 256]),
                op0=ALU.mult, op1=ALU.add)), reads=[st, b.biasm], writes=[sc])
            P.op("act", (lambda e, sc=sc, pt=pt: e.activation(out=pt, in_=sc[:, :], func=AF.Exp)), reads=[sc], writes=[ptT])
        return outs

    def pv(oc, step, outs):
        c = oc // 4
        for hh, vcols in ((0, slice(64, 192)), (1, slice(0, 128))):
            acc_ = PACC[hh]
            pt, ptT = outs[hh]
            for jj in range(2):
                n = step * 2 + jj
                j = n % 4
                for kt in range(2):
                    if n == 0 and kt == 0:
                        continue
                    kb = n - 1 + kt
                    sl = (jj * 2 + kt) * 128
                    P.op("pe", (lambda e, acc_=acc_, kb=kb, sl=sl, j=j, kt=kt, vcols=vcols, pt=pt, n=n: e.matmul(
                        acc_[:, j * 128:(j + 1) * 128], lhsT=b.vx[c][:, kb, vcols], rhs=pt[:, sl:sl + 128],
                        start=(kt == 0 or n == 0), stop=(kt == 1))),
                         reads=[b.vx[c], ptT], writes=[acc_], accum=not (j == 0 and (kt == 0 or n == 0)))

    def normalise(oc, bg):
        ts = slice(bg * 512, (bg + 1) * 512)
        for hh in range(2):
            acc_ = PACC[hh]
            h = 2 * oc + hh
            orow = slice(hh * 64, (hh + 1) * 64)
            drow = slice((1 - hh) * 64, (2 - hh) * 64)
            dn = b.dn[b.dn_i % 2]
            b.dn_i += 1
            P.op("act", (lambda e, dn=dn, acc_=acc_, drow=drow, h=h: e.activation(out=dn[drow, :], in_=acc_[drow, :], func=AF.Ln,
                                                                               bias=b.esink[drow, h:h + 1], scale=1.0)),
                 reads=[acc_, b.esink], writes=[dn])
            P.op("act", (lambda e, dn=dn, drow=drow: e.activation(out=dn[drow, :], in_=dn[drow, :], func=AF.Exp, scale=-1.0)), reads=[dn], writes=[dn])
            P.op("dve", (lambda e, dn=dn, acc_=acc_, drow=drow, orow=orow: e.tensor_tensor(
                out=b.OTp[oc % 4][orow, ts], in0=acc_[orow, :], in1=dn[drow, :], op=ALU.mult)),
                 reads=[acc_, dn], writes=[b.OTp_T[oc % 4][bg]], accum=(hh == 1))

    for t in qproj_tasks(0):
        t()
    for oc in range(8):
        if (DBG_STOP == 3 or DBG_DUMP) and oc == 1:
            b.dump("qn0", lambda: b.qn[0][:, :], S, b.qn_T[0])
            b.dump("ot0", lambda: b.OTp[0][:, :], S, b.OTp_T[0])
        if DBG_STOP == 3 and oc == 1:
            return
        if oc < 7:
            prepQ.extend(qproj_tasks(oc + 1))
        nxt = scores(oc, 0)
        for step in range(8):
            cur = nxt
            if step < 7:
                nxt = scores(oc, step + 1)
            pv(oc, step, cur)
            if step % 2 == 1:
                normalise(oc, step // 2)
            pop_tasks()
        while prepQ:
            prepQ.pop(0)()
        if oc % 4 == 3:
            for t in outproj_group_tasks(oc // 4):
                t()


def attn_out_proj_pair(b, wo, ot, ot_T, ps_pool=None):
    P = b.P
    for dc in range(8):
        for T in range(NT):
            ts = slice(T * TT, (T + 1) * TT)
            if ps_pool is None:
                op_ = b.PS[b.ps_rr % 8]
            else:
                op_ = ps_pool[b.ps_rr % len(ps_pool)]
            b.ps_rr += 1
            P.op("pe", (lambda e, op_=op_, dc=dc, ts=ts: e.matmul(op_[:, :], lhsT=wo[:, dc * 128:(dc + 1) * 128], rhs=ot[:, ts], start=True, stop=True)),
                 reads=[wo, ot_T[T]], writes=[op_])
            x = b.xT[dc][T]
            P.op("dve", (lambda e, op_=op_, dc=dc, ts=ts: e.tensor_tensor(out=b.xT_t[:, dc, ts], in0=op_[:, :], in1=b.xT_t[:, dc, ts], op=ALU.add)),
                 reads=[op_, x], writes=[x])


def attn_out_proj(b, oreq):
    P = b.P
    for dc in range(8):
        wo = b.w_get(oreq[dc])
        for T in range(NT):
            ts = slice(T * TT, (T + 1) * TT)
            op_ = b.PS[b.ps_rr % 8]
            b.ps_rr += 1
            for oc in range(8):
                P.op("pe", (lambda e, op_=op_, oc=oc, ts=ts, wo=wo: e.matmul(op_[:, :], lhsT=wo[:, oc * 128:(oc + 1) * 128], rhs=b.OT_t[:, oc, ts],
                                                                          start=(oc == 0), stop=(oc == 7))),
                     reads=[wo, b.OT[oc][T]], writes=[op_], accum=(oc != 0))
            x = b.xT[dc][T]
            P.op("dve", (lambda e, op_=op_, dc=dc, ts=ts: e.tensor_tensor(out=b.xT_t[:, dc, ts], in0=op_[:, :], in1=b.xT_t[:, dc, ts], op=ALU.add)),
                 reads=[op_, x], writes=[x])


B_SLOT_C, B_SLOT_KR, B_SLOT_O = 0, 3, 4
PC_BQN, PC_BKVN, PC_BQG, PC_BKG, PC_BKRG = 66, 68, 69, 70, 71
CST_F, CST_PH = 0, 1
TWO_PI = float(2.0 * np.pi)
CW1 = 6.28125
CW2 = float(2.0 * np.pi - 6.28125)
MLA_SCALE = float(96.0 ** -0.5)
MASK_ENG = os.environ.get("MASK_ENG", "pool")
FUSE_NORM = os.environ.get("FUSE_NORM", "1") == "1"
ROPEK_ENG = os.environ.get("ROPEK_ENG", "pool")
CPH = int(os.environ.get("CPH", "0"))


def mixB_setup(b):
    nc = b.nc
    b.bw = nc.dram_tensor("bw", [12, 128, WSLOT], F32, kind="ExternalInput").ap()
    b.bwh = nc.dram_tensor("bwh", [16, 128, 384], F32, kind="ExternalInput").ap()
    b.cst_d = nc.dram_tensor("cst", [128, 4], F32, kind="ExternalInput").ap()
    if not hasattr(b, "pos_d"):
        b.pos_d = nc.dram_tensor("pos", [1, S], I32, kind="ExternalInput").ap()


def mixB_declare(b):
    r = {}
    r["c"] = [b.w_declare(b.bw[B_SLOT_C + i], WSLOT) for i in range(3)]
    r["kr"] = b.w_declare(b.bw[B_SLOT_KR], WSLOT)
    r["h"], r["o"] = [], []
    for h in range(16):
        r["h"].append(b.w_declare(b.bwh[h], 384))
        if h % 2 == 1:
            r["o"].append(b.w_declare(b.bw[B_SLOT_O + h // 2], WSLOT))
    return r


def mixB_consts(b):
    nc, P = b.nc, b.P
    if b.mix_mark is None:
        b.mix_mark = b.sb_off
    b.sb_off = b.mix_mark
    b.fence_new = True
    CS = b.sb("CS", [128, S], F32)
    krfin = b.sb("krfin", [128, S], F32)
    uvs = [b.sb("uv", [128, TT], BF16) for _ in range(2)]
    cst = b.sb("cst", [128, 4], F32)
    tri = b.sb("tri", [128, 128], BF16)
    mA = b.sb("mA", [128, 128], BF16)
    sel = b.sb("sel", [128, 128], BF16)
    posi = b.sb("posi", [128, TT], I32)
    b.fence_new = False
    uv_i = [0]
    cqn = b.act[0:2]
    cqn_T = b.act_T[0:2]
    ckvn = b.act[2]
    ckvn_T = b.act_T[2]
    sqK = [(b.act[3][:, i * 512:(i + 1) * 512], b.act_T[3][i]) for i in range(4)]
    pts = [(b.act[4][:, i * 512:(i + 1) * 512], b.act_T[4][i]) for i in range(4)]
    pt_i = [0]
    OTp = [b.act[5], b.otx]
    OTp_T = [b.act_T[5], b.otx_T]
    qh = [b.hT_t[:, i, :] for i in range(2)]
    qh_T = [b.hT[i] for i in range(2)]
    kh = [b.hT_t[:, 2 + i, :] for i in range(2)]
    kh_T = [b.hT[2 + i] for i in range(2)]
    vxs, vxs_T = [], []
    for i in range(2):
        v = b.hT_t[:, 4 + 2 * i:6 + 2 * i, :].rearrange("p a s -> p (a s)")[:, 0:NB * 192].rearrange("p (n c) -> p n c", c=192)
        vxs.append(v)
        vxs_T.append(b.hT[4 + 2 * i] + b.hT[5 + 2 * i])

    P.op("sp", lambda e: e.dma_start(out=cst[:, :], in_=b.cst_d[:, :]), writes=[cst], dma=True)
    P.op("pool", lambda e: e.memset(tri[:, :], 1.0), writes=[tri])
    P.op("pool", lambda e: e.affine_select(out=tri[:, :], in_=tri[:, :], pattern=[[1, 128]], compare_op=ALU.is_ge, fill=0.0,
                                           base=0, channel_multiplier=-1), reads=[tri], writes=[tri])
    P.op("pool", lambda e: e.memset(sel[:, :], 1.0), writes=[sel])
    for r0 in (0, 32):
        P.op("pool", (lambda e, r0=r0: e.affine_select(out=sel[r0:r0 + 32, :], in_=sel[r0:r0 + 32, :], pattern=[[-1, 128]], compare_op=ALU.is_equal,
                                                      fill=0.0, base=0, channel_multiplier=1)), reads=[sel], writes=[sel])
    P.op("pool", lambda e: e.memset(sel[64:128, :], 0.0), reads=[sel], writes=[sel])
    for u_ in uvs:
        P.op("dve", (lambda e, u_=u_: e.memset(u_[:, :], 0.0)), writes=[u_])
    P.op("dve", lambda e: e.memset(mA[:, :], 1.0), writes=[mA])
    P.op("dve", lambda e: e.memset(mA[32:64, :], 0.0), writes=[mA])
    for T in range(NT):
        ts = slice(T * TT, (T + 1) * TT)
        a = b.rstd[T % 2]
        kf = b.rstd[2 + T % 2]
        P.op("sp", (lambda e, ts=ts: e.dma_start(out=posi[:, :], in_=b.pos_d[0:1, ts].partition_broadcast(128))), writes=[posi], dma=True)
        P.op("dve", (lambda e, a=a: e.tensor_copy(out=a[:, :], in_=posi[:, :])), reads=[posi], writes=[a])
        P.op("dve", (lambda e, a=a: e.tensor_scalar(out=a[:, :], in0=a[:, :], scalar1=cst[:, CST_F:CST_F + 1], scalar2=cst[:, CST_PH:CST_PH + 1],
                                                    op0=ALU.mult, op1=ALU.add)), reads=[a, cst], writes=[a])
        ki = posi
        P.op("dve", (lambda e, a=a, kf=kf: e.tensor_scalar(out=kf[:, :], in0=a[:, :], scalar1=1.0 / TWO_PI, scalar2=None, op0=ALU.mult)),
             reads=[a], writes=[kf])
        P.op("dve", (lambda e, kf=kf: e.tensor_copy(out=ki[:, :], in_=kf[:, :])), reads=[kf], writes=[ki])
        P.op("dve", (lambda e, kf=kf: e.tensor_copy(out=kf[:, :], in_=ki[:, :])), reads=[ki], writes=[kf])
        P.op("dve", (lambda e, a=a, kf=kf: e.scalar_tensor_tensor(out=a[:, :], in0=kf[:, :], scalar=-CW1, in1=a[:, :], op0=ALU.mult, op1=ALU.add)),
             reads=[a, kf], writes=[a])
        P.op("dve", (lambda e, a=a, kf=kf: e.scalar_tensor_tensor(out=a[:, :], in0=kf[:, :], scalar=-CW2, in1=a[:, :], op0=ALU.mult, op1=ALU.add)),
             reads=[a, kf], writes=[a])
        P.op("dve", (lambda e, a=a, kf=kf: e.tensor_scalar(out=kf[:, :], in0=a[:, :], scalar1=float(np.pi), scalar2=-TWO_PI, op0=ALU.is_gt, op1=ALU.mult)),
             reads=[a], writes=[kf])
        P.op("dve", (lambda e, a=a, kf=kf: e.tensor_tensor(out=a[:, :], in0=a[:, :], in1=kf[:, :], op=ALU.add)), reads=[a, kf], writes=[a])
        P.op("dve", (lambda e, a=a, kf=kf: e.tensor_scalar(out=kf[:, :], in0=a[:, :], scalar1=-float(np.pi), scalar2=TWO_PI, op0=ALU.is_lt, op1=ALU.mult)),
             reads=[a], writes=[kf])
        P.op("dve", (lambda e, a=a, kf=kf: e.tensor_tensor(out=a[:, :], in0=a[:, :], in1=kf[:, :], op=ALU.add)), reads=[a, kf], writes=[a])
        P.op("dve", (lambda e, a=a: e.tensor_scalar(out=a[:, :], in0=a[:, :], scalar1=float(np.pi), scalar2=-float(np.pi), op0=ALU.min, op1=ALU.max)),
             reads=[a], writes=[a])
        P.op("act", (lambda e, a=a, ts=ts: e.activation(out=CS[0:64, ts], in_=a[0:64, :], func=AF.Sin)), reads=[a], writes=[CS])

    b.mla = dict(CS=CS, krfin=krfin, uvs=uvs, tri=tri, mA=mA, sel=sel, uv_i=uv_i, cqn=cqn, cqn_T=cqn_T, ckvn=ckvn, ckvn_T=ckvn_T,
                 sqK=sqK, pts=pts, pt_i=pt_i, OTp=OTp, OTp_T=OTp_T, qh=qh, qh_T=qh_T, kh=kh, kh_T=kh_T, vxs=vxs, vxs_T=vxs_T)


def mixB_run(b, r, l, next_gcol=None):
    nc, P = b.nc, b.P
    PSA = b.PS[0:2]
    PST = b.PS[2:4]
    PSH = b.PS[4:6]
    PSP = b.PS[6:8]
    st_i = [0]
    pp_i = [0]

    def psp():
        t = PSP[pp_i[0] % 2]
        pp_i[0] += 1
        return t

    if not hasattr(b, "mla"):
        mixB_consts(b)
    L = b.mla
    CS, krfin, uvs, tri, mA, sel, uv_i = L["CS"], L["krfin"], L["uvs"], L["tri"], L["mA"], L["sel"], L["uv_i"]
    cqn, cqn_T, ckvn, ckvn_T, sqK, pts, pt_i = L["cqn"], L["cqn_T"], L["ckvn"], L["ckvn_T"], L["sqK"], L["pts"], L["pt_i"]
    OTp, OTp_T, qh, qh_T, kh, kh_T, vxs, vxs_T = L["OTp"], L["OTp_T"], L["qh"], L["qh_T"], L["kh"], L["kh_T"], L["vxs"], L["vxs_T"]
    if DBG_STOP == 10:
        return
    b.rmsnorm(PC_MIX + 8 * l)

    def proj8(w, wcols, out_ps, T, mrows=128):
        ts = slice(T * TT, (T + 1) * TT)
        for c in range(8):
            P.op("pe", (lambda e, c=c, ts=ts: e.matmul(out_ps[0:mrows, :], lhsT=w[:, c * wcols:(c + 1) * wcols], rhs=b.hT_t[:, c, ts],
                                                       start=(c == 0), stop=(c == 7))),
                 reads=[w, b.hT[c][T]], writes=[out_ps], accum=(c != 0))

    def rstd_from(ss_ps, n, rs):
        P.op("act", (lambda e: e.activation(out=rs[:, :], in_=ss_ps[:, :], func=AF.Ln, scale=1.0 / n, bias=b.epsc[:, 0:1])),
             reads=[ss_ps, b.epsc], writes=[rs])
        P.op("act", (lambda e: e.activation(out=rs[:, :], in_=rs[:, :], func=AF.Exp, scale=-0.5)), reads=[rs], writes=[rs])

    def rope_sum(uv, out_ps):
        P.op("pe", (lambda e: e.matmul(out_ps[:, :], lhsT=sel[:, :], rhs=uv[:, :], start=True, stop=True)), reads=[sel, uv], writes=[out_ps])

    wc = [b.w_get(i) for i in r["c"]]
    wkr = b.w_get(r["kr"])
    for T in range(NT):
        ts = slice(T * TT, (T + 1) * TT)
        raws = [PSH[0], PSH[1]]
        ss = psp()
        for c2 in range(2):
            proj8(wc[c2], 128, raws[c2], T)
            sq = b.sq[b.sq_i % len(b.sq)]
            b.sq_i += 1
            P.op("act", (lambda e, sq=sq, raw=raws[c2]: e.activation(out=sq[:, :], in_=raw[:, :], func=AF.Square)), reads=[raws[c2]], writes=[sq])
            P.op("pe", (lambda e, sq=sq, c2=c2, ss=ss: e.matmul(ss[:, :], lhsT=b.ones[:, :], rhs=sq[:, :], start=(c2 == 0), stop=(c2 == 1))),
                 reads=[b.ones, sq], writes=[ss], accum=(c2 != 0))
        rs = b.sg[b.sg_i % len(b.sg)]
        b.sg_i += 1
        rstd_from(ss, 256.0, rs)
        for c2 in range(2):
            P.op("dve", (lambda e, c2=c2, rs=rs, ts=ts, raw=raws[c2]: e.scalar_tensor_tensor(
                out=cqn[c2][:, ts], in0=raw[:, :], scalar=b.par[:, PC_BQN + c2:PC_BQN + c2 + 1], in1=rs[:, :], op0=ALU.mult, op1=ALU.mult)),
                 reads=[raws[c2], rs, b.par_t], writes=[cqn_T[c2][T]])
        if CPH == 1:
            continue
        raw = PSH[0]
        ss = psp()
        proj8(wc[2], 128, raw, T)
        sq = b.sq[b.sq_i % len(b.sq)]
        b.sq_i += 1
        P.op("act", (lambda e, sq=sq, raw=raw: e.activation(out=sq[:, :], in_=raw[:, :], func=AF.Square)), reads=[raw], writes=[sq])
        P.op("pe", (lambda e, sq=sq, ss=ss: e.matmul(ss[:, :], lhsT=b.ones[:, :], rhs=sq[:, :], start=True, stop=True)), reads=[b.ones, sq], writes=[ss])
        rs = b.sg[b.sg_i % len(b.sg)]
        b.sg_i += 1
        rstd_from(ss, 128.0, rs)
        P.op("dve", (lambda e, rs=rs, ts=ts, raw=raw: e.scalar_tensor_tensor(
            out=ckvn[:, ts], in0=raw[:, :], scalar=b.par[:, PC_BKVN:PC_BKVN + 1], in1=rs[:, :], op0=ALU.mult, op1=ALU.mult)),
             reads=[raw, rs, b.par_t], writes=[ckvn_T[T]])
        if CPH == 2:
            continue
        raw = PSH[1]
        proj8(wkr, 128, raw, T)
        sqk, sqk_T = sqK[T]
        if CPH != 6:
            P.op("dve", (lambda e, sqk=sqk: e.memset(sqk[32:64, :], 0.0)), writes=[sqk_T])
            if CPH != 7:
                P.op("act", (lambda e, sqk=sqk, raw=raw: e.activation(out=sqk[0:32, :], in_=raw[0:32, :], func=AF.Square)), reads=[raw], writes=[sqk_T], accum=True)
        if CPH == 8:
            continue
        uv = uvs[uv_i[0] % 2]
        uv_i[0] += 1
        P.op("dve", (lambda e, uv=uv, raw=raw, ts=ts: e.scalar_tensor_tensor(
            out=uv[0:64, :], in0=raw[0:64, :], scalar=b.par[0:64, PC_BKRG:PC_BKRG + 1], in1=CS[0:64, ts], op0=ALU.mult, op1=ALU.mult)),
             reads=[raw, CS, b.par_t], writes=[uv])
        if CPH == 4:
            continue
        kps = psp()
        rope_sum(uv, kps)
        if CPH == 5:
            continue
        P.op("act", (lambda e, kps=kps, ts=ts: e.activation(out=krfin[0:32, ts], in_=kps[0:32, :], func=AF.Copy)), reads=[kps], writes=[krfin])

    if DBG_STOP == 11:
        return
    for i in range(2):
        P.op(MASK_ENG, (lambda e, i=i: e.memset(vxs[i], 1.0)), writes=vxs_T[i])
        P.op(MASK_ENG, (lambda e, i=i: e.memset(qh[i][32:64, :], 0.0)), writes=qh_T[i])
        P.op(MASK_ENG, (lambda e, i=i: e.memset(kh[i][32:64, :], 0.0)), writes=kh_T[i])

    prepQ, bgQ, normQ = [], [], []

    def prep_tasks(h):
        tasks = []
        q_t, q_T = qh[h % 2], qh_T[h % 2]
        k_t, k_T = kh[h % 2], kh_T[h % 2]
        vx, vx_T = vxs[h % 2], vxs_T[h % 2]
        for T in range(NT):
            ts = slice(T * TT, (T + 1) * TT)
            ctx = {}

            def tA(T=T, ts=ts, ctx=ctx):
                wh = b.w_get(r["h"][h])
                ctx["wh"] = wh
                qraw, kraw = PSH[0], PSH[1]
                for c2 in range(2):
                    P.op("pe", (lambda e, c2=c2: e.matmul(qraw[:, :], lhsT=wh[:, c2 * 128:(c2 + 1) * 128], rhs=cqn[c2][:, ts],
                                                          start=(c2 == 0), stop=(c2 == 1))),
                         reads=[wh, cqn_T[c2][T]], writes=[qraw], accum=(c2 != 0))
                sq = b.sq[b.sq_i % len(b.sq)]
                b.sq_i += 1
                ctx["sq"] = sq
                P.op("act", (lambda e: e.activation(out=sq[:, :], in_=qraw[:, :], func=AF.Square)), reads=[qraw], writes=[sq])
                P.op("pe", (lambda e: e.matmul(kraw[:, :], lhsT=wh[:, 192:320], rhs=ckvn[:, ts], start=True, stop=True)),
                     reads=[wh, ckvn_T[T]], writes=[kraw])
                sqk, sqk_T = sqK[T]
                P.op("act", (lambda e: e.activation(out=sqk[64:128, :], in_=kraw[64:128, :], func=AF.Square)), reads=[kraw], writes=[sqk_T])
            tasks.append(tA)
            if T == 0:
                for half in range(2):
                    def tV(half=half, ctx=ctx):
                        wh = ctx["wh"]
                        vp = psp()
                        for j in range(8):
                            n = half * 8 + j
                            P.op("pe", (lambda e, j=j, n=n: e.matmul(vp[:, j * 64:(j + 1) * 64], lhsT=ckvn[:, n * 128:(n + 1) * 128], rhs=wh[:, 320:384],
                                                                 start=True, stop=True)),
                                 reads=[wh, ckvn_T[n // 4]], writes=[vp], accum=(j != 0))
                        P.op("act", (lambda e: e.activation(out=vx[:, half * 8:(half + 1) * 8, 64:128],
                                                            in_=vp[:, :].rearrange("p (j d) -> p j d", j=8), func=AF.Copy)),
                             reads=[vp], writes=vx_T)
                    tasks.append(tV)

            def t1(ctx=ctx):
                ss = psp()
                sq = ctx["sq"]
                P.op("pe", (lambda e: e.matmul(ss[:, :], lhsT=mA[:, :], rhs=sq[:, :], start=True, stop=True)), reads=[mA, sq], writes=[ss])
                rs = b.sg[b.sg_i % len(b.sg)]
                b.sg_i += 1
                ctx["rs"] = rs
                rstd_from(ss, 96.0, rs)
            tasks.append(t1)

            def t2(T=T, ts=ts, ctx=ctx):
                rs = ctx["rs"]
                qraw = PSH[0]
                P.op("dve", (lambda e: e.scalar_tensor_tensor(out=q_t[64:128, ts], in0=qraw[64:128, :], scalar=b.par[64:128, PC_BQG:PC_BQG + 1],
                                                              in1=rs[64:128, :], op0=ALU.mult, op1=ALU.mult)), reads=[qraw, rs, b.par_t], writes=[q_T[T]])
                uv = uvs[uv_i[0] % 2]
                uv_i[0] += 1
                ctx["uv"] = uv
                P.op("dve", (lambda e: e.scalar_tensor_tensor(out=uv[0:64, :], in0=qraw[0:64, :], scalar=b.par[0:64, PC_BQG:PC_BQG + 1], in1=CS[0:64, ts],
                                                              op0=ALU.mult, op1=ALU.mult)), reads=[qraw, CS, b.par_t], writes=[uv])
            tasks.append(t2)

            def t3(T=T, ts=ts, ctx=ctx):
                rs = ctx["rs"]
                qps = psp()
                rope_sum(ctx["uv"], qps)
                P.op("dve", (lambda e: e.tensor_tensor(out=q_t[0:32, ts], in0=qps[0:32, :], in1=rs[0:32, :], op=ALU.mult)),
                     reads=[qps, rs], writes=[q_T[T]], accum=True)
            tasks.append(t3)

            def t4(T=T, ctx=ctx):
                sqk, sqk_T = sqK[T]
                ssk = psp()
                P.op("pe", (lambda e: e.matmul(ssk[:, :], lhsT=mA[:, :], rhs=sqk, start=True, stop=True)), reads=[mA, sqk_T], writes=[ssk])
                rsk = b.sg[b.sg_i % len(b.sg)]
                b.sg_i += 1
                ctx["rsk"] = rsk
                rstd_from(ssk, 96.0, rsk)
            tasks.append(t4)

            def t5(T=T, ts=ts, ctx=ctx):
                rsk = ctx["rsk"]
                kraw = PSH[1]
                P.op("dve", (lambda e: e.scalar_tensor_tensor(out=k_t[64:128, ts], in0=kraw[64:128, :], scalar=b.par[64:128, PC_BKG:PC_BKG + 1],
                                                              in1=rsk[64:128, :], op0=ALU.mult, op1=ALU.mult)), reads=[kraw, rsk, b.par_t], writes=[k_T[T]])
                P.op(ROPEK_ENG, (lambda e: e.tensor_tensor(out=k_t[0:32, ts], in0=krfin[0:32, ts], in1=rsk[0:32, :], op=ALU.mult)),
                     reads=[krfin, rsk], writes=[k_T[T]], accum=True)
            tasks.append(t5)
        return tasks

    def norm_task(h, Qc, acc):
        def t():
            ts = slice(Qc * 512, (Qc + 1) * 512)
            hh = h % 2
            orow = slice(hh * 64, (hh + 1) * 64)
            drow = slice((1 - hh) * 64, (2 - hh) * 64)
            dn = b.rstd[2 + (h * 4 + Qc) % 2]
            P.op("dve", (lambda e: e.reciprocal(out=dn[drow, :], in_=acc[drow, :])), reads=[acc], writes=[dn])
            ot, ot_T = OTp[(h // 2) % 2], OTp_T[(h // 2) % 2]
            P.op("dve", (lambda e: e.tensor_tensor(out=ot[orow, ts], in0=acc[orow, :], in1=dn[drow, :], op=ALU.mult)),
                 reads=[acc, dn], writes=[ot_T[Qc]], accum=(hh == 1))
        return t

    def outproj_tasks(p):
        ot, ot_T = OTp[p % 2], OTp_T[p % 2]
        tasks = []
        ctx = {}
        fuse = (next_gcol is not None) and p == 7
        order = [(dc, T) for T in range(NT) for dc in range(8)] if fuse else [(dc, T) for dc in range(8) for T in range(NT)]
        if fuse:
            b.set_sqn(0, 1)
        for (dc, T) in order:
            if True:
                def t(dc=dc, T=T):
                    if "wo" not in ctx:
                        ctx["wo"] = b.w_get(r["o"][p])
                    wo = ctx["wo"]
                    ts = slice(T * TT, (T + 1) * TT)
                    op_ = psp()
                    P.op("pe", (lambda e: e.matmul(op_[:, :], lhsT=wo[:, dc * 128:(dc + 1) * 128], rhs=ot[:, ts], start=True, stop=True)),
                         reads=[wo, ot_T[T]], writes=[op_])
                    x = b.xT[dc][T]
                    P.op("dve", (lambda e: e.tensor_tensor(out=b.xT_t[:, dc, ts], in0=op_[:, :], in1=b.xT_t[:, dc, ts], op=ALU.add)),
                         reads=[op_, x], writes=[x])
                    if fuse and dc == 7:
                        if T > 0:
                            b.norm_finish(T - 1, next_gcol)
                        b.norm_sq(T)
                        if T == NT - 1:
                            b.norm_finish(T, next_gcol)
                            b.norm_done = True
                tasks.append(t)
        return tasks

    def pop_tasks():
        if normQ:
            normQ.pop(0)()
        if prepQ:
            prepQ.pop(0)()
        if bgQ:
            bgQ.pop(0)()

    def attend(h, Qc):
        q_t, q_T = qh[h % 2], qh_T[h % 2]
        k_t, k_T = kh[h % 2], kh_T[h % 2]
        vx, vx_T = vxs[h % 2], vxs_T[h % 2]
        vcols = slice(64, 192) if h % 2 == 0 else slice(0, 128)
        acc = PSA[(h * 4 + Qc) % 2]
        jmax = 4 * Qc + 3
        while len(normQ) > 1:
            normQ.pop(0)()

        def scores(j):
            c0 = max(0, j - 4 * Qc) * 128
            st = PST[st_i[0] % 2]
            st_i[0] += 1
            P.op("pe", (lambda e, st=st, j=j, c0=c0: e.matmul(st[:, c0:512], lhsT=k_t[:, j * 128:(j + 1) * 128], rhs=q_t[:, Qc * 512 + c0:(Qc + 1) * 512],
                                                         start=True, stop=True)), reads=[k_T[j // 4], q_T[Qc]], writes=[st])
            pt, ptT = pts[pt_i[0] % 4]
            pt_i[0] += 1
            P.op("act", (lambda e, st=st, pt=pt, c0=c0: e.activation(out=pt[:, c0:512], in_=st[:, c0:512], func=AF.Exp, scale=MLA_SCALE)),
                 reads=[st], writes=[ptT])
            if j >= 4 * Qc:
                P.op(MASK_ENG, (lambda e, pt=pt, c0=c0: e.tensor_tensor(out=pt[:, c0:c0 + 128], in0=pt[:, c0:c0 + 128], in1=tri[:, :], op=ALU.mult)),
                     reads=[ptT, tri], writes=[ptT])
            return pt, ptT, c0

        nxt = scores(0)
        for j in range(jmax + 1):
            pt, ptT, c0 = nxt
            if j < jmax:
                nxt = scores(j + 1)
            P.op("pe", (lambda e, acc=acc, j=j, c0=c0, pt=pt: e.matmul(acc[:, c0:512], lhsT=vx[:, j, vcols], rhs=pt[:, c0:512], start=(j == 0), stop=(j == jmax))),
                 reads=vx_T + [ptT], writes=[acc], accum=(j != 0))
            pop_tasks()
        normQ.append(norm_task(h, Qc, acc))

    for t in prep_tasks(0):
        t()
    if DBG_STOP == 12:
        return
    if DBG_DUMP:
        b.dump("CS", lambda: CS[:, :], S, [CS])
        b.dump("krfin", lambda: krfin[:, :], S, [krfin])
        b.dump("cqn0", lambda: cqn[0][:, :], S, cqn_T[0])
        b.dump("ckvn", lambda: ckvn[:, :], S, ckvn_T)
        b.dump("qh0", lambda: qh[0], S, qh_T[0])
        b.dump("kh0", lambda: kh[0], S, kh_T[0])
        b.dump("vx0", lambda: vxs[0].rearrange("p n c -> p (n c)"), NB * 192, vxs_T[0])
    for h in range(16):
        if DBG_STOP == 7 and h == 2:
            while normQ:
                normQ.pop(0)()
            while bgQ:
                bgQ.pop(0)()
            return
        if h < 15:
            prepQ.extend(prep_tasks(h + 1))
        for Qc in range(4):
            attend(h, Qc)
        while prepQ:
            prepQ.pop(0)()
        while normQ:
            normQ.pop(0)()
        if h % 2 == 1:
            bgQ.extend(outproj_tasks(h // 2))
        if h % 2 == 0 or h == 15:
            while bgQ:
                bgQ.pop(0)()
```
